# Optimizing a Trainium2 kernel written in Bass

```python
import jax, jax.numpy as jnp
from jax import lax
import numpy as np

D_MODEL = 2048
BATCH = 16
SEQ = 256
DEPTH = 4
DEC_BATCH = 2
DEC_SEQ = 4096
PAST_LEN = 512

GRID_W = 64
HEAD_DIM = 128
N_HEADS_A = 8
N_KV_A = 2
A_WIDTH = N_HEADS_A * HEAD_DIM
KV_WIDTH = N_KV_A * HEAD_DIM
WINDOW = 128
BLOCK = 128
ROPE_BASE = 10000.0
ROT_AXIS = HEAD_DIM // 2
ATTN_SCALE = HEAD_DIM ** -0.5
NEG_MASK = -1e9
CONF_W = 512
CONF_K = 31
SC_W = 512
SC_K = 3
HG_HEADS = 4
HG_DK = 128
HG_DV = 128
HG_W = HG_HEADS * HG_DK
HG_CHUNK = 32
N_BRANCH = 4
IN_COLS = A_WIDTH + 2 * KV_WIDTH + 2 * CONF_W + 3 * SC_W + 5 * HG_W
D_FF = -(-8 * D_MODEL // (3 * 256)) * 256
EPS = 1e-6

kernel_name = 'hybrid_diffusion_parallel_trunk_step'


def rms_norm(x, g):
    xf = x.astype(jnp.float32)
    y = xf * lax.rsqrt(jnp.mean(xf * xf, axis=-1, keepdims=True) + EPS)
    return (y * g.astype(jnp.float32)).astype(x.dtype)


def layer_norm(x, g, b):
    xf = x.astype(jnp.float32)
    xc = xf - jnp.mean(xf, axis=-1, keepdims=True)
    var = jnp.mean(xc * xc, axis=-1, keepdims=True)
    return (xc * lax.rsqrt(var + EPS) * g.astype(jnp.float32) + b.astype(jnp.float32)).astype(x.dtype)


def modulate(x, g, shift, scale):
    return rms_norm(x, g) * (1 + scale) + shift


def depthwise_conv(x, w):
    pad = w.shape[0] // 2
    return lax.conv_general_dilated(x, w[:, None, :].astype(x.dtype), window_strides=(1,),
                                    padding=[(pad, pad)], dimension_numbers=('NWC', 'WIO', 'NWC'),
                                    feature_group_count=x.shape[-1])


def split_columns(z):
    sizes = [A_WIDTH, KV_WIDTH, KV_WIDTH, CONF_W, CONF_W, SC_W, SC_W, SC_W, HG_W, HG_W, HG_W, HG_W, HG_W]
    parts, start = [], 0
    for s in sizes:
        parts.append(z[..., start:start + s])
        start += s
    return parts


def axial_rope_tables(T):
    rows = T // GRID_W
    row = jnp.repeat(jnp.arange(rows, dtype=jnp.float32), GRID_W)
    col = jnp.tile(jnp.arange(GRID_W, dtype=jnp.float32), rows)
    inv = ROPE_BASE ** (-jnp.arange(0, ROT_AXIS, 2, dtype=jnp.float32) / ROT_AXIS)
    ang = jnp.stack([row[:, None] * inv, col[:, None] * inv], axis=1)
    return jnp.cos(ang), jnp.sin(ang)


def apply_rope(x, cos, sin):
    B, T, H, _ = x.shape
    xr = x.astype(jnp.float32).reshape(B, T, H, 2, 2, ROT_AXIS // 2)
    x1, x2 = xr[..., 0, :], xr[..., 1, :]
    c = cos[None, :, None]
    s = sin[None, :, None]
    out = jnp.stack([x1 * c - x2 * s, x2 * c + x1 * s], axis=-2)
    return out.reshape(B, T, H, HEAD_DIM).astype(x.dtype)


def context_attention(q, k, v, sink):
    B, S, H, D = q.shape
    G = H // N_KV_A
    nb = S // BLOCK
    qb = q.reshape(B, nb, BLOCK, N_KV_A, G, D).swapaxes(0, 1)
    sink_l = sink.astype(jnp.float32).reshape(N_KV_A, G)[None, :, :, None, None]

    def one_block(qj):
        s = jnp.einsum('bqkgd,bskd->bkgqs', qj, k, preferred_element_type=jnp.float32) * ATTN_SCALE
        logits = jnp.concatenate([s, jnp.broadcast_to(sink_l, s.shape[:-1] + (1,))], axis=-1)
        p = jax.nn.softmax(logits, axis=-1)[..., :-1].astype(v.dtype)
        return jnp.einsum('bkgqs,bskd->bqkgd', p, v)

    out = lax.map(one_block, qb)
    return out.swapaxes(0, 1).reshape(B, S, H * D)


def latent_window_attention(q, k, v, k_ctx, v_ctx, sink):
    B, T, H, D = q.shape
    G = H // N_KV_A
    nb = T // BLOCK
    pad = ((0, 0), (BLOCK, BLOCK), (0, 0), (0, 0))
    kp = jnp.pad(k, pad)
    vp = jnp.pad(v, pad)
    qb = q.reshape(B, nb, BLOCK, N_KV_A, G, D).swapaxes(0, 1)
    sink_l = sink.astype(jnp.float32).reshape(N_KV_A, G)[None, :, :, None, None]
    offs_q = jnp.arange(BLOCK)
    offs_k = jnp.arange(3 * BLOCK) - BLOCK
    band = jnp.abs(offs_k[None, :] - offs_q[:, None]) <= WINDOW

    def one_block(args):
        j, qj = args
        kw = lax.dynamic_slice_in_dim(kp, j * BLOCK, 3 * BLOCK, axis=1)
        vw = lax.dynamic_slice_in_dim(vp, j * BLOCK, 3 * BLOCK, axis=1)
        kpos = j * BLOCK + offs_k
        valid = band & ((kpos >= 0) & (kpos < T))[None, :]
        s_w = jnp.einsum('bqkgd,bskd->bkgqs', qj, kw, preferred_element_type=jnp.float32) * ATTN_SCALE
        s_w = jnp.where(valid, s_w, NEG_MASK)
        s_c = jnp.einsum('bqkgd,bskd->bkgqs', qj, k_ctx, preferred_element_type=jnp.float32) * ATTN_SCALE
        logits = jnp.concatenate([s_w, s_c, jnp.broadcast_to(sink_l, s_w.shape[:-1] + (1,))], axis=-1)
        p = jax.nn.softmax(logits, axis=-1)
        p_w = p[..., :3 * BLOCK].astype(v.dtype)
        p_c = p[..., 3 * BLOCK:-1].astype(v.dtype)
        return jnp.einsum('bkgqs,bskd->bqkgd', p_w, vw) + jnp.einsum('bkgqs,bskd->bqkgd', p_c, v_ctx)

    out = lax.map(one_block, (jnp.arange(nb), qb))
    return out.swapaxes(0, 1).reshape(B, T, H * D)


def hgrn2_scan(q, k, v, logf, s0):
    B, T, H, _ = q.shape
    n = T // HG_CHUNK

    def chunks(a):
        return a.reshape(B, n, HG_CHUNK, H, a.shape[-1]).transpose(1, 0, 3, 2, 4)

    causal = jnp.tril(jnp.ones((HG_CHUNK, HG_CHUNK), dtype=bool))[:, :, None]

    def step(S, inp):
        qc, kc, vc, gc = inp
        b = jnp.cumsum(gc, axis=2)
        o = jnp.einsum('bhtd,bhdv->bhtv', qc * jnp.exp(b), S)
        diff = b[:, :, :, None, :] - b[:, :, None, :, :]
        decay = jnp.where(causal, jnp.exp(jnp.where(causal, diff, 0.0)), 0.0)
        att = jnp.einsum('bhtd,bhsd,bhtsd->bhts', qc, kc, decay)
        o = o + jnp.einsum('bhts,bhsv->bhtv', att, vc)
        b_last = b[:, :, -1:, :]
        S = jnp.exp(b_last[:, :, 0, :])[..., None] * S + jnp.einsum('bhsd,bhsv->bhdv', kc * jnp.exp(b_last - b), vc)
        return S, o

    S, o = lax.scan(step, s0, (chunks(q), chunks(k), chunks(v), chunks(logf)))
    return o.transpose(1, 0, 3, 2, 4).reshape(B, T, H, -1), S


def hgrn2_mixer(hq, hf_fwd, hf_bwd, hi, hg, lb_pair, norm_g, s_f0, s_b0):
    B, T, _ = hq.shape

    def heads(a):
        return a.astype(jnp.float32).reshape(B, T, HG_HEADS, -1)

    q = jax.nn.silu(heads(hq)) * HG_DK ** -0.5
    v = heads(hi)

    def decay(f_logit, lb):
        f_logit = heads(f_logit)
        lb = lb.reshape(HG_HEADS, HG_DK)
        logf = jax.nn.log_sigmoid(f_logit) + jnp.log1p(lb * jnp.exp(-f_logit))
        k = (1 - lb) * jax.nn.sigmoid(-f_logit)
        return k, logf

    k_f, g_f = decay(hf_fwd, lb_pair[0])
    k_b, g_b = decay(hf_bwd, lb_pair[1])
    o_f, s_f = hgrn2_scan(q, k_f, v, g_f, s_f0)
    rev = lambda a: jnp.flip(a, axis=1)
    o_b, s_b = hgrn2_scan(rev(q), rev(k_b), rev(v), rev(g_b), s_b0)
    o = o_f + rev(o_b)
    o = rms_norm(o, norm_g) * jax.nn.silu(heads(hg))
    return o.reshape(B, T, HG_W).astype(hq.dtype), s_f, s_b


def trunk_layer(x, mod, p, l, lb_pair, rope, ctx):
    B, T, _ = x.shape
    shift1, scale1, gate1, shift2, scale2, gate2 = jnp.split(mod, 6, axis=-1)
    h = modulate(x, p['norm1_g'][l], shift1, scale1)
    (qa, ka, va, conf_a, conf_g, sc_b, sc_c, sc_h,
     hq, hf_fwd, hf_bwd, hi, hg) = split_columns(h @ p['w_in'][l])
    qa = rms_norm(qa.reshape(B, T, N_HEADS_A, HEAD_DIM), p['q_norm_g'][l])
    ka = rms_norm(ka.reshape(B, T, N_KV_A, HEAD_DIM), p['k_norm_g'][l])
    va = va.reshape(B, T, N_KV_A, HEAD_DIM)
    sink = p['attn_sink'][l]
    if ctx is None:
        attn = context_attention(qa, ka, va, sink)
        s_f0 = jnp.zeros((B, HG_HEADS, HG_DK, HG_DV), jnp.float32)
        s_b0 = s_f0
    else:
        k_ctx, v_ctx, s_f0, s_b0 = ctx
        cos, sin = rope
        attn = latent_window_attention(apply_rope(qa, cos, sin), apply_rope(ka, cos, sin), va, k_ctx, v_ctx, sink)
    y_a = attn @ p['w_attn_out'][l]
    u = depthwise_conv(conf_a * jax.nn.sigmoid(conf_g), p['conf_dw_w'][l]) + p['conf_dw_b'][l]
    u = layer_norm(u, p['conf_ln_g'][l], p['conf_ln_b'][l])
    y_b = jax.nn.silu(u) @ p['w_conf_out'][l]
    y_c = (sc_b * depthwise_conv(sc_c * sc_h, p['sc_conv_w'][l])) @ p['w_sc_out'][l]
    o_d, s_f, s_b = hgrn2_mixer(hq, hf_fwd, hf_bwd, hi, hg, lb_pair, p['hg_norm_g'][l],
                                s_f0.astype(jnp.float32), s_b0.astype(jnp.float32))
    y_d = o_d @ p['w_hg_out'][l]
    gates = jax.nn.sigmoid(h @ p['w_gate'][l] + p['b_gate'][l]).reshape(B, T, N_BRANCH, D_MODEL)
    merged = jnp.einsum('btnd,btnd->btd', gates, jnp.stack([y_a, y_b, y_c, y_d], axis=2))
    x = x + gate1 * (merged @ p['w_o'][l])
    h2 = modulate(x, p['norm2_g'][l], shift2, scale2)
    ff = (jax.nn.silu(h2 @ p['ffn_w1'][l]) * (h2 @ p['ffn_w3'][l])) @ p['ffn_w2'][l]
    x = x + gate2 * ff
    produced = (ka, va, s_f, s_b) if ctx is None else None
    return x, produced


def setup_inputs(seed: int = 0) -> dict:
    key = jax.random.key(seed)
    ks = jax.random.split(key, 40)
    D = D_MODEL

    def nrm(k, shape, scale=1.0):
        return jax.random.normal(k, shape, jnp.float32) * scale

    return {
        'x_prompt': nrm(ks[0], (BATCH, SEQ, D)),
        'x_sample': nrm(ks[1], (DEC_BATCH, DEC_SEQ, D)),
        'cache_k': nrm(ks[2], (DEC_BATCH, DEPTH, PAST_LEN, N_KV_A, HEAD_DIM)),
        'cache_v': nrm(ks[3], (DEC_BATCH, DEPTH, PAST_LEN, N_KV_A, HEAD_DIM)),
        'state_hgrn_fwd': nrm(ks[4], (DEC_BATCH, DEPTH, HG_HEADS, HG_DK, HG_DV), 0.3),
        'state_hgrn_bwd': nrm(ks[5], (DEC_BATCH, DEPTH, HG_HEADS, HG_DK, HG_DV), 0.3),
        'c': nrm(ks[6], (DEC_BATCH, D)),
        'c_ctx': nrm(ks[7], (D,)),
        'ada_w': nrm(ks[8], (DEPTH, D, 6 * D), 0.5 * D ** -0.5),
        'ada_b': nrm(ks[9], (DEPTH, 6 * D), 0.01),
        'norm1_g': 1.0 + nrm(ks[10], (DEPTH, D), 0.02),
        'norm2_g': 1.0 + nrm(ks[11], (DEPTH, D), 0.02),
        'w_in': nrm(ks[12], (DEPTH, D, IN_COLS), D ** -0.5),
        'q_norm_g': 1.0 + nrm(ks[13], (DEPTH, HEAD_DIM), 0.02),
        'k_norm_g': 1.0 + nrm(ks[14], (DEPTH, HEAD_DIM), 0.02),
        'attn_sink': nrm(ks[15], (DEPTH, N_HEADS_A), 0.5),
        'w_attn_out': nrm(ks[16], (DEPTH, A_WIDTH, D), A_WIDTH ** -0.5),
        'conf_dw_w': nrm(ks[17], (DEPTH, CONF_K, CONF_W), CONF_K ** -0.5),
        'conf_dw_b': nrm(ks[18], (DEPTH, CONF_W), 0.01),
        'conf_ln_g': 1.0 + nrm(ks[19], (DEPTH, CONF_W), 0.02),
        'conf_ln_b': nrm(ks[20], (DEPTH, CONF_W), 0.01),
        'w_conf_out': nrm(ks[21], (DEPTH, CONF_W, D), CONF_W ** -0.5),
        'sc_conv_w': nrm(ks[22], (DEPTH, SC_K, SC_W), SC_K ** -0.5),
        'w_sc_out': nrm(ks[23], (DEPTH, SC_W, D), SC_W ** -0.5),
        'hg_lb': nrm(ks[24], (2, DEPTH, HG_W), 0.1),
        'hg_norm_g': 1.0 + nrm(ks[25], (DEPTH, HG_DV), 0.02),
        'w_hg_out': nrm(ks[26], (DEPTH, HG_W, D), HG_W ** -0.5),
        'w_gate': nrm(ks[27], (DEPTH, D, N_BRANCH * D), D ** -0.5),
        'b_gate': nrm(ks[28], (DEPTH, N_BRANCH * D), 0.01),
        'w_o': nrm(ks[29], (DEPTH, D, D), D ** -0.5),
        'ffn_w1': nrm(ks[30], (DEPTH, D, D_FF), D ** -0.5),
        'ffn_w3': nrm(ks[31], (DEPTH, D, D_FF), D ** -0.5),
        'ffn_w2': nrm(ks[32], (DEPTH, D_FF, D), D_FF ** -0.5),
    }


def reference(x_prompt, x_sample, cache_k, cache_v, state_hgrn_fwd, state_hgrn_bwd, c, c_ctx,
              ada_w, ada_b, norm1_g, norm2_g, w_in, q_norm_g, k_norm_g, attn_sink, w_attn_out,
              conf_dw_w, conf_dw_b, conf_ln_g, conf_ln_b, w_conf_out, sc_conv_w, w_sc_out,
              hg_lb, hg_norm_g, w_hg_out, w_gate, b_gate, w_o, ffn_w1, ffn_w3, ffn_w2):
    p = {
        'norm1_g': norm1_g, 'norm2_g': norm2_g, 'w_in': w_in, 'q_norm_g': q_norm_g, 'k_norm_g': k_norm_g,
        'attn_sink': attn_sink, 'w_attn_out': w_attn_out, 'conf_dw_w': conf_dw_w, 'conf_dw_b': conf_dw_b,
        'conf_ln_g': conf_ln_g, 'conf_ln_b': conf_ln_b, 'w_conf_out': w_conf_out, 'sc_conv_w': sc_conv_w,
        'w_sc_out': w_sc_out, 'hg_norm_g': hg_norm_g, 'w_hg_out': w_hg_out, 'w_gate': w_gate,
        'b_gate': b_gate, 'w_o': w_o, 'ffn_w1': ffn_w1, 'ffn_w3': ffn_w3, 'ffn_w2': ffn_w2,
    }
    lb = jax.nn.softmax(hg_lb.astype(jnp.float32), axis=1)
    lb = jnp.maximum(jnp.cumsum(lb, axis=1) - lb[:, :1], 0.0)
    rope = axial_rope_tables(x_sample.shape[1])
    xp = x_prompt
    xs = x_sample
    ks_out, vs_out, sf_out, sb_out = [], [], [], []
    for l in range(DEPTH):
        mod_ctx = jax.nn.silu(c_ctx) @ ada_w[l] + ada_b[l]
        mod_lat = (jax.nn.silu(c) @ ada_w[l] + ada_b[l])[:, None, :]
        xp, (k_l, v_l, sf_l, sb_l) = trunk_layer(xp, mod_ctx, p, l, lb[:, l], None, None)
        xs, _ = trunk_layer(xs, mod_lat, p, l, lb[:, l], rope,
                            (cache_k[:, l], cache_v[:, l], state_hgrn_fwd[:, l], state_hgrn_bwd[:, l]))
        ks_out.append(k_l)
        vs_out.append(v_l)
        sf_out.append(sf_l.astype(x_prompt.dtype))
        sb_out.append(sb_l.astype(x_prompt.dtype))
    new_cache_k = jnp.stack(ks_out, axis=1)
    new_cache_v = jnp.stack(vs_out, axis=1)
    new_state_hgrn_fwd = jnp.stack(sf_out, axis=1)
    new_state_hgrn_bwd = jnp.stack(sb_out, axis=1)
    return (xp, xs, new_cache_k, new_cache_v, new_state_hgrn_fwd, new_state_hgrn_bwd)
```

```python
import numpy as np
from contextlib import ExitStack
import concourse.bass as bass
import concourse.mybir as mybir
from concourse.bass_utils import run_bass_kernel_spmd

F32 = mybir.dt.float32
BF16 = mybir.dt.bfloat16
AF = mybir.ActivationFunctionType
ALU = mybir.AluOpType
AX = mybir.AxisListType

D = 2048
KD = 16
L = 4
DFF = 5632
INC = 6656
EPS = 1e-6
SCALE = 128 ** -0.5
NEG = -30000.0
ARENA = 25600
CQ, CK, CV, CCA, CCG, CSB, CSC, CSH, CHQ, CHF, CHB, CHI, CHG = (
    0, 1024, 1280, 1536, 2048, 2560, 3072, 3584, 4096, 4608, 5120, 5632, 6144)


class Sem:
    def __init__(self, name):
        self.name = name
        self.h = None
        self.count = 0


class FW:
    ENGS = ("pe", "act", "dve", "pool", "sp")

    def __init__(self):
        self.sems = {}
        self.rec = {e: [] for e in self.ENGS}
        self.waited = {e: {} for e in self.ENGS}
        self.lastw = {}
        self.readers = {}
        for e in self.ENGS:
            self.sem(e)

    def sem(self, name):
        if name not in self.sems:
            self.sems[name] = Sem(name)
        return self.sems[name]

    def op(self, eng, fn, reads=(), writes=(), lane=None, linc=16):
        deps = {}

        def add(ev):
            if ev is not None and deps.get(ev[0], 0) < ev[1]:
                deps[ev[0]] = ev[1]

        for k in reads:
            add(self.lastw.get(k))
        for k in writes:
            add(self.lastw.get(k))
            for ev in self.readers.get(k, {}).items():
                add(ev)
        waits = []
        wd = self.waited[eng]
        for s, v in deps.items():
            if wd.get(s, 0) < v:
                wd[s] = v
                waits.append((s, v))
        if lane is None:
            sm = self.sems[eng]
            sm.count += 1
            inc = 1
        else:
            sm = self.sem(lane)
            sm.count += linc
            inc = linc
        ev = (sm.name, sm.count)
        self.rec[eng].append((waits, fn, sm.name, inc))
        for k in writes:
            self.lastw[k] = ev
            self.readers[k] = {}
        for k in reads:
            r = self.readers.setdefault(k, {})
            if r.get(ev[0], 0) < ev[1]:
                r[ev[0]] = ev[1]
        return ev

    def barrier(self, engs=("pe", "act", "dve", "sp")):
        cur = [(s.name, s.count) for s in self.sems.values() if s.count > 0]
        for e in engs:
            waits = []
            wd = self.waited[e]
            for s, v in cur:
                if wd.get(s, 0) < v:
                    wd[s] = v
                    waits.append((s, v))
            if waits:
                self.rec[e].append((waits, None, None, 0))

    def emit(self, nc, stack):
        for s in self.sems.values():
            s.h = stack.enter_context(nc.semaphore(s.name))
        block = stack.enter_context(nc.Block())
        sems = self.sems

        def run(e, lst):
            for waits, fn, sname, inc in lst:
                for s, v in waits:
                    e.wait_ge(sems[s].h, v)
                if fn is not None:
                    fn(e).then_inc(sems[sname].h, inc)

        rec = self.rec

        @block.tensor
        def _(e):
            run(e, rec["pe"])

        @block.scalar
        def _(e):
            run(e, rec["act"])

        @block.vector
        def _(e):
            run(e, rec["dve"])

        @block.gpsimd
        def _(e):
            run(e, rec["pool"])

        @block.sync
        def _(e):
            run(e, rec["sp"])


PRM = {}
_off = 0
for _n, _w in (("n1g", L * 16), ("n2g", L * 16), ("adab", L * 96), ("bgate", L * 64), ("cfw", L * 4 * 31),
               ("cfb", L * 4), ("cflg", L * 4), ("cflb", L * 4), ("scw", L * 4 * 3), ("hglb", 2 * L * 4),
               ("hgng", L), ("sink", L * 8), ("sel", 32), ("cm", 4)):
    PRM[_n] = _off
    _off += _w
NP_ = _off

XR = {"A": 1280, "B": 1152}
XO = dict(kf=("A", 0), kl=("A", 32768), vf=("A", 65536), vl=("A", 98304), uL=("A", 131072), uR=("A", 139264),
          wL=("A", 147456), wR=("A", 147968), Sf=("B", 0), Sb=("B", 65536), Af=("B", 131072), Ab=("B", 131584))


def fm(v, nchunks):
    v = np.asarray(v, np.float32)
    lead = v.shape[:-1]
    v = v.reshape(lead + (nchunks, 128))
    return np.moveaxis(v, -1, 0)


def pack_params(inp, core):
    P = np.zeros((128, NP_), np.float32)

    def put(name, arr):
        a = np.ascontiguousarray(arr).reshape(128, -1)
        P[:, PRM[name]:PRM[name] + a.shape[1]] = a

    put("n1g", fm(inp["norm1_g"], 16))
    put("n2g", fm(inp["norm2_g"], 16))
    put("adab", fm(inp["ada_b"], 96))
    put("bgate", fm(inp["b_gate"], 64))
    put("cfw", np.transpose(fm(np.transpose(inp["conf_dw_w"], (0, 1, 2)), 4), (0, 1, 3, 2)))
    put("cfb", fm(inp["conf_dw_b"], 4))
    put("cflg", fm(inp["conf_ln_g"], 4))
    put("cflb", fm(inp["conf_ln_b"], 4))
    put("scw", np.transpose(fm(inp["sc_conv_w"], 4), (0, 1, 3, 2)))
    put("hglb", fm(inp["hg_lb"], 4))
    put("hgng", np.transpose(np.asarray(inp["hg_norm_g"], np.float32), (1, 0)))
    put("sink", np.broadcast_to(np.asarray(inp["attn_sink"], np.float32).reshape(1, L * 8), (128, L * 8)))
    j = core % 4
    sel = np.zeros(16, np.float32)
    for i in range(4):
        sel[i] = 1.0 if i == j - 1 else 0.0
        sel[4 + i] = 1.0 if i == j + 1 else 0.0
        sel[8 + i] = 1.0 if i < j else 0.0
        sel[12 + i] = 1.0 if i > j else 0.0
    sel = np.concatenate([sel, 1.0 - sel])
    put("sel", np.broadcast_to(sel.reshape(1, 32), (128, 32)))
    cm = np.zeros((128, 4), np.float32)
    for p in range(128):
        cm[p, p // 32] = 1.0
    put("cm", cm)
    return P


class Prog:
    def __init__(self, do_sample=True, nl=L):
        self.do_sample = do_sample
        self.nl = nl
        self.nc = bass.Bass("TRN2", target_bir_lowering=False)
        self.fw = FW()
        self.wi = 0
        self.psi = 0
        self.tmpi = 0
        self.xkeys = []

    def dram_in(self, name, shape):
        return self.nc.dram_tensor(name, list(shape), F32, kind="ExternalInput").ap()

    def dram_out(self, name, shape):
        return self.nc.dram_tensor(name, list(shape), F32, kind="ExternalOutput").ap()

    def V(self, fn, reads=(), writes=()):
        return self.fw.op("dve", fn, reads, writes)

    def A(self, fn, reads=(), writes=()):
        return self.fw.op("act", fn, reads, writes)

    def PE(self, mms, reads=(), writes=()):
        def fn(e, mms=mms):
            ins = None
            for (o, l, r, st, sp) in mms:
                ins = e.matmul(o, l, r, start=st, stop=sp)
            return ins
        return self.fw.op("pe", fn, reads, writes)

    def TR(self, out, in_, ident, reads=(), writes=()):
        return self.fw.op("pe", lambda e: e.transpose(out, in_, ident), reads, writes)

    def DMA(self, out, in_, reads=(), writes=(), lane="ld", eng="sp", slow=False):
        if slow:
            ev = self.fw.op(eng, lambda e: e.dma_start(out=out, in_=in_, allow_slow_non_contiguous=True), reads, writes,
                            lane=lane)
        else:
            ev = self.fw.op(eng, lambda e: e.dma_start(out=out, in_=in_), reads, writes, lane=lane)
        wd = self.fw.waited[eng]
        if wd.get(ev[0], 0) < ev[1]:
            wd[ev[0]] = ev[1]
            self.fw.rec[eng].append(([ev], None, None, 0))
        return ev

    def DMAc(self, out, in_, reads=(), writes=()):
        self.fw.barrier(("pool",))
        return self.DMA(out, in_, reads, writes, lane="pl", eng="pool")

    def act(self, out, in_, func, bias=None, scale=1.0, reads=(), writes=()):
        if bias is None:
            return self.A(lambda e: e.activation(out=out, in_=in_, func=func, scale=scale), reads, writes)
        return self.A(lambda e: e.activation(out=out, in_=in_, func=func, bias=bias, scale=scale), reads, writes)

    def tt(self, out, a, b, op, reads=(), writes=()):
        return self.V(lambda e: e.tensor_tensor(out=out, in0=a, in1=b, op=op), reads, writes)

    def ts(self, out, a, s1, s2, op0, op1=None, reads=(), writes=()):
        if op1 is None:
            return self.V(lambda e: e.tensor_scalar(out=out, in0=a, scalar1=s1, scalar2=None, op0=op0), reads, writes)
        return self.V(lambda e: e.tensor_scalar(out=out, in0=a, scalar1=s1, scalar2=s2, op0=op0, op1=op1), reads, writes)

    def stt(self, out, in0, scalar, in1, op0, op1, reads=(), writes=()):
        return self.V(lambda e: e.scalar_tensor_tensor(out=out, in0=in0, scalar=scalar, in1=in1, op0=op0, op1=op1),
                      reads, writes)

    def cp(self, out, in_, reads=(), writes=()):
        return self.V(lambda e: e.tensor_copy(out=out, in_=in_), reads, writes)

    def memset(self, ap, val, writes=()):
        return self.V(lambda e: e.memset(ap, val), (), writes)

    def ps(self):
        i = self.psi % 6
        self.psi += 1
        return self.psb[i], ("ps", i)

    def tmp(self):
        i = self.tmpi % 4
        self.tmpi += 1
        return self.tmps[i], ("tmp", i)

    def wload(self, W, k0, nk, c0, ncol):
        s = self.wi % 2
        self.wi += 1
        src = W[k0 * 128:(k0 + nk) * 128, c0:c0 + ncol].rearrange("(k p) n -> p k n", p=128)
        dst = self.wsl[s][:, 0:nk, 0:ncol]
        self.fw.op("pool", lambda e: e.dma_start(out=dst, in_=src), (), [("w", s)], lane="w%d" % s)
        return self.wsl[s], ("w", s)

    def xin_v(self, name, n):
        b, o = XO[name]
        flat = self.xin[b].rearrange("r c -> (r c)")
        return flat[o:o + n]

    def xout_v(self, name, n):
        b, o = XO[name]
        return self.xout[b].rearrange("(r q) c -> r (q c)", r=4)[:, o:o + n]

    def xput(self, name, dst, src, reads):
        key = ("xin", XO.get(name, XO.get({"A0": "Af", "A1": "Ab", "S0": "Sf", "S1": "Sb"}.get(name, name)))[0], name,
               len(self.xkeys))
        self.xkeys.append(key)
        self.DMA(dst, src, reads=reads, writes=[key], lane="xw")

    def build(self):
        nc = self.nc
        di = self.dram_in
        NL_ = self.nl
        self.xp_in = di("xT_p", (D, 512))
        self.xs_in = di("xT_s", (D, 1024))
        self.ck_in = di("ck", (L, 512, 256))
        self.cv_in = di("cv", (L, 512, 256))
        self.s0f_in = di("s0f", (L, 4, 128, 128))
        self.s0b_in = di("s0b", (L, 4, 128, 128))
        self.cvec_in = di("cvec", (128, 16, 2))
        self.prm_in = di("prm", (128, NP_))
        self.qkg_in = di("qkg", (L, 128, 256))
        self.ident_in = di("ident", (128, 128))
        self.hmask_in = di("hmask", (128, 2, 128))
        self.abias_in = di("abias", (128, 10, 128))
        self.rope_in = di("rope", (128, 8, 2, 64))
        self.smask_in = di("smask", (128, 1024))
        self.W = {}
        for n, shp in (("ada_w", (NL_, D, 6 * D)), ("w_in", (NL_, D, INC)), ("w_attn_out", (NL_, 1024, D)),
                       ("w_conf_out", (NL_, 512, D)), ("w_sc_out", (NL_, 512, D)), ("w_hg_out", (NL_, 512, D)),
                       ("w_gate", (NL_, D, 4 * D)), ("w_o", (NL_, D, D)), ("ffn_w1", (NL_, D, DFF)),
                       ("ffn_w3", (NL_, D, DFF)), ("ffn_w2", (NL_, DFF, D))):
            self.W[n] = di(n, shp)
        do = self.dram_out
        self.yp_out = do("yT_p", (D, 512))
        self.ys_out = do("yT_s", (D, 1024))
        self.nk_out = do("nk", (2, L, 256, 256))
        self.nv_out = do("nv", (2, L, 256, 256))
        self.nsf_out = do("nsf", (2, L, 4, 128, 128))
        self.nsb_out = do("nsb", (2, L, 4, 128, 128))
        self.xin = {b: nc.dram_tensor("xch_in" + b, [128, XR[b]], F32).ap() for b in XR}
        self.xout = {b: nc.dram_tensor("xch_out" + b, [512, XR[b]], F32).ap() for b in XR}

        with ExitStack() as st:
            sb = lambda n, s, d: st.enter_context(nc.sbuf_tensor(n, s, d))
            self.xT = sb("xT", [128, 16, 1024], F32)
            self.mg = sb("mg", [128, 16, 1024], BF16)
            self.hT = sb("hT", [128, 16, 512], BF16)
            self.wsl = [sb("wsl%d" % i, [128, 16, 256], BF16) for i in range(2)]
            self.rstd = sb("rstd", [128, 1024], F32)
            self.prm = sb("prm_s", [128, NP_], F32)
            self.modT = sb("modT", [128, L, 96, 2], F32)
            self.gsh = sb("gsh", [128, 2, 16], F32)
            self.identF = sb("identF", [128, 128], F32)
            self.identB = sb("identB", [128, 128], BF16)
            self.onesB = sb("onesB", [128, 128], BF16)
            self.onesF = sb("onesF", [128, 128], F32)
            self.epsT = sb("epsT", [128, 1], F32)
            self.lbv = sb("lbv", [128, 2, L, 4], F32)
            self.oml = sb("oml", [128, 2, L, 4], F32)
            self.csb = sb("csb", [128, 16, 2], BF16)
            self.tmps = [sb("tmp%d" % i, [128, 512], F32) for i in range(4)]
            self.hmask = sb("hmask_s", [128, 2, 128], F32)
            self.smask = sb("smask_s", [128, 1024], F32)
            self.arena = sb("arena", [128, ARENA], BF16)
            self.psb = [st.enter_context(nc.psum_tensor("ps%d" % i, [128, 512], F32)) for i in range(6)]
            self.psT = st.enter_context(nc.psum_tensor("psT", [128, 1024], BF16))
            self.psX = st.enter_context(nc.psum_tensor("psX", [128, 512], F32))
            self.program()
            self.fw.barrier(("sp",))
            self.fw.emit(nc, st)
        return nc

    def carve(self, off, n, dtype=BF16):
        if dtype == BF16:
            assert off + n <= ARENA, (off, n)
            return self.arena[:, off:off + n], off + n
        off += off % 2
        assert off + 2 * n <= ARENA, (off, n)
        return self.arena[:, off:off + 2 * n].bitcast(F32), off + 2 * n

    def program(self):
        fw = self.fw
        self.DMA(self.prm[:], self.prm_in, writes=["prm"])
        self.DMA(self.identF[:], self.ident_in, writes=["identF"])
        self.DMA(self.hmask[:], self.hmask_in, writes=["hmask"])
        self.DMA(self.smask[:], self.smask_in, writes=["smask"])
        self.cp(self.identB[:], self.identF[:], ["identF"], ["identB"])
        self.memset(self.onesB[:], 1.0, ["onesB"])
        self.memset(self.onesF[:], 1.0, ["onesF"])
        self.memset(self.epsT[:], EPS, ["epsT"])
        self.prologue()
        import os
        if "P" in os.environ.get("KPASS", "PS"):
            for l in range(self.nl):
                self.layer_pass(l, "P")
            self.DMA(self.yp_out.rearrange("(k p) t -> p k t", p=128), self.xT[:, :, 0:512], reads=["x"], writes=["yp"],
                     lane="out")
            fw.barrier()
        if self.do_sample:
            for l in range(self.nl):
                self.layer_pass(l, "S")
            self.DMA(self.ys_out.rearrange("(k p) t -> p k t", p=128), self.xT[:, :, 0:1024], reads=["x"],
                     writes=["ys"], lane="out")

    def prologue(self):
        prm = self.prm
        t, tk = self.tmp()
        cview = t[:, 0:32].rearrange("p (k t) -> p k t", t=2)
        self.DMA(cview, self.cvec_in, writes=[tk])
        self.act(self.csb[:], cview, AF.Silu, reads=[tk], writes=["csb"])
        lbr = prm[:, PRM["hglb"]:PRM["hglb"] + 32].rearrange("p (d l h) -> p d l h", d=2, l=L)
        e_, ek = self.tmp()
        ev = e_[:, 0:32].rearrange("p (d l h) -> p d l h", d=2, l=L)
        self.act(ev, lbr, AF.Exp, reads=["prm"], writes=[ek])
        s_, sk = self.tmp()
        sv = s_[:, 0:8].rearrange("p (d h) -> p d h", d=2)
        self.tt(sv, ev[:, :, 0, :], ev[:, :, 1, :], ALU.add, [ek], [sk])
        self.tt(sv, sv, ev[:, :, 2, :], ALU.add, [ek, sk], [sk])
        self.tt(sv, sv, ev[:, :, 3, :], ALU.add, [ek, sk], [sk])
        self.V(lambda e: e.reciprocal(out=sv, in_=sv), [sk], [sk])
        for l in range(L):
            self.tt(ev[:, :, l, :], ev[:, :, l, :], sv, ALU.mult, [ek, sk], [ek])
        self.memset(self.lbv[:, :, 0, :], 0.0, ["lbv"])
        for l in range(1, L):
            self.tt(self.lbv[:, :, l, :], self.lbv[:, :, l - 1, :], ev[:, :, l, :], ALU.add, [ek, "lbv"], ["lbv"])
        self.ts(self.lbv[:], self.lbv[:], 0.0, None, ALU.max, reads=["lbv"], writes=["lbv"])
        self.ts(self.oml[:], self.lbv[:], -1.0, 1.0, ALU.mult, ALU.add, reads=["lbv"], writes=["oml"])
        for l in range(self.nl):
            for cb in range(48):
                w, wk = self.wload(self.W["ada_w"][l], 0, 16, cb * 256, 256)
                for m in range(2):
                    ch = cb * 2 + m
                    pt, pk = self.ps()
                    mms = [(pt[:, 0:2], w[:, kc, m * 128:(m + 1) * 128], self.csb[:, kc, :], kc == 0, kc == 15)
                           for kc in range(16)]
                    self.PE(mms, [wk, "csb"], [pk])
                    b = prm[:, PRM["adab"] + l * 96 + ch:PRM["adab"] + l * 96 + ch + 1]
                    self.ts(self.modT[:, l, ch, :], pt[:, 0:2], b, None, ALU.add, reads=[pk, "prm"], writes=["modT"])
        for part in (1, 4):
            v = self.modT[:, 0:self.nl, part * 16:(part + 1) * 16, :]
            self.ts(v, v, 1.0, None, ALU.add, reads=["modT"], writes=["modT"])

    def set_norm(self, l, which, kind):
        prm = self.prm
        nm = "n1g" if which == 1 else "n2g"
        g = prm[:, PRM[nm] + l * 16:PRM[nm] + l * 16 + 16]
        base = 0 if which == 1 else 48
        shift = self.modT[:, l, base:base + 16, kind]
        scale = self.modT[:, l, base + 16:base + 32, kind]
        self.tt(self.gsh[:, 0, :], g, scale, ALU.mult, ["prm", "modT"], ["gsh"])
        self.cp(self.gsh[:, 1, :], shift, ["modT"], ["gsh"])

    def norm_stats(self, T):
        for tt_ in range(T // 512):
            sl = slice(tt_ * 512, (tt_ + 1) * 512)
            for kc in range(16):
                self.act(self.hT[:, kc, :], self.xT[:, kc, sl], AF.Square, reads=["x"], writes=[("h", kc)])
            mms = [(self.psX[:], self.onesB[:], self.hT[:, kc, :], kc == 0, kc == 15) for kc in range(16)]
            self.PE(mms, [("h", kc) for kc in range(16)] + ["onesB"], ["psX"])
            self.act(self.rstd[:, sl], self.psX[:], AF.Ln, bias=self.epsT[:, 0:1], scale=1.0 / D,
                     reads=["psX", "epsT"], writes=["rstd"])
            self.act(self.rstd[:, sl], self.rstd[:, sl], AF.Exp, scale=-0.5, reads=["rstd"], writes=["rstd"])

    def make_h(self, tt_):
        sl = slice(tt_ * 512, (tt_ + 1) * 512)
        for kc in range(16):
            t, tk = self.tmp()
            self.tt(t[:], self.xT[:, kc, sl], self.rstd[:, sl], ALU.mult, ["x", "rstd"], [tk])
            self.act(self.hT[:, kc, :], t[:], AF.Identity, bias=self.gsh[:, 1, kc:kc + 1], scale=self.gsh[:, 0, kc:kc + 1],
                     reads=[tk, "gsh"], writes=[("h", kc)])
        return [("h", kc) for kc in range(16)]

    def proj_fm(self, W, c0, ncol, rhs_list, rkeys, consume):
        nk = len(rhs_list)
        nblk = (ncol + 255) // 256
        n = rhs_list[0].shape[1]
        for b in range(nblk):
            bc = min(256, ncol - b * 256)
            w, wk = self.wload(W, 0, nk, c0 + b * 256, bc)
            for m in range(bc // 128):
                pt, pk = self.ps()
                mms = [(pt[:, 0:n], w[:, k, m * 128:(m + 1) * 128], rhs_list[k], k == 0, k == nk - 1) for k in range(nk)]
                self.PE(mms, [wk] + list(rkeys), [pk])
                consume(b * 2 + m, pt, pk)

    def layer_pass(self, l, mode):
        import os
        fw = self.fw
        T = 512 if mode == "P" else 1024
        kind = 0 if mode == "P" else 1
        if l == 0:
            src = self.xp_in if mode == "P" else self.xs_in
            self.DMA(self.xT[:, :, 0:T], src.rearrange("(k p) t -> p k t", p=128), writes=["x"])
        self.set_norm(l, 1, kind)
        self.norm_stats(T)
        fw.barrier()
        mix = os.environ.get("KMIX", "BCADF")
        if mode == "S":
            self.pre_phase(l, T)
            fw.barrier()
        if "B" in mix:
            self.mixer_B(l, mode, T)
            fw.barrier()
        if "C" in mix:
            self.mixer_C(l, mode, T)
            fw.barrier()
        if "A" in mix:
            self.mixer_A(l, mode, T)
            fw.barrier()
        if "D" in mix:
            self.mixer_D(l, mode, T)
            fw.barrier()
        if "F" in mix:
            self.wo_ffn(l, mode, T)
            fw.barrier()

    def pre_phase(self, l, T):
        self.xkeys = []
        Win = self.W["w_in"][l]
        off = 0
        ropeT, off = self.carve(off, 8 * 128, F32)
        self.ropeT = ropeT.rearrange("p (b c f) -> p b c f", b=8, c=2)
        self.DMA(self.ropeT, self.rope_in, writes=["ropeT"])
        qkg, off = self.carve(off, 256, F32)
        self.DMA(qkg, self.qkg_in[l], writes=["qkg"])
        sq, off = self.carve(off, 256, F32)
        qn, off = self.carve(off, 256, F32)
        st2, off = self.carve(off, 8, F32)
        rt, off = self.carve(off, 512, F32)
        u16, off = self.carve(off, 16, F32)
        for tt_, tb, nk_, nv_, nu, nw, c16, wcol in ((0, 0, "kf", "vf", "uL", "wL", slice(0, 16), 0),
                                                     (1, 3, "kl", "vl", "uR", "wR", slice(496, 512), 15)):
            hk = self.make_h(tt_)
            blk = tt_ * 4 + tb
            w, wk = self.wload(Win, 0, 16, CK, 256)
            pt, pk = self.ps()
            self.PE([(pt[:, 0:256], self.hT[:, kc, tb * 128:(tb + 1) * 128], w[:, kc, 0:256], kc == 0, kc == 15)
                     for kc in range(16)], [wk] + hk, [pk])
            self.qk_norm(pt, pk, sq, st2, qn, qkg[:, 128:256])
            self.rope(qn, blk, rt)
            self.xput(nk_, self.xin_v(nk_, 32768).rearrange("(t c) -> t c", c=256), qn, ["qn"])
            w, wk = self.wload(Win, 0, 16, CV, 256)
            pt, pk = self.ps()
            self.PE([(pt[:, 0:256], self.hT[:, kc, tb * 128:(tb + 1) * 128], w[:, kc, 0:256], kc == 0, kc == 15)
                     for kc in range(16)], [wk] + hk, [pk])
            self.cp(qn, pt[:, 0:256], [pk], ["qn"])
            self.xput(nv_, self.xin_v(nv_, 32768).rearrange("(t c) -> t c", c=256), qn, ["qn"])
            hr = [self.hT[:, kc, c16] for kc in range(16)]
            for ch in range(4):
                hold = {}

                def cg(m, pt, pk, hold=hold):
                    t, tk = self.tmp()
                    self.act(t[:, 0:16], pt[:, 0:16], AF.Sigmoid, reads=[pk], writes=[tk])
                    hold["g"] = (t, tk)

                self.proj_fm(Win, CCG + ch * 128, 128, hr, hk, cg)

                def ca(m, pt, pk, hold=hold, ch=ch, nu=nu):
                    t, tk = hold["g"]
                    self.tt(u16, pt[:, 0:16], t[:, 0:16], ALU.mult, [pk, tk], ["u16"])
                    dst = self.xin_v(nu, 8192).rearrange("(c p w) -> c p w", c=4, p=128)[ch]
                    self.xput(nu, dst, u16, ["u16"])

                self.proj_fm(Win, CCA + ch * 128, 128, hr, hk, ca)

                def cc(m, pt, pk, hold=hold):
                    t, tk = self.tmp()
                    self.cp(t[:, 0:16], pt[:, 0:16], [pk], [tk])
                    hold["c"] = (t, tk)

                self.proj_fm(Win, CSC + ch * 128, 128, hr, hk, cc)

                def chh(m, pt, pk, hold=hold, ch=ch, nw=nw, wcol=wcol):
                    t, tk = hold["c"]
                    self.tt(u16, pt[:, 0:16], t[:, 0:16], ALU.mult, [pk, tk], ["u16"])
                    dst = self.xin_v(nw, 512).rearrange("(c p o) -> c p o", c=4, o=1)[ch]
                    self.xput(nw, dst, u16[:, wcol:wcol + 1], ["u16"])

                self.proj_fm(Win, CSH + ch * 128, 128, hr, hk, chh)
        self.fw.barrier()
        self.mixer_D(l, "S", T, summary=True)
        self.fw.barrier()
        grp = [[0, 1, 2, 3], [4, 5, 6, 7]]
        ka = [k for k in self.xkeys if k[1] == "A"]
        kb = [k for k in self.xkeys if k[1] == "B"]
        self.fw.op("pool", lambda e: e.collective_compute("AllGather", ALU.bypass, replica_groups=grp,
                                                          ins=[self.xin["A"]], outs=[self.xout["A"]]),
                   reads=ka, writes=["xoutA"], lane="cc", linc=1)
        self.fw.op("pool", lambda e: e.collective_compute("AllGather", ALU.bypass, replica_groups=grp,
                                                          ins=[self.xin["B"]], outs=[self.xout["B"]]),
                   reads=kb + ["xoutA"], writes=["xout", "xoutA"], lane="cc", linc=1)

    def qk_norm(self, pt, pk, sq, st2, qn, gain):
        self.act(sq, pt[:, 0:256], AF.Square, reads=[pk], writes=["sq"])
        self.V(lambda e: e.reduce_sum(out=st2[:, 0:2], in_=sq.rearrange("p (h d) -> p h d", h=2), axis=AX.X),
               ["sq"], ["st2"])
        self.act(st2[:, 2:4], st2[:, 0:2], AF.Ln, bias=self.epsT[:, 0:1], scale=1.0 / 128,
                 reads=["st2", "epsT"], writes=["st2b"])
        self.act(st2[:, 4:6], st2[:, 2:4], AF.Exp, scale=-0.5, reads=["st2b"], writes=["st2c"])
        for hd in range(2):
            self.stt(qn[:, hd * 128:(hd + 1) * 128], pt[:, hd * 128:(hd + 1) * 128], st2[:, 4 + hd:5 + hd],
                     gain, ALU.mult, ALU.mult, reads=[pk, "st2c", "qkg"], writes=["qn"])

    def rope(self, qn, blk, rt):
        v = qn.rearrange("p (h a f r) -> p h a f r", h=2, a=2, f=2)
        a1 = v[:, :, :, 0, :]
        a2 = v[:, :, :, 1, :]
        cos = self.ropeT[:, blk, 0, :].rearrange("p (a r) -> p a r", a=2).unsqueeze(1).to_broadcast([128, 2, 2, 32])
        sin = self.ropeT[:, blk, 1, :].rearrange("p (a r) -> p a r", a=2).unsqueeze(1).to_broadcast([128, 2, 2, 32])
        t = [rt[:, i * 128:(i + 1) * 128].rearrange("p (h a r) -> p h a r", h=2, a=2) for i in range(4)]
        self.tt(t[0], a1, cos, ALU.mult, ["qn", "ropeT"], ["rt0"])
        self.tt(t[1], a2, sin, ALU.mult, ["qn", "ropeT"], ["rt1"])
        self.tt(t[2], a2, cos, ALU.mult, ["qn", "ropeT"], ["rt2"])
        self.tt(t[3], a1, sin, ALU.mult, ["qn", "ropeT"], ["rt3"])
        self.tt(a1, t[0], t[1], ALU.subtract, ["rt0", "rt1"], ["qn"])
        self.tt(a2, t[2], t[3], ALU.add, ["rt2", "rt3", "qn"], ["qn"])

    def branch_out(self, l, n, moT, nk, Wout, T, first):
        prm = self.prm
        for tt_ in range(T // 512):
            sl = slice(tt_ * 512, (tt_ + 1) * 512)
            hk = self.make_h(tt_)
            for cb in range(8):
                wg, wgk = self.wload(self.W["w_gate"][l], 0, 16, n * D + cb * 256, 256)
                pgs = []
                for m in range(2):
                    pg, pgk = self.ps()
                    self.PE([(pg[:], wg[:, kc, m * 128:(m + 1) * 128], self.hT[:, kc, :], kc == 0, kc == 15)
                             for kc in range(16)], [wgk] + hk, [pgk])
                    pgs.append((pg, pgk))
                wo, wok = self.wload(Wout, 0, nk, cb * 256, 256)
                for m in range(2):
                    ch = cb * 2 + m
                    pg, pgk = pgs[m]
                    py, pyk = self.ps()
                    self.PE([(py[:], wo[:, k, m * 128:(m + 1) * 128], moT[:, k, sl], k == 0, k == nk - 1)
                             for k in range(nk)], [wok, "mo"], [pyk])
                    t, tk = self.tmp()
                    bcol = PRM["bgate"] + l * 64 + n * 16 + ch
                    self.act(t[:], pg[:], AF.Sigmoid, bias=prm[:, bcol:bcol + 1], reads=[pgk, "prm"], writes=[tk])
                    if first:
                        self.tt(self.mg[:, ch, sl], t[:], py[:], ALU.mult, [tk, pyk], [("mg", ch)])
                    else:
                        self.tt(t[:], t[:], py[:], ALU.mult, [tk, pyk], [tk])
                        self.tt(self.mg[:, ch, sl], self.mg[:, ch, sl], t[:], ALU.add, [tk, ("mg", ch)], [("mg", ch)])

    def segs(self, mode):
        return [(0, 256), (256, 256)] if mode == "P" else [(0, 1024)]

    def mixer_B(self, l, mode, T):
        prm = self.prm
        segs = self.segs(mode)
        HAL = 15
        TP = T + 2 * HAL * len(segs)
        off = 0
        u, off = self.carve(off, 4 * TP)
        u = u.rearrange("p (c t) -> p c t", c=4)
        acc, off = self.carve(off, 4 * T, F32)
        acc = acc.rearrange("p (c t) -> p c t", c=4)
        yb, off = self.carve(off, 4 * T)
        yb = yb.rearrange("p (c t) -> p c t", c=4)
        self.memset(u, 0.0, ["u"])

        def upos(t0):
            for si, (s0, ln) in enumerate(segs):
                if s0 <= t0 < s0 + ln:
                    return si * (ln + 2 * HAL) + HAL + (t0 - s0)

        Win = self.W["w_in"][l]
        for tt_ in range(T // 512):
            hk = self.make_h(tt_)
            hr = [self.hT[:, kc, :] for kc in range(16)]
            for ch in range(4):
                hold = {}

                def cons_g(m, pt, pk, hold=hold):
                    t, tk = self.tmp()
                    self.act(t[:], pt[:], AF.Sigmoid, reads=[pk], writes=[tk])
                    hold["g"] = (t, tk)

                self.proj_fm(Win, CCG + ch * 128, 128, hr, hk, cons_g)

                def cons_a(m, pt, pk, ch=ch, tt_=tt_, hold=hold):
                    t, tk = hold["g"]
                    for (s0, ln) in segs:
                        a = max(s0, tt_ * 512)
                        b = min(s0 + ln, (tt_ + 1) * 512)
                        if a >= b:
                            continue
                        p0 = upos(a)
                        self.tt(u[:, ch, p0:p0 + (b - a)], pt[:, a - tt_ * 512:b - tt_ * 512],
                                t[:, a - tt_ * 512:b - tt_ * 512], ALU.mult, [pk, tk], ["u"])

                self.proj_fm(Win, CCA + ch * 128, 128, hr, hk, cons_a)
        if mode == "S":
            cand, off = self.carve(off, 4 * 4 * 16, F32)
            cand = cand.rearrange("p (c r w) -> p c r w", c=4, r=4)
            for side, nm, selb, lo, pos in ((0, "uR", 0, 1, 0), (1, "uL", 4, 0, HAL + T)):
                src = self.xout_v(nm, 8192).rearrange("r (c p w) -> c p r w", c=4, p=128)
                for ch in range(4):
                    self.DMA(cand[:, ch, :, :], src[ch], reads=["xout"], writes=["cand"])
                for ch in range(4):
                    dst = u[:, ch, pos:pos + HAL]
                    for r in range(4):
                        sc_ = prm[:, PRM["sel"] + selb + r:PRM["sel"] + selb + r + 1]
                        if r == 0:
                            self.ts(dst, cand[:, ch, r, lo:lo + HAL], sc_, None, ALU.mult, reads=["cand", "prm", "u"],
                                    writes=["u"])
                        else:
                            self.stt(dst, cand[:, ch, r, lo:lo + HAL], sc_, dst, ALU.mult, ALU.add,
                                     reads=["cand", "prm", "u"], writes=["u"])
        for ch in range(4):
            wb = PRM["cfw"] + (l * 4 + ch) * 31
            bb = PRM["cfb"] + l * 4 + ch
            for (s0, ln) in segs:
                p0 = upos(s0) - HAL
                o = acc[:, ch, s0:s0 + ln]
                self.ts(o, u[:, ch, p0:p0 + ln], prm[:, wb:wb + 1], prm[:, bb:bb + 1], ALU.mult, ALU.add,
                        reads=["u", "prm"], writes=[("acc", ch)])
                for k in range(1, 31):
                    self.stt(o, u[:, ch, p0 + k:p0 + k + ln], prm[:, wb + k:wb + k + 1], o, ALU.mult, ALU.add,
                             reads=["u", "prm", ("acc", ch)], writes=[("acc", ch)])
        for tt_ in range(T // 512):
            sl = slice(tt_ * 512, (tt_ + 1) * 512)
            pm, pmk = self.ps()
            self.PE([(pm[:], self.onesF[:], acc[:, ch, sl], ch == 0, ch == 3) for ch in range(4)],
                    [("acc", ch) for ch in range(4)] + ["onesF"], [pmk])
            mean, mk = self.tmp()
            self.ts(mean[:], pm[:], 1.0 / 512, None, ALU.mult, reads=[pmk], writes=[mk])
            for ch in range(4):
                self.tt(acc[:, ch, sl], acc[:, ch, sl], mean[:], ALU.subtract, [("acc", ch), mk], [("acc", ch)])
            pv, pvk = self.ps()
            for ch in range(4):
                t, tk = self.tmp()
                self.act(t[:], acc[:, ch, sl], AF.Square, reads=[("acc", ch)], writes=[tk])
                self.PE([(pv[:], self.onesF[:], t[:], ch == 0, ch == 3)], [tk, "onesF"] + ([pvk] if ch else []), [pvk])
            rs, rk = self.tmp()
            self.act(rs[:], pv[:], AF.Ln, bias=self.epsT[:, 0:1], scale=1.0 / 512, reads=[pvk, "epsT"], writes=[rk])
            self.act(rs[:], rs[:], AF.Exp, scale=-0.5, reads=[rk], writes=[rk])
            for ch in range(4):
                self.tt(acc[:, ch, sl], acc[:, ch, sl], rs[:], ALU.mult, [("acc", ch), rk], [("acc", ch)])
                gcol = PRM["cflg"] + l * 4 + ch
                bcol = PRM["cflb"] + l * 4 + ch
                self.act(yb[:, ch, sl], acc[:, ch, sl], AF.Silu, bias=prm[:, bcol:bcol + 1],
                         scale=prm[:, gcol:gcol + 1], reads=[("acc", ch), "prm"], writes=["mo"])
        self.branch_out(l, 1, yb, 4, self.W["w_conf_out"][l], T, first=True)

    def mixer_C(self, l, mode, T):
        prm = self.prm
        segs = self.segs(mode)
        TP = T + 2 * len(segs)
        off = 0
        w_, off = self.carve(off, 4 * TP)
        w_ = w_.rearrange("p (c t) -> p c t", c=4)
        sbb, off = self.carve(off, 4 * T)
        sbb = sbb.rearrange("p (c t) -> p c t", c=4)
        yc, off = self.carve(off, 4 * T)
        yc = yc.rearrange("p (c t) -> p c t", c=4)
        acc, off = self.carve(off, T, F32)
        self.memset(w_, 0.0, ["u"])

        def upos(t0):
            for si, (s0, ln) in enumerate(segs):
                if s0 <= t0 < s0 + ln:
                    return si * (ln + 2) + 1 + (t0 - s0)

        Win = self.W["w_in"][l]
        for tt_ in range(T // 512):
            sl = slice(tt_ * 512, (tt_ + 1) * 512)
            hk = self.make_h(tt_)
            hr = [self.hT[:, kc, :] for kc in range(16)]
            for ch in range(4):
                self.proj_fm(Win, CSB + ch * 128, 128, hr, hk,
                             lambda m, pt, pk, ch=ch, sl=sl: self.cp(sbb[:, ch, sl], pt[:], [pk], ["sbb"]))
                hold = {}

                def cons_c(m, pt, pk, hold=hold):
                    t, tk = self.tmp()
                    self.cp(t[:], pt[:], [pk], [tk])
                    hold["c"] = (t, tk)

                self.proj_fm(Win, CSC + ch * 128, 128, hr, hk, cons_c)

                def cons_h(m, pt, pk, ch=ch, tt_=tt_, hold=hold):
                    t, tk = hold["c"]
                    for (s0, ln) in segs:
                        a = max(s0, tt_ * 512)
                        b = min(s0 + ln, (tt_ + 1) * 512)
                        if a >= b:
                            continue
                        p0 = upos(a)
                        self.tt(w_[:, ch, p0:p0 + (b - a)], pt[:, a - tt_ * 512:b - tt_ * 512],
                                t[:, a - tt_ * 512:b - tt_ * 512], ALU.mult, [pk, tk], ["u"])

                self.proj_fm(Win, CSH + ch * 128, 128, hr, hk, cons_h)
        if mode == "S":
            cand, off = self.carve(off, 16, F32)
            cand = cand.rearrange("p (c r w) -> p c r w", c=4, r=4)
            for side, nm, selb, pos in ((0, "wR", 0, 0), (1, "wL", 4, 1 + T)):
                src = self.xout_v(nm, 512).rearrange("r (c p w) -> c p r w", c=4, p=128)
                for ch in range(4):
                    self.DMA(cand[:, ch, :, :], src[ch], reads=["xout"], writes=["cand"], slow=True)
                for ch in range(4):
                    dst = w_[:, ch, pos:pos + 1]
                    for r in range(4):
                        sc_ = prm[:, PRM["sel"] + selb + r:PRM["sel"] + selb + r + 1]
                        if r == 0:
                            self.ts(dst, cand[:, ch, r, :], sc_, None, ALU.mult, reads=["cand", "prm", "u"], writes=["u"])
                        else:
                            self.stt(dst, cand[:, ch, r, :], sc_, dst, ALU.mult, ALU.add, reads=["cand", "prm", "u"],
                                     writes=["u"])
        for ch in range(4):
            wb = PRM["scw"] + (l * 4 + ch) * 3
            for (s0, ln) in segs:
                p0 = upos(s0) - 1
                o = acc[:, s0:s0 + ln]
                self.ts(o, w_[:, ch, p0:p0 + ln], prm[:, wb:wb + 1], None, ALU.mult, reads=["u", "prm"], writes=["acc"])
                for k in (1, 2):
                    self.stt(o, w_[:, ch, p0 + k:p0 + k + ln], prm[:, wb + k:wb + k + 1], o, ALU.mult, ALU.add,
                             reads=["u", "prm", "acc"], writes=["acc"])
            self.tt(yc[:, ch, 0:T], acc[:, 0:T], sbb[:, ch, 0:T], ALU.mult, ["acc", "sbb"], ["mo"])
        self.branch_out(l, 2, yc, 4, self.W["w_sc_out"][l], T, first=False)

    def mixer_A(self, l, mode, T):
        prm = self.prm
        segs = self.segs(mode)
        NB = T // 128
        off = 0
        qT, off = self.carve(off, 8 * T)
        qT = qT.rearrange("p (h t) -> p h t", h=8)
        kT, off = self.carve(off, 2 * T)
        kT = kT.rearrange("p (h t) -> p h t", h=2)
        vtm, off = self.carve(off, NB * 256)
        vtm = vtm.rearrange("p (b c) -> p b c", c=256)
        base = off
        qkg, off = self.carve(off, 256, F32)
        sq, off = self.carve(off, 256, F32)
        qn, off = self.carve(off, 256, F32)
        qnb, off = self.carve(off, 256)
        st2, off = self.carve(off, 8, F32)
        if mode == "S":
            ropeT, off = self.carve(off, 8 * 128, F32)
            self.ropeT = ropeT.rearrange("p (b c f) -> p b c f", b=8, c=2)
            self.DMA(self.ropeT, self.rope_in, writes=["ropeT"])
            rt, off = self.carve(off, 512, F32)
        self.DMA(qkg, self.qkg_in[l], writes=["qkg"])
        Win = self.W["w_in"][l]
        for tt_ in range(T // 512):
            hk = self.make_h(tt_)
            for wb in range(6):
                w, wk = self.wload(Win, 0, 16, wb * 256, 256)
                for tb in range(4):
                    blk = tt_ * 4 + tb
                    tsl = slice(blk * 128, (blk + 1) * 128)
                    pt, pk = self.ps()
                    self.PE([(pt[:, 0:256], self.hT[:, kc, tb * 128:(tb + 1) * 128], w[:, kc, 0:256], kc == 0, kc == 15)
                             for kc in range(16)], [wk] + hk, [pk])
                    if wb == 5:
                        self.cp(vtm[:, blk, :], pt[:, 0:256], [pk], [("v", blk)])
                        if mode == "P":
                            self.cp(qn, pt[:, 0:256], [pk], ["qn"])
                            seq, r0 = blk // 2, (blk % 2) * 128
                            self.DMA(self.nv_out[seq, l, r0:r0 + 128, :], qn, reads=["qn"], writes=["nvo"], lane="out")
                        continue
                    self.qk_norm(pt, pk, sq, st2, qn, qkg[:, 0:128] if wb < 4 else qkg[:, 128:256])
                    if mode == "S":
                        self.rope(qn, blk, rt)
                    if wb == 4 and mode == "P":
                        seq, r0 = blk // 2, (blk % 2) * 128
                        self.DMA(self.nk_out[seq, l, r0:r0 + 128, :], qn, reads=["qn"], writes=["nko"], lane="out")
                    self.cp(qnb, qn, ["qn"], ["qnb"])
                    for hd in range(2):
                        self.TR(self.psT[:, hd * 128:(hd + 1) * 128], qnb[:, hd * 128:(hd + 1) * 128], self.identB[:],
                                ["qnb", "identB"], ["psT"])
                    if wb < 4:
                        dst = qT[:, wb * 2:wb * 2 + 2, tsl]
                        key = ("qT", wb // 2, blk)
                    else:
                        dst = kT[:, :, tsl]
                        key = ("kT", blk)
                    self.cp(dst, self.psT[:, 0:256].rearrange("p (h t) -> p h t", h=2), ["psT"], [key])
        self.fw.barrier()
        off = base
        pT = []
        for i in range(2):
            a, off = self.carve(off, 512)
            pT.append(a)
        sinkx, off = self.carve(off, 1024, F32)
        den, off = self.carve(off, 512, F32)
        s8, off = self.carve(off, 8, F32)
        sx = prm[:, PRM["sink"] + l * 8:PRM["sink"] + l * 8 + 8]
        self.act(s8, sx, AF.Exp, reads=["prm"], writes=["s8"])
        for h in range(8):
            self.cp(sinkx[:, h * 128:(h + 1) * 128], s8[:, h:h + 1].to_broadcast([128, 128]), ["s8"], ["sinkx"])
        if mode == "S":
            ctmp, off = self.carve(off, 4 * 256)
            ctmp = ctmp.rearrange("p (r c) -> p r c", r=4)
            ctxK, off = self.carve(off, 2 * 512)
            ctxK = ctxK.rearrange("p (h s) -> p h s", h=2)
            ctxV, off = self.carve(off, 4 * 256)
            ctxV = ctxV.rearrange("p (b c) -> p b c", b=4)
            candK, off = self.carve(off, 4 * 2 * 128)
            candK = candK.rearrange("p (r h s) -> p r h s", r=4, h=2)
            candV, off = self.carve(off, 4 * 256)
            candV = candV.rearrange("p (r c) -> p r c", r=4)
            biasB, off = self.carve(off, 10 * 128)
            biasB = biasB.rearrange("p (i t) -> p i t", i=10)
            self.DMAc(biasB, self.abias_in, writes=["biasB"])
            self.DMAc(ctxV, self.cv_in[l].rearrange("(b p) c -> p b c", p=128), writes=["ctx"])
            self.DMAc(ctmp, self.ck_in[l].rearrange("(b p) c -> p b c", p=128), writes=["ctmp"])
            for b in range(4):
                for h in range(2):
                    self.TR(self.psT[:, 0:128], ctmp[:, b, h * 128:(h + 1) * 128], self.identB[:], ["ctmp", "identB"], ["psT"])
                    self.cp(ctxK[:, h, b * 128:(b + 1) * 128], self.psT[:, 0:128], ["psT"], ["ctx"])

            def load_cands(nk_, nv_):
                self.DMAc(candV, self.xout_v(nv_, 32768).rearrange("r (t c) -> t r c", c=256), reads=["xout"],
                          writes=["cand"])
                self.DMAc(ctmp, self.xout_v(nk_, 32768).rearrange("r (t c) -> t r c", c=256), reads=["xout"],
                          writes=["ctmp"])
                for r in range(4):
                    for h in range(2):
                        self.TR(self.psT[:, 0:128], ctmp[:, r, h * 128:(h + 1) * 128], self.identB[:], ["ctmp", "identB"],
                                ["psT"])
                        self.cp(candK[:, r, h, :], self.psT[:, 0:128], ["psT"], ["cand"])
        pi = 0
        for (s0, ln) in segs:
            b0, nb = s0 // 128, ln // 128
            for qb in range(b0, b0 + nb):
                qsl = slice(qb * 128, (qb + 1) * 128)
                if mode == "S" and qb == 0:
                    load_cands("kl", "vl")
                if mode == "S" and qb == NB - 1:
                    load_cands("kf", "vf")
                for kvh in range(2):
                    hs = slice(kvh * 128, (kvh + 1) * 128)

                    def loc(kb, bias):
                        return (kT[:, kvh, kb * 128:(kb + 1) * 128], vtm[:, kb, hs], [("kT", kb), ("v", kb)], bias)

                    if mode == "P":
                        kbs = [loc(kb, None) for kb in range(b0, b0 + nb)]
                    else:
                        kbs = []
                        if qb == 0:
                            kbs += [(candK[:, r, kvh, :], candV[:, r, hs], ["cand"], 2 + r) for r in range(4)]
                        else:
                            kbs.append(loc(qb - 1, 0))
                        kbs.append(loc(qb, None))
                        if qb == NB - 1:
                            kbs += [(candK[:, r, kvh, :], candV[:, r, hs], ["cand"], 6 + r) for r in range(4)]
                        else:
                            kbs.append(loc(qb + 1, 1))
                        kbs += [(ctxK[:, kvh, cb * 128:(cb + 1) * 128], ctxV[:, cb, hs], ["ctx"], None) for cb in range(4)]
                    qrhs = qT[:, kvh * 4:(kvh + 1) * 4, qsl]
                    ppv, ppvk = self.psb[0], ("ps", 0)
                    pden, pdenk = self.psb[1], ("ps", 1)
                    nkb = len(kbs)
                    for i, (kap, vap, keys, bias) in enumerate(kbs):
                        psc, psck = self.psb[2 + pi % 4], ("ps", 2 + pi % 4)
                        mms = [(psc[:], kap, qrhs, True, bias is None)]
                        rk = list(keys) + [("qT", kvh, qb)]
                        if bias is not None:
                            mms += [(psc[:, g * 128:(g + 1) * 128], self.identB[:], biasB[:, bias, :], False, g == 3)
                                    for g in range(4)]
                            rk += ["identB", "biasB"]
                        self.PE(mms, rk, [psck])
                        p = pT[pi % 2]
                        pkk = ("pT", pi % 2)
                        pi += 1
                        self.act(p, psc[:], AF.Exp, scale=SCALE, reads=[psck], writes=[pkk])
                        self.PE([(ppv[:], vap, p, i == 0, i == nkb - 1)], list(keys) + [pkk] + ([ppvk] if i else []), [ppvk])
                        self.PE([(pden[:], self.onesB[:], p, i == 0, i == nkb - 1)], ["onesB", pkk] + ([pdenk] if i else []),
                                [pdenk])
                    self.tt(den, pden[:], sinkx[:, kvh * 512:(kvh + 1) * 512], ALU.add, [pdenk, "sinkx"], ["den"])
                    self.V(lambda e: e.reciprocal(out=den, in_=den), ["den"], ["den"])
                    self.tt(qrhs, ppv[:].rearrange("p (h t) -> p h t", h=4), den.rearrange("p (h t) -> p h t", h=4),
                            ALU.mult, [ppvk, "den"], [("qT", kvh, qb), "mo"])
        self.branch_out(l, 0, qT, 8, self.W["w_attn_out"][l], T, first=False)

    def mixer_D(self, l, mode, T, summary=False):
        prm = self.prm
        segs = self.segs(mode)
        NB = T // 128
        NCH = T // 32
        off = 0
        if not summary:
            od, off = self.carve(off, 4 * T)
            od = od.rearrange("p (h t) -> p h t", h=4)
        base = off
        Win = self.W["w_in"][l]
        for hh in range(4):
            off = base
            vt, off = self.carve(off, NB * 128)
            vt = vt.rearrange("p (b c) -> p b c", c=128)
            lf = []
            for d in range(2):
                a, off = self.carve(off, T, F32)
                lf.append(a)
            Bc, off = self.carve(off, T, F32)
            Ex, off = self.carve(off, T, F32)
            kk, off = self.carve(off, T)
            Kt, off = self.carve(off, T)
            Ktm, off = self.carve(off, 4 * 128)
            Ktm = Ktm.rearrange("p (c d) -> p c d", c=4)
            Sf32, off = self.carve(off, 128, F32)
            Sbf = []
            for i in range(2):
                a, off = self.carve(off, 128)
                Sbf.append(a)
            Ach, off = self.carve(off, NCH, F32)
            At, off = self.carve(off, 2, F32)
            if not summary:
                qf, off = self.carve(off, T)
                gT, off = self.carve(off, T)
                o32, off = self.carve(off, T, F32)
                Qi, off = self.carve(off, T)
                Qc, off = self.carve(off, T)
                Kc, off = self.carve(off, T)
                attS, off = self.carve(off, 128)
                Bmid, off = self.carve(off, NCH, F32)
                if mode == "S":
                    Sc, off = self.carve(off, 4 * 128, F32)
                    Sc = Sc.rearrange("p (r c) -> p r c", r=4)
                    Ac, off = self.carve(off, 4, F32)
                    coef, off = self.carve(off, 2, F32)
                    tS, off = self.carve(off, 128, F32)
            for tt_ in range(T // 512):
                sl = slice(tt_ * 512, (tt_ + 1) * 512)
                hk = self.make_h(tt_)
                hr = [self.hT[:, kc, :] for kc in range(16)]
                if not summary:
                    self.proj_fm(Win, CHQ + hh * 128, 128, hr, hk,
                                 lambda m, pt, pk, sl=sl: self.act(qf[:, sl], pt[:], AF.Silu, reads=[pk], writes=["qf0"]))
                    self.proj_fm(Win, CHG + hh * 128, 128, hr, hk,
                                 lambda m, pt, pk, sl=sl: self.act(gT[:, sl], pt[:], AF.Silu, reads=[pk], writes=["gT"]))
                for d, cbase in ((0, CHF), (1, CHB)):
                    def cons_f(m, pt, pk, d=d, sl=sl):
                        t, tk = self.tmp()
                        self.act(t[:], pt[:], AF.Sigmoid, reads=[pk], writes=[tk])
                        self.ts(t[:], t[:], self.oml[:, d, l, hh:hh + 1], self.lbv[:, d, l, hh:hh + 1], ALU.mult, ALU.add,
                                reads=[tk, "oml", "lbv"], writes=[tk])
                        self.act(lf[d][:, sl], t[:], AF.Ln, reads=[tk], writes=[("lf", d)])
                    self.proj_fm(Win, cbase + hh * 128, 128, hr, hk, cons_f)
                w, wk = self.wload(Win, 0, 16, CHI + hh * 128, 128)
                for tb in range(4):
                    blk = tt_ * 4 + tb
                    pt, pk = self.ps()
                    self.PE([(pt[:, 0:128], self.hT[:, kc, tb * 128:(tb + 1) * 128], w[:, kc, 0:128], kc == 0, kc == 15)
                             for kc in range(16)], [wk] + hk, [pk])
                    self.cp(vt[:, blk, :], pt[:, 0:128], [pk], ["vt"])
            if not summary:
                self.ts(qf[:, 0:T], qf[:, 0:T], SCALE, None, ALU.mult, reads=["qf0"], writes=["qf"])
            for d in range(2):
                self.act(Ex[:, 0:T], lf[d][:, 0:T], AF.Exp, reads=[("lf", d)], writes=["Ex"])
                self.ts(kk[:, 0:T], Ex[:, 0:T], -1.0, 1.0, ALU.mult, ALU.add, reads=["Ex"], writes=["kk"])
                B3 = Bc.rearrange("p (c j) -> p c j", j=32)
                E3 = Ex.rearrange("p (c j) -> p c j", j=32)
                self.V(lambda e, d=d: e.tensor_tensor_scan(out=Bc[:, 0:T], data0=self.smask[:, 0:T], data1=lf[d][:, 0:T],
                                                          initial=0.0, op0=ALU.mult, op1=ALU.add),
                       [("lf", d), "smask"], ["Bc"])
                self.cp(Ach[:, 0:NCH], B3[:, 0:NCH, 31], ["Bc"], ["Btot"])
                if d == 1:
                    self.tt(B3[:, 0:NCH, :], Ach[:, 0:NCH].unsqueeze(2).to_broadcast([128, NCH, 32]), B3[:, 0:NCH, :],
                            ALU.subtract, ["Bc", "Btot"], ["Bc"])
                    self.tt(Bc[:, 0:T], Bc[:, 0:T], lf[d][:, 0:T], ALU.add, ["Bc", ("lf", d)], ["Bc"])
                if not summary:
                    self.cp(Bmid[:, 0:NCH], B3[:, 0:NCH, 15], ["Bc"], ["Bmid"])
                    self.act(Ex[:, 0:T], Bc[:, 0:T], AF.Exp, reads=["Bc", "kk"], writes=["Ex"])
                    self.tt(Qi[:, 0:T], qf[:, 0:T], Ex[:, 0:T], ALU.mult, ["qf", "Ex"], ["Qi"])
                self.tt(E3[:, 0:NCH, :], Ach[:, 0:NCH].unsqueeze(2).to_broadcast([128, NCH, 32]), B3[:, 0:NCH, :],
                        ALU.subtract, ["Bc", "Btot", "Ex", "Qi", "kk"], ["Ex"])
                self.act(Ex[:, 0:T], Ex[:, 0:T], AF.Exp, reads=["Ex"], writes=["Ex"])
                self.tt(Kt[:, 0:T], kk[:, 0:T], Ex[:, 0:T], ALU.mult, ["kk", "Ex"], ["Kt"])
                if not summary:
                    self.tt(E3[:, 0:NCH, :], B3[:, 0:NCH, :], Bmid[:, 0:NCH].unsqueeze(2).to_broadcast([128, NCH, 32]),
                            ALU.subtract, ["Bc", "Bmid", "Ex", "Kt"], ["Ex"])
                    self.ts(Ex[:, 0:T], Ex[:, 0:T], -40.0, 40.0, ALU.max, ALU.min, reads=["Ex"], writes=["Ex"])
                    self.act(Bc[:, 0:T], Ex[:, 0:T], AF.Exp, reads=["Ex", "Bmid"], writes=["Bc"])
                    self.tt(Qc[:, 0:T], qf[:, 0:T], Bc[:, 0:T], ALU.mult, ["qf", "Bc"], ["Qc"])
                    self.act(Bc[:, 0:T], Ex[:, 0:T], AF.Exp, scale=-1.0, reads=["Ex", "Qc"], writes=["Bc"])
                    self.tt(Kc[:, 0:T], kk[:, 0:T], Bc[:, 0:T], ALU.mult, ["kk", "Bc"], ["Kc"])
                if summary:
                    self.V(lambda e: e.reduce_sum(out=At[:, 0:1], in_=Ach[:, 0:NCH], axis=AX.X), ["Btot"], ["At"])
                    self.act(At[:, 0:1], At[:, 0:1], AF.Exp, reads=["At"], writes=["At"])
                    dst = self.xin_v("Af" if d == 0 else "Ab", 512).rearrange("(h p o) -> h p o", h=4, o=1)[hh]
                    self.xput("A%d" % d, dst, At[:, 0:1], ["At"])
                self.act(Ach[:, 0:NCH], Ach[:, 0:NCH], AF.Exp, reads=["Btot", "Ex", "At"], writes=["Ach"])
                for si, (s0, ln) in enumerate(segs):
                    nblk = ln // 128
                    blks = list(range(s0 // 128, s0 // 128 + nblk))
                    if d == 1:
                        blks = blks[::-1]
                    if mode == "P" or summary:
                        self.memset(Sf32, 0.0, ["S"])
                    else:
                        s0in = (self.s0f_in if d == 0 else self.s0b_in)[l, hh]
                        self.DMA(Sf32, s0in, writes=["S"])
                        self.DMA(Sc, self.xout_v("Sf" if d == 0 else "Sb", 65536).rearrange(
                            "r (h p c) -> h p r c", h=4, p=128)[hh], reads=["xout"], writes=["Sc"])
                        self.DMA(Ac.rearrange("p (r o) -> p r o", o=1), self.xout_v("Af" if d == 0 else "Ab", 512).rearrange(
                            "r (h p o) -> h p r o", h=4, o=1)[hh], reads=["xout"], writes=["Ac"], slow=True)
                        order = [0, 1, 2, 3] if d == 0 else [3, 2, 1, 0]
                        sb_ = PRM["sel"] + (8 if d == 0 else 12)
                        for r in order:
                            m_ = prm[:, sb_ + r:sb_ + r + 1]
                            om_ = prm[:, sb_ + 16 + r:sb_ + 16 + r + 1]
                            self.ts(coef[:, 0:1], Ac[:, r:r + 1], m_, om_, ALU.mult, ALU.add, reads=["Ac", "prm"],
                                    writes=["coef"])
                            self.ts(tS, Sc[:, r, :], m_, None, ALU.mult, reads=["Sc", "prm"], writes=["tS"])
                            self.stt(Sf32, Sf32, coef[:, 0:1], tS, ALU.mult, ALU.add, reads=["S", "coef", "tS"], writes=["S"])
                    self.cp(Sbf[0], Sf32, ["S"], [("Sb", 0)])
                    sidx = 0
                    for blk in blks:
                        bsl = slice(blk * 128, (blk + 1) * 128)
                        if not summary:
                            pa, pak = self.ps()
                            self.PE([(pa[:, 0:128], Kc[:, bsl], Qc[:, bsl], True, True)], ["Kc", "Qc"], [pak])
                            self.tt(attS, pa[:, 0:128], self.hmask[:, d, :], ALU.mult, [pak, "hmask"], ["attS"])
                        self.TR(self.psT[:, 512:640], Kt[:, bsl], self.identB[:], ["Kt", "identB"], ["psT2"])
                        for c in range(4):
                            self.ts(Ktm[:, c, :], self.psT[:, 512:640], prm[:, PRM["cm"] + c:PRM["cm"] + c + 1], None,
                                    ALU.mult, reads=["psT2", "prm"], writes=[("Ktm", c)])
                        if not summary:
                            po, pok = self.ps()
                            self.PE([(po[:, 0:128], vt[:, blk, :], attS, True, False)], ["vt", "attS"], [pok])
                        corder = [0, 1, 2, 3] if d == 0 else [3, 2, 1, 0]
                        for ci, c in enumerate(corder):
                            gch = blk * 4 + c
                            csl = slice(gch * 32, gch * 32 + 32)
                            if not summary:
                                self.PE([(po[:, c * 32:(c + 1) * 32], Sbf[sidx % 2], Qi[:, csl], False, ci == 3)],
                                        [("Sb", sidx % 2), "Qi", pok], [pok])
                            pu, puk = self.ps()
                            self.PE([(pu[:, 0:128], Ktm[:, c, :], vt[:, blk, :], True, True)], [("Ktm", c), "vt"], [puk])
                            self.stt(Sf32, Sf32, Ach[:, gch:gch + 1], pu[:, 0:128], ALU.mult, ALU.add,
                                     reads=["S", "Ach", puk], writes=["S"])
                            sidx += 1
                            if not summary:
                                self.cp(Sbf[sidx % 2], Sf32, ["S"], [("Sb", sidx % 2)])
                        if not summary:
                            if d == 0:
                                self.cp(o32[:, bsl], po[:, 0:128], [pok], ["o32"])
                            else:
                                self.tt(o32[:, bsl], o32[:, bsl], po[:, 0:128], ALU.add, [pok, "o32"], ["o32"])
                    if mode == "P":
                        dst = (self.nsf_out if d == 0 else self.nsb_out)[si, l, hh]
                        self.DMA(dst, Sf32, reads=["S"], writes=["nso"], lane="out")
                    if summary:
                        dst = self.xin_v("Sf" if d == 0 else "Sb", 65536).rearrange("(h p c) -> h p c", h=4, p=128)[hh]
                        self.xput("S%d" % d, dst, Sf32, ["S"])
            if not summary:
                for tt_ in range(T // 512):
                    sl = slice(tt_ * 512, (tt_ + 1) * 512)
                    t, tk = self.tmp()
                    self.act(t[:], o32[:, sl], AF.Square, reads=["o32"], writes=[tk])
                    pm, pmk = self.ps()
                    self.PE([(pm[:], self.onesF[:], t[:], True, True)], [tk, "onesF"], [pmk])
                    r, rk = self.tmp()
                    self.act(r[:], pm[:], AF.Ln, bias=self.epsT[:, 0:1], scale=1.0 / 128, reads=[pmk, "epsT"], writes=[rk])
                    self.act(r[:], r[:], AF.Exp, scale=-0.5, reads=[rk], writes=[rk])
                    self.tt(r[:], r[:], o32[:, sl], ALU.mult, [rk, "o32"], [rk])
                    self.stt(od[:, hh, sl], r[:], prm[:, PRM["hgng"] + l:PRM["hgng"] + l + 1], gT[:, sl], ALU.mult, ALU.mult,
                             reads=[rk, "prm", "gT"], writes=["mo"])
            self.fw.barrier()
        if not summary:
            self.branch_out(l, 3, od, 4, self.W["w_hg_out"][l], T, first=False)

    def wo_ffn(self, l, mode, T):
        kind = 0 if mode == "P" else 1
        g1 = self.modT[:, l, 32:48, kind]
        g2 = self.modT[:, l, 80:96, kind]
        for tt_ in range(T // 512):
            sl = slice(tt_ * 512, (tt_ + 1) * 512)
            for cb in range(8):
                w, wk = self.wload(self.W["w_o"][l], 0, 16, cb * 256, 256)
                for m in range(2):
                    ch = cb * 2 + m
                    pt, pk = self.ps()
                    self.PE([(pt[:], w[:, kc, m * 128:(m + 1) * 128], self.mg[:, kc, sl], kc == 0, kc == 15)
                             for kc in range(16)], [wk] + [("mg", kc) for kc in range(16)], [pk])
                    self.stt(self.xT[:, ch, sl], pt[:], g1[:, ch:ch + 1], self.xT[:, ch, sl], ALU.mult, ALU.add,
                             reads=[pk, "modT", "x"], writes=["x"])
        self.fw.barrier()
        self.set_norm(l, 2, kind)
        self.norm_stats(T)
        ff, _ = self.carve(0, 11 * 512)
        ff = ff.rearrange("p (j t) -> p j t", j=11)
        for tt_ in range(T // 512):
            sl = slice(tt_ * 512, (tt_ + 1) * 512)
            hk = self.make_h(tt_)
            hr = [self.hT[:, kc, :] for kc in range(16)]
            for grp in range(4):
                for j in range(11):
                    col = (grp * 11 + j) * 128
                    hold = {}

                    def c1(m, pt, pk, hold=hold):
                        t, tk = self.tmp()
                        self.act(t[:], pt[:], AF.Silu, reads=[pk], writes=[tk])
                        hold["s"] = (t, tk)

                    self.proj_fm(self.W["ffn_w1"][l], col, 128, hr, hk, c1)

                    def c3(m, pt, pk, hold=hold, j=j):
                        t, tk = hold["s"]
                        self.tt(ff[:, j, :], t[:], pt[:], ALU.mult, [tk, pk], [("ff", j)])

                    self.proj_fm(self.W["ffn_w3"][l], col, 128, hr, hk, c3)
                for cb in range(8):
                    w, wk = self.wload(self.W["ffn_w2"][l], grp * 11, 11, cb * 256, 256)
                    for m in range(2):
                        ch = cb * 2 + m
                        pt, pk = self.ps()
                        self.PE([(pt[:], w[:, j, m * 128:(m + 1) * 128], ff[:, j, :], j == 0, j == 10) for j in range(11)],
                                [wk] + [("ff", j) for j in range(11)], [pk])
                        self.stt(self.xT[:, ch, sl], pt[:], g2[:, ch:ch + 1], self.xT[:, ch, sl], ALU.mult, ALU.add,
                                 reads=[pk, "modT", "x"], writes=["x"])


_CACHE = {}


def host_consts(core):
    j = core % 4
    ident = np.eye(128, dtype=np.float32)
    s_ = np.arange(128)[:, None]
    t_ = np.arange(128)[None, :]
    same = (s_ // 32) == (t_ // 32)
    hm = np.zeros((128, 2, 128), np.float32)
    hm[:, 0, :] = (same & (s_ <= t_)).astype(np.float32)
    hm[:, 1, :] = (same & (s_ >= t_)).astype(np.float32)
    sm = np.ones((128, 1024), np.float32)
    sm[:, ::32] = 0.0
    low = np.where(s_ >= t_, 0.0, NEG).astype(np.float32)
    up = np.where(s_ <= t_, 0.0, NEG).astype(np.float32)
    allneg = np.full((128, 128), NEG, np.float32)
    ab = np.zeros((128, 10, 128), np.float32)
    ab[:, 0, :] = low
    ab[:, 1, :] = up
    for r in range(4):
        ab[:, 2 + r, :] = low if r == j - 1 else allneg
        ab[:, 6 + r, :] = up if r == j + 1 else allneg
    tg = j * 1024 + np.arange(1024)
    row = (tg // 64).astype(np.float32)
    col = (tg % 64).astype(np.float32)
    inv = (np.float32(10000.0) ** (-np.arange(0, 64, 2, dtype=np.float32) / np.float32(64))).astype(np.float32)
    ang = np.stack([row[:, None] * inv, col[:, None] * inv], axis=1).astype(np.float32)
    cs = np.stack([np.cos(ang), np.sin(ang)], axis=1).reshape(1024, 2, 64)
    rope = np.ascontiguousarray(cs.reshape(8, 128, 2, 64).transpose(1, 0, 2, 3)).astype(np.float32)
    return ident, hm, sm, ab, rope


def make_in_maps(inp, pr):
    in_maps = []
    xp = inp["x_prompt"].astype(np.float32)
    xs = inp["x_sample"].astype(np.float32)
    for c in range(8):
        g, j = c // 4, c % 4
        ident, hm, sm, ab, rope = host_consts(c)
        m = {
            "xT_p": np.ascontiguousarray(xp[2 * c:2 * c + 2].reshape(512, D).T),
            "xT_s": np.ascontiguousarray(xs[g, j * 1024:(j + 1) * 1024].T),
            "ck": np.ascontiguousarray(inp["cache_k"][g].reshape(L, 512, 256)),
            "cv": np.ascontiguousarray(inp["cache_v"][g].reshape(L, 512, 256)),
            "s0f": np.ascontiguousarray(inp["state_hgrn_fwd"][g]),
            "s0b": np.ascontiguousarray(inp["state_hgrn_bwd"][g]),
            "cvec": np.ascontiguousarray(np.stack([fm(inp["c_ctx"], 16), fm(inp["c"][g], 16)], axis=-1)),
            "prm": pack_params(inp, c),
            "qkg": np.ascontiguousarray(np.broadcast_to(
                np.concatenate([inp["q_norm_g"], inp["k_norm_g"]], axis=1)[:, None, :], (L, 128, 256))).astype(np.float32),
            "ident": ident, "hmask": hm, "smask": sm, "abias": ab, "rope": rope,
        }
        for n in pr.W:
            m[n] = np.ascontiguousarray(inp[n][:pr.nl], dtype=np.float32)
        in_maps.append(m)
    return in_maps


def run_and_gather(nc, in_maps):
    res = run_bass_kernel_spmd(nc, in_maps, core_ids=list(range(8)))
    R = res.results
    yp = np.stack([R[c]["yT_p"].T.reshape(2, 256, D) for c in range(8)]).reshape(16, 256, D)
    ys = np.stack([np.concatenate([R[g * 4 + j]["yT_s"].T for j in range(4)], 0) for g in range(2)])
    nk = np.concatenate([R[c]["nk"] for c in range(8)], 0).reshape(16, L, 256, 2, 128)
    nv = np.concatenate([R[c]["nv"] for c in range(8)], 0).reshape(16, L, 256, 2, 128)
    nsf = np.concatenate([R[c]["nsf"] for c in range(8)], 0)
    nsb = np.concatenate([R[c]["nsb"] for c in range(8)], 0)
    return (yp.astype(np.float32), ys.astype(np.float32), nk.astype(np.float32), nv.astype(np.float32),
            nsf.astype(np.float32), nsb.astype(np.float32))


def kernel(**inp):
    inp = {k: np.asarray(v) for k, v in inp.items()}
    if "pr" not in _CACHE:
        pr = Prog(do_sample=True)
        _CACHE["pr"] = pr
        _CACHE["nc"] = pr.build()
    pr = _CACHE["pr"]
    return run_and_gather(_CACHE["nc"], make_in_maps(inp, pr))
```

```python
import numpy as np
from contextlib import ExitStack
import concourse.bass as bass
import concourse.mybir as mybir
from concourse.bass_utils import run_bass_kernel_spmd

F32 = mybir.dt.float32
BF16 = mybir.dt.bfloat16
AF = mybir.ActivationFunctionType
ALU = mybir.AluOpType
AX = mybir.AxisListType

D = 2048
KD = 16
L = 4
DFF = 5632
INC = 6656
EPS = 1e-6
SCALE = 128 ** -0.5
NEG = -30000.0
ARENA = 25600
CQ, CK, CV, CCA, CCG, CSB, CSC, CSH, CHQ, CHF, CHB, CHI, CHG = (
    0, 1024, 1280, 1536, 2048, 2560, 3072, 3584, 4096, 4608, 5120, 5632, 6144)


class Sem:
    def __init__(self, name):
        self.name = name
        self.h = None
        self.count = 0


class FW:
    ENGS = ("pe", "act", "dve", "pool", "sp")

    def __init__(self):
        self.sems = {}
        self.rec = {e: [] for e in self.ENGS}
        self.waited = {e: {} for e in self.ENGS}
        self.lastw = {}
        self.readers = {}
        for e in self.ENGS:
            self.sem(e)

    def sem(self, name):
        if name not in self.sems:
            self.sems[name] = Sem(name)
        return self.sems[name]

    def op(self, eng, fn, reads=(), writes=(), lane=None, linc=16):
        deps = {}

        def add(ev):
            if ev is not None and deps.get(ev[0], 0) < ev[1]:
                deps[ev[0]] = ev[1]

        for k in reads:
            add(self.lastw.get(k))
        for k in writes:
            add(self.lastw.get(k))
            for ev in self.readers.get(k, {}).items():
                add(ev)
        if eng == "pe":
            deps.pop("pe", None)
        waits = []
        wd = self.waited[eng]
        for s, v in deps.items():
            if wd.get(s, 0) < v:
                wd[s] = v
                waits.append((s, v))
        if lane is None:
            sm = self.sems[eng]
            sm.count += 1
            inc = 1
        else:
            sm = self.sem(lane)
            sm.count += linc
            inc = linc
        ev = (sm.name, sm.count)
        self.rec[eng].append((waits, fn, sm.name, inc))
        for k in writes:
            self.lastw[k] = ev
            self.readers[k] = {}
        for k in reads:
            r = self.readers.setdefault(k, {})
            if r.get(ev[0], 0) < ev[1]:
                r[ev[0]] = ev[1]
        return ev

    def barrier(self, engs=("pe", "act", "dve", "sp")):
        cur = [(s.name, s.count) for s in self.sems.values() if s.count > 0]
        for e in engs:
            waits = []
            wd = self.waited[e]
            for s, v in cur:
                if wd.get(s, 0) < v:
                    wd[s] = v
                    waits.append((s, v))
            if waits:
                self.rec[e].append((waits, None, None, 0))

    def emit(self, nc, stack):
        for s in self.sems.values():
            s.h = stack.enter_context(nc.semaphore(s.name))
        block = stack.enter_context(nc.Block())
        sems = self.sems

        def run(e, lst):
            for waits, fn, sname, inc in lst:
                for s, v in waits:
                    e.wait_ge(sems[s].h, v)
                if fn is not None:
                    fn(e).then_inc(sems[sname].h, inc)

        rec = self.rec

        @block.tensor
        def _(e):
            run(e, rec["pe"])

        @block.scalar
        def _(e):
            run(e, rec["act"])

        @block.vector
        def _(e):
            run(e, rec["dve"])

        @block.gpsimd
        def _(e):
            run(e, rec["pool"])

        @block.sync
        def _(e):
            run(e, rec["sp"])


PRM = {}
_off = 0
for _n, _w in (("n1g", L * 16), ("n2g", L * 16), ("adab", L * 96), ("bgate", L * 64), ("cfw", L * 4 * 31),
               ("cfb", L * 4), ("cflg", L * 4), ("cflb", L * 4), ("scw", L * 4 * 3), ("hglb", 2 * L * 4),
               ("hgng", L), ("sink", L * 8), ("sel", 32), ("cm", 4)):
    PRM[_n] = _off
    _off += _w
NP_ = _off

XR = {"A": 1280, "B": 1152}
XO = dict(kf=("A", 0), kl=("A", 32768), vf=("A", 65536), vl=("A", 98304), uL=("A", 131072), uR=("A", 139264),
          wL=("A", 147456), wR=("A", 147968), Sf=("B", 0), Sb=("B", 65536), Af=("B", 131072), Ab=("B", 131584))


def fm(v, nchunks):
    v = np.asarray(v, np.float32)
    lead = v.shape[:-1]
    v = v.reshape(lead + (nchunks, 128))
    return np.moveaxis(v, -1, 0)


def pack_params(inp, core):
    P = np.zeros((128, NP_), np.float32)

    def put(name, arr):
        a = np.ascontiguousarray(arr).reshape(128, -1)
        P[:, PRM[name]:PRM[name] + a.shape[1]] = a

    put("n1g", fm(inp["norm1_g"], 16))
    put("n2g", fm(inp["norm2_g"], 16))
    put("adab", fm(inp["ada_b"], 96))
    put("bgate", fm(inp["b_gate"], 64))
    put("cfw", np.transpose(fm(np.transpose(inp["conf_dw_w"], (0, 1, 2)), 4), (0, 1, 3, 2)))
    put("cfb", fm(inp["conf_dw_b"], 4))
    put("cflg", fm(inp["conf_ln_g"], 4))
    put("cflb", fm(inp["conf_ln_b"], 4))
    put("scw", np.transpose(fm(inp["sc_conv_w"], 4), (0, 1, 3, 2)))
    put("hglb", fm(inp["hg_lb"], 4))
    put("hgng", np.transpose(np.asarray(inp["hg_norm_g"], np.float32), (1, 0)))
    put("sink", np.broadcast_to(np.asarray(inp["attn_sink"], np.float32).reshape(1, L * 8), (128, L * 8)))
    j = core % 4
    sel = np.zeros(16, np.float32)
    for i in range(4):
        sel[i] = 1.0 if i == j - 1 else 0.0
        sel[4 + i] = 1.0 if i == j + 1 else 0.0
        sel[8 + i] = 1.0 if i < j else 0.0
        sel[12 + i] = 1.0 if i > j else 0.0
    sel = np.concatenate([sel, 1.0 - sel])
    put("sel", np.broadcast_to(sel.reshape(1, 32), (128, 32)))
    cm = np.zeros((128, 4), np.float32)
    for p in range(128):
        cm[p, p // 32] = 1.0
    put("cm", cm)
    return P


class Prog:
    def __init__(self, do_sample=True, nl=L):
        self.do_sample = do_sample
        self.nl = nl
        self.nc = bass.Bass("TRN2", target_bir_lowering=False)
        self.fw = FW()
        self.wi = 0
        self.psi = 0
        self.tmpi = 0
        self.xkeys = []

    def dram_in(self, name, shape):
        return self.nc.dram_tensor(name, list(shape), F32, kind="ExternalInput").ap()

    def dram_out(self, name, shape):
        return self.nc.dram_tensor(name, list(shape), F32, kind="ExternalOutput").ap()

    def V(self, fn, reads=(), writes=()):
        return self.fw.op("dve", fn, reads, writes)

    def A(self, fn, reads=(), writes=()):
        return self.fw.op("act", fn, reads, writes)

    def PE(self, mms, reads=(), writes=()):
        def fn(e, mms=mms):
            ins = None
            for (o, l, r, st, sp) in mms:
                ins = e.matmul(o, l, r, start=st, stop=sp)
            return ins
        return self.fw.op("pe", fn, reads, writes)

    def TR(self, out, in_, ident, reads=(), writes=()):
        return self.fw.op("pe", lambda e: e.transpose(out, in_, ident), reads, writes)

    def DMA(self, out, in_, reads=(), writes=(), lane="ld", eng="sp", slow=False):
        if slow:
            ev = self.fw.op(eng, lambda e: e.dma_start(out=out, in_=in_, allow_slow_non_contiguous=True), reads, writes,
                            lane=lane)
        else:
            ev = self.fw.op(eng, lambda e: e.dma_start(out=out, in_=in_), reads, writes, lane=lane)
        wd = self.fw.waited[eng]
        if wd.get(ev[0], 0) < ev[1]:
            wd[ev[0]] = ev[1]
            self.fw.rec[eng].append(([ev], None, None, 0))
        return ev

    def DMAc(self, out, in_, reads=(), writes=()):
        self.fw.barrier(("pool",))
        return self.DMA(out, in_, reads, writes, lane="pl", eng="pool")

    def act(self, out, in_, func, bias=None, scale=1.0, reads=(), writes=()):
        if bias is None:
            return self.A(lambda e: e.activation(out=out, in_=in_, func=func, scale=scale), reads, writes)
        return self.A(lambda e: e.activation(out=out, in_=in_, func=func, bias=bias, scale=scale), reads, writes)

    def tt(self, out, a, b, op, reads=(), writes=()):
        return self.V(lambda e: e.tensor_tensor(out=out, in0=a, in1=b, op=op), reads, writes)

    def ts(self, out, a, s1, s2, op0, op1=None, reads=(), writes=()):
        if op1 is None:
            return self.V(lambda e: e.tensor_scalar(out=out, in0=a, scalar1=s1, scalar2=None, op0=op0), reads, writes)
        return self.V(lambda e: e.tensor_scalar(out=out, in0=a, scalar1=s1, scalar2=s2, op0=op0, op1=op1), reads, writes)

    def stt(self, out, in0, scalar, in1, op0, op1, reads=(), writes=()):
        return self.V(lambda e: e.scalar_tensor_tensor(out=out, in0=in0, scalar=scalar, in1=in1, op0=op0, op1=op1),
                      reads, writes)

    def cp(self, out, in_, reads=(), writes=()):
        return self.V(lambda e: e.tensor_copy(out=out, in_=in_), reads, writes)

    def memset(self, ap, val, writes=()):
        return self.V(lambda e: e.memset(ap, val), (), writes)

    def ps(self):
        i = self.psi % 6
        self.psi += 1
        return self.psb[i], ("ps", i)

    def tmp(self):
        i = self.tmpi % 4
        self.tmpi += 1
        return self.tmps[i], ("tmp", i)

    def wload(self, W, k0, nk, c0, ncol):
        s = self.wi % 2
        self.wi += 1
        src = W[k0 * 128:(k0 + nk) * 128, c0:c0 + ncol].rearrange("(k p) n -> p k n", p=128)
        dst = self.wsl[s][:, 0:nk, 0:ncol]
        self.fw.op("pool", lambda e: e.dma_start(out=dst, in_=src), (), [("w", s)], lane="w%d" % s)
        return self.wsl[s], ("w", s)

    def xin_v(self, name, n):
        b, o = XO[name]
        flat = self.xin[b].rearrange("r c -> (r c)")
        return flat[o:o + n]

    def xout_v(self, name, n):
        b, o = XO[name]
        return self.xout[b].rearrange("(r q) c -> r (q c)", r=4)[:, o:o + n]

    def xput(self, name, dst, src, reads):
        key = ("xin", XO.get(name, XO.get({"A0": "Af", "A1": "Ab", "S0": "Sf", "S1": "Sb"}.get(name, name)))[0], name,
               len(self.xkeys))
        self.xkeys.append(key)
        self.DMA(dst, src, reads=reads, writes=[key], lane="xw")

    def build(self):
        nc = self.nc
        di = self.dram_in
        NL_ = self.nl
        self.xp_in = di("xT_p", (D, 512))
        self.xs_in = di("xT_s", (D, 1024))
        self.ck_in = di("ck", (L, 512, 256))
        self.cv_in = di("cv", (L, 512, 256))
        self.s0f_in = di("s0f", (L, 4, 128, 128))
        self.s0b_in = di("s0b", (L, 4, 128, 128))
        self.cvec_in = di("cvec", (128, 16, 2))
        self.prm_in = di("prm", (128, NP_))
        self.qkg_in = di("qkg", (L, 128, 256))
        self.ident_in = di("ident", (128, 128))
        self.hmask_in = di("hmask", (128, 2, 128))
        self.abias_in = di("abias", (128, 10, 128))
        self.rope_in = di("rope", (128, 8, 2, 64))
        self.smask_in = di("smask", (128, 1024))
        self.W = {}
        for n, shp in (("ada_w", (NL_, D, 6 * D)), ("w_in", (NL_, D, INC)), ("w_attn_out", (NL_, 1024, D)),
                       ("w_conf_out", (NL_, 512, D)), ("w_sc_out", (NL_, 512, D)), ("w_hg_out", (NL_, 512, D)),
                       ("w_gate", (NL_, D, 4 * D)), ("w_o", (NL_, D, D)), ("ffn_w1", (NL_, D, DFF)),
                       ("ffn_w3", (NL_, D, DFF)), ("ffn_w2", (NL_, DFF, D))):
            self.W[n] = di(n, shp)
        do = self.dram_out
        self.yp_out = do("yT_p", (D, 512))
        self.ys_out = do("yT_s", (D, 1024))
        self.nk_out = do("nk", (2, L, 256, 256))
        self.nv_out = do("nv", (2, L, 256, 256))
        self.nsf_out = do("nsf", (2, L, 4, 128, 128))
        self.nsb_out = do("nsb", (2, L, 4, 128, 128))
        self.xin = {b: nc.dram_tensor("xch_in" + b, [128, XR[b]], F32).ap() for b in XR}
        self.xout = {b: nc.dram_tensor("xch_out" + b, [512, XR[b]], F32).ap() for b in XR}

        with ExitStack() as st:
            sb = lambda n, s, d: st.enter_context(nc.sbuf_tensor(n, s, d))
            self.xT = sb("xT", [128, 16, 1024], F32)
            self.mg = sb("mg", [128, 16, 1024], BF16)
            self.hT = sb("hT", [128, 16, 512], BF16)
            self.wsl = [sb("wsl%d" % i, [128, 16, 256], BF16) for i in range(2)]
            self.rstd = sb("rstd", [128, 1024], F32)
            self.prm = sb("prm_s", [128, NP_], F32)
            self.modT = sb("modT", [128, L, 96, 2], F32)
            self.gsh = sb("gsh", [128, 2, 16], F32)
            self.identF = sb("identF", [128, 128], F32)
            self.identB = sb("identB", [128, 128], BF16)
            self.onesB = sb("onesB", [128, 128], BF16)
            self.onesF = sb("onesF", [128, 128], F32)
            self.epsT = sb("epsT", [128, 1], F32)
            self.lbv = sb("lbv", [128, 2, L, 4], F32)
            self.oml = sb("oml", [128, 2, L, 4], F32)
            self.csb = sb("csb", [128, 16, 2], BF16)
            self.tmps = [sb("tmp%d" % i, [128, 512], F32) for i in range(4)]
            self.hmask = sb("hmask_s", [128, 2, 128], F32)
            self.smask = sb("smask_s", [128, 1024], F32)
            self.arena = sb("arena", [128, ARENA], BF16)
            self.psb = [st.enter_context(nc.psum_tensor("ps%d" % i, [128, 512], F32)) for i in range(6)]
            self.psT = st.enter_context(nc.psum_tensor("psT", [128, 1024], BF16))
            self.psX = st.enter_context(nc.psum_tensor("psX", [128, 512], F32))
            self.program()
            self.fw.barrier(("sp",))
            self.fw.emit(nc, st)
        return nc

    def carve(self, off, n, dtype=BF16):
        if dtype == BF16:
            assert off + n <= ARENA, (off, n)
            return self.arena[:, off:off + n], off + n
        off += off % 2
        assert off + 2 * n <= ARENA, (off, n)
        return self.arena[:, off:off + 2 * n].bitcast(F32), off + 2 * n

    def program(self):
        fw = self.fw
        self.DMA(self.prm[:], self.prm_in, writes=["prm"])
        self.DMA(self.identF[:], self.ident_in, writes=["identF"])
        self.DMA(self.hmask[:], self.hmask_in, writes=["hmask"])
        self.DMA(self.smask[:], self.smask_in, writes=["smask"])
        self.cp(self.identB[:], self.identF[:], ["identF"], ["identB"])
        self.memset(self.onesB[:], 1.0, ["onesB"])
        self.memset(self.onesF[:], 1.0, ["onesF"])
        self.memset(self.epsT[:], EPS, ["epsT"])
        self.prologue()
        import os
        if "P" in os.environ.get("KPASS", "PS"):
            for l in range(self.nl):
                self.layer_pass(l, "P")
            self.DMA(self.yp_out.rearrange("(k p) t -> p k t", p=128), self.xT[:, :, 0:512], reads=["x"], writes=["yp"],
                     lane="out")
            fw.barrier()
        if self.do_sample:
            for l in range(self.nl):
                self.layer_pass(l, "S")
            self.DMA(self.ys_out.rearrange("(k p) t -> p k t", p=128), self.xT[:, :, 0:1024], reads=["x"],
                     writes=["ys"], lane="out")

    def prologue(self):
        prm = self.prm
        t, tk = self.tmp()
        cview = t[:, 0:32].rearrange("p (k t) -> p k t", t=2)
        self.DMA(cview, self.cvec_in, writes=[tk])
        self.act(self.csb[:], cview, AF.Silu, reads=[tk], writes=["csb"])
        lbr = prm[:, PRM["hglb"]:PRM["hglb"] + 32].rearrange("p (d l h) -> p d l h", d=2, l=L)
        e_, ek = self.tmp()
        ev = e_[:, 0:32].rearrange("p (d l h) -> p d l h", d=2, l=L)
        self.act(ev, lbr, AF.Exp, reads=["prm"], writes=[ek])
        s_, sk = self.tmp()
        sv = s_[:, 0:8].rearrange("p (d h) -> p d h", d=2)
        self.tt(sv, ev[:, :, 0, :], ev[:, :, 1, :], ALU.add, [ek], [sk])
        self.tt(sv, sv, ev[:, :, 2, :], ALU.add, [ek, sk], [sk])
        self.tt(sv, sv, ev[:, :, 3, :], ALU.add, [ek, sk], [sk])
        self.V(lambda e: e.reciprocal(out=sv, in_=sv), [sk], [sk])
        for l in range(L):
            self.tt(ev[:, :, l, :], ev[:, :, l, :], sv, ALU.mult, [ek, sk], [ek])
        self.memset(self.lbv[:, :, 0, :], 0.0, ["lbv"])
        for l in range(1, L):
            self.tt(self.lbv[:, :, l, :], self.lbv[:, :, l - 1, :], ev[:, :, l, :], ALU.add, [ek, "lbv"], ["lbv"])
        self.ts(self.lbv[:], self.lbv[:], 0.0, None, ALU.max, reads=["lbv"], writes=["lbv"])
        self.ts(self.oml[:], self.lbv[:], -1.0, 1.0, ALU.mult, ALU.add, reads=["lbv"], writes=["oml"])
        for l in range(self.nl):
            for cb in range(48):
                w, wk = self.wload(self.W["ada_w"][l], 0, 16, cb * 256, 256)
                for m in range(2):
                    ch = cb * 2 + m
                    pt, pk = self.ps()
                    mms = [(pt[:, 0:2], w[:, kc, m * 128:(m + 1) * 128], self.csb[:, kc, :], kc == 0, kc == 15)
                           for kc in range(16)]
                    self.PE(mms, [wk, "csb"], [pk])
                    b = prm[:, PRM["adab"] + l * 96 + ch:PRM["adab"] + l * 96 + ch + 1]
                    self.ts(self.modT[:, l, ch, :], pt[:, 0:2], b, None, ALU.add, reads=[pk, "prm"], writes=["modT"])
        for part in (1, 4):
            v = self.modT[:, 0:self.nl, part * 16:(part + 1) * 16, :]
            self.ts(v, v, 1.0, None, ALU.add, reads=["modT"], writes=["modT"])

    def set_norm(self, l, which, kind):
        prm = self.prm
        nm = "n1g" if which == 1 else "n2g"
        g = prm[:, PRM[nm] + l * 16:PRM[nm] + l * 16 + 16]
        base = 0 if which == 1 else 48
        shift = self.modT[:, l, base:base + 16, kind]
        scale = self.modT[:, l, base + 16:base + 32, kind]
        self.tt(self.gsh[:, 0, :], g, scale, ALU.mult, ["prm", "modT"], ["gsh"])
        self.cp(self.gsh[:, 1, :], shift, ["modT"], ["gsh"])

    def norm_stats(self, T):
        for tt_ in range(T // 512):
            sl = slice(tt_ * 512, (tt_ + 1) * 512)
            for kc in range(16):
                self.act(self.hT[:, kc, :], self.xT[:, kc, sl], AF.Square, reads=["x"], writes=[("h", kc)])
            mms = [(self.psX[:], self.onesB[:], self.hT[:, kc, :], kc == 0, kc == 15) for kc in range(16)]
            self.PE(mms, [("h", kc) for kc in range(16)] + ["onesB"], ["psX"])
            self.act(self.rstd[:, sl], self.psX[:], AF.Ln, bias=self.epsT[:, 0:1], scale=1.0 / D,
                     reads=["psX", "epsT"], writes=["rstd"])
            self.act(self.rstd[:, sl], self.rstd[:, sl], AF.Exp, scale=-0.5, reads=["rstd"], writes=["rstd"])

    def make_h(self, tt_, buf=None):
        sl = slice(tt_ * 512, (tt_ + 1) * 512)
        hb, kp = (self.hT, "h") if buf is None else (buf, "h2")
        for kc in range(16):
            t, tk = self.tmp()
            self.tt(t[:], self.xT[:, kc, sl], self.rstd[:, sl], ALU.mult, ["x", "rstd"], [tk])
            self.act(hb[:, kc, :], t[:], AF.Identity, bias=self.gsh[:, 1, kc:kc + 1], scale=self.gsh[:, 0, kc:kc + 1],
                     reads=[tk, "gsh"], writes=[(kp, kc)])
        return [(kp, kc) for kc in range(16)]

    def h_tiles(self, T, h2off):
        tiles = [(self.hT, self.make_h(0))]
        if T > 512:
            b, _ = self.carve(h2off, 16 * 512)
            b = b.rearrange("p (k t) -> p k t", k=16)
            tiles.append((b, self.make_h(1, b)))
        return tiles

    def proj_fm(self, W, c0, ncol, rhs_list, rkeys, consume):
        nk = len(rhs_list)
        nblk = (ncol + 255) // 256
        n = rhs_list[0].shape[1]
        for b in range(nblk):
            bc = min(256, ncol - b * 256)
            w, wk = self.wload(W, 0, nk, c0 + b * 256, bc)
            for m in range(bc // 128):
                pt, pk = self.ps()
                mms = [(pt[:, 0:n], w[:, k, m * 128:(m + 1) * 128], rhs_list[k], k == 0, k == nk - 1) for k in range(nk)]
                self.PE(mms, [wk] + list(rkeys), [pk])
                consume(b * 2 + m, pt, pk)

    def layer_pass(self, l, mode):
        import os
        fw = self.fw
        T = 512 if mode == "P" else 1024
        kind = 0 if mode == "P" else 1
        if l == 0:
            src = self.xp_in if mode == "P" else self.xs_in
            self.DMA(self.xT[:, :, 0:T], src.rearrange("(k p) t -> p k t", p=128), writes=["x"])
        self.set_norm(l, 1, kind)
        self.norm_stats(T)
        fw.barrier()
        mix = os.environ.get("KMIX", "BCADF")
        if mode == "S":
            self.pre_phase(l, T)
            fw.barrier()
        if "B" in mix:
            self.mixer_B(l, mode, T)
            fw.barrier()
        if "C" in mix:
            self.mixer_C(l, mode, T)
            fw.barrier()
        if "A" in mix:
            self.mixer_A(l, mode, T)
            fw.barrier()
        if "D" in mix:
            self.mixer_D(l, mode, T)
            fw.barrier()
        if "F" in mix:
            self.wo_ffn(l, mode, T)
            fw.barrier()

    def pre_phase(self, l, T):
        self.xkeys = []
        Win = self.W["w_in"][l]
        off = 0
        ropeT, off = self.carve(off, 8 * 128, F32)
        self.ropeT = ropeT.rearrange("p (b c f) -> p b c f", b=8, c=2)
        self.DMA(self.ropeT, self.rope_in, writes=["ropeT"])
        qkg, off = self.carve(off, 256, F32)
        self.DMA(qkg, self.qkg_in[l], writes=["qkg"])
        sq, off = self.carve(off, 256, F32)
        qn, off = self.carve(off, 256, F32)
        st2, off = self.carve(off, 8, F32)
        rt, off = self.carve(off, 512, F32)
        u16, off = self.carve(off, 16, F32)
        for tt_, tb, nk_, nv_, nu, nw, c16, wcol in ((0, 0, "kf", "vf", "uL", "wL", slice(0, 16), 0),
                                                     (1, 3, "kl", "vl", "uR", "wR", slice(496, 512), 15)):
            hk = self.make_h(tt_)
            blk = tt_ * 4 + tb
            w, wk = self.wload(Win, 0, 16, CK, 256)
            pt, pk = self.ps()
            self.PE([(pt[:, 0:256], self.hT[:, kc, tb * 128:(tb + 1) * 128], w[:, kc, 0:256], kc == 0, kc == 15)
                     for kc in range(16)], [wk] + hk, [pk])
            self.qk_norm(pt, pk, sq, st2, qn, qkg[:, 128:256])
            self.rope(qn, blk, rt)
            self.xput(nk_, self.xin_v(nk_, 32768).rearrange("(t c) -> t c", c=256), qn, ["qn"])
            w, wk = self.wload(Win, 0, 16, CV, 256)
            pt, pk = self.ps()
            self.PE([(pt[:, 0:256], self.hT[:, kc, tb * 128:(tb + 1) * 128], w[:, kc, 0:256], kc == 0, kc == 15)
                     for kc in range(16)], [wk] + hk, [pk])
            self.cp(qn, pt[:, 0:256], [pk], ["qn"])
            self.xput(nv_, self.xin_v(nv_, 32768).rearrange("(t c) -> t c", c=256), qn, ["qn"])
            hr = [self.hT[:, kc, c16] for kc in range(16)]
            for ch in range(4):
                hold = {}

                def cg(m, pt, pk, hold=hold):
                    t, tk = self.tmp()
                    self.act(t[:, 0:16], pt[:, 0:16], AF.Sigmoid, reads=[pk], writes=[tk])
                    hold["g"] = (t, tk)

                self.proj_fm(Win, CCG + ch * 128, 128, hr, hk, cg)

                def ca(m, pt, pk, hold=hold, ch=ch, nu=nu):
                    t, tk = hold["g"]
                    self.tt(u16, pt[:, 0:16], t[:, 0:16], ALU.mult, [pk, tk], ["u16"])
                    dst = self.xin_v(nu, 8192).rearrange("(c p w) -> c p w", c=4, p=128)[ch]
                    self.xput(nu, dst, u16, ["u16"])

                self.proj_fm(Win, CCA + ch * 128, 128, hr, hk, ca)

                def cc(m, pt, pk, hold=hold):
                    t, tk = self.tmp()
                    self.cp(t[:, 0:16], pt[:, 0:16], [pk], [tk])
                    hold["c"] = (t, tk)

                self.proj_fm(Win, CSC + ch * 128, 128, hr, hk, cc)

                def chh(m, pt, pk, hold=hold, ch=ch, nw=nw, wcol=wcol):
                    t, tk = hold["c"]
                    self.tt(u16, pt[:, 0:16], t[:, 0:16], ALU.mult, [pk, tk], ["u16"])
                    dst = self.xin_v(nw, 512).rearrange("(c p o) -> c p o", c=4, o=1)[ch]
                    self.xput(nw, dst, u16[:, wcol:wcol + 1], ["u16"])

                self.proj_fm(Win, CSH + ch * 128, 128, hr, hk, chh)
        self.fw.barrier()
        self.mixer_D(l, "S", T, summary=True)
        self.fw.barrier()
        grp = [[0, 1, 2, 3], [4, 5, 6, 7]]
        ka = [k for k in self.xkeys if k[1] == "A"]
        kb = [k for k in self.xkeys if k[1] == "B"]
        self.fw.op("pool", lambda e: e.collective_compute("AllGather", ALU.bypass, replica_groups=grp,
                                                          ins=[self.xin["A"]], outs=[self.xout["A"]]),
                   reads=ka, writes=["xoutA"], lane="cc", linc=1)
        self.fw.op("pool", lambda e: e.collective_compute("AllGather", ALU.bypass, replica_groups=grp,
                                                          ins=[self.xin["B"]], outs=[self.xout["B"]]),
                   reads=kb + ["xoutA"], writes=["xout", "xoutA"], lane="cc", linc=1)

    def qk_norm(self, pt, pk, sq, st2, qn, gain):
        self.act(sq, pt[:, 0:256], AF.Square, reads=[pk], writes=["sq"])
        self.V(lambda e: e.reduce_sum(out=st2[:, 0:2], in_=sq.rearrange("p (h d) -> p h d", h=2), axis=AX.X),
               ["sq"], ["st2"])
        self.act(st2[:, 2:4], st2[:, 0:2], AF.Ln, bias=self.epsT[:, 0:1], scale=1.0 / 128,
                 reads=["st2", "epsT"], writes=["st2b"])
        self.act(st2[:, 4:6], st2[:, 2:4], AF.Exp, scale=-0.5, reads=["st2b"], writes=["st2c"])
        for hd in range(2):
            self.stt(qn[:, hd * 128:(hd + 1) * 128], pt[:, hd * 128:(hd + 1) * 128], st2[:, 4 + hd:5 + hd],
                     gain, ALU.mult, ALU.mult, reads=[pk, "st2c", "qkg"], writes=["qn"])

    def rope(self, qn, blk, rt):
        v = qn.rearrange("p (h a f r) -> p h a f r", h=2, a=2, f=2)
        a1 = v[:, :, :, 0, :]
        a2 = v[:, :, :, 1, :]
        cos = self.ropeT[:, blk, 0, :].rearrange("p (a r) -> p a r", a=2).unsqueeze(1).to_broadcast([128, 2, 2, 32])
        sin = self.ropeT[:, blk, 1, :].rearrange("p (a r) -> p a r", a=2).unsqueeze(1).to_broadcast([128, 2, 2, 32])
        t = [rt[:, i * 128:(i + 1) * 128].rearrange("p (h a r) -> p h a r", h=2, a=2) for i in range(4)]
        self.tt(t[0], a1, cos, ALU.mult, ["qn", "ropeT"], ["rt0"])
        self.tt(t[1], a2, sin, ALU.mult, ["qn", "ropeT"], ["rt1"])
        self.tt(t[2], a2, cos, ALU.mult, ["qn", "ropeT"], ["rt2"])
        self.tt(t[3], a1, sin, ALU.mult, ["qn", "ropeT"], ["rt3"])
        self.tt(a1, t[0], t[1], ALU.subtract, ["rt0", "rt1"], ["qn"])
        self.tt(a2, t[2], t[3], ALU.add, ["rt2", "rt3", "qn"], ["qn"])

    def branch_out(self, l, n, moT, nk, Wout, T, first, h2off):
        prm = self.prm
        tiles = self.h_tiles(T, h2off)
        for cb in range(8):
            wg, wgk = self.wload(self.W["w_gate"][l], 0, 16, n * D + cb * 256, 256)
            pgs = {}
            for m in range(2):
                for ti, (hb, hk) in enumerate(tiles):
                    pg, pgk = self.ps() if len(tiles) == 1 else (self.psb[m * 2 + ti], ("ps", m * 2 + ti))
                    self.PE([(pg[:], wg[:, kc, m * 128:(m + 1) * 128], hb[:, kc, :], kc == 0, kc == 15)
                             for kc in range(16)], [wgk] + hk, [pgk])
                    pgs[(m, ti)] = (pg, pgk)
            wo, wok = self.wload(Wout, 0, nk, cb * 256, 256)
            for m in range(2):
                ch = cb * 2 + m
                for ti in range(len(tiles)):
                    sl = slice(ti * 512, (ti + 1) * 512)
                    pg, pgk = pgs[(m, ti)]
                    py, pyk = self.ps() if len(tiles) == 1 else (self.psb[4 + ti], ("ps", 4 + ti))
                    self.PE([(py[:], wo[:, k, m * 128:(m + 1) * 128], moT[:, k, sl], k == 0, k == nk - 1)
                             for k in range(nk)], [wok, "mo"], [pyk])
                    t, tk = self.tmp()
                    bcol = PRM["bgate"] + l * 64 + n * 16 + ch
                    self.act(t[:], pg[:], AF.Sigmoid, bias=prm[:, bcol:bcol + 1], reads=[pgk, "prm"], writes=[tk])
                    if first:
                        self.tt(self.mg[:, ch, sl], t[:], py[:], ALU.mult, [tk, pyk], [("mg", ch)])
                    else:
                        self.tt(t[:], t[:], py[:], ALU.mult, [tk, pyk], [tk])
                        self.tt(self.mg[:, ch, sl], self.mg[:, ch, sl], t[:], ALU.add, [tk, ("mg", ch)], [("mg", ch)])

    def segs(self, mode):
        return [(0, 256), (256, 256)] if mode == "P" else [(0, 1024)]

    def mixer_B(self, l, mode, T):
        prm = self.prm
        segs = self.segs(mode)
        HAL = 15
        TP = T + 2 * HAL * len(segs)
        off = 0
        u, off = self.carve(off, 4 * TP)
        u = u.rearrange("p (c t) -> p c t", c=4)
        acc, off = self.carve(off, 4 * T, F32)
        acc = acc.rearrange("p (c t) -> p c t", c=4)
        yb, off = self.carve(off, 4 * T)
        yb = yb.rearrange("p (c t) -> p c t", c=4)
        self.memset(u, 0.0, ["u"])

        def upos(t0):
            for si, (s0, ln) in enumerate(segs):
                if s0 <= t0 < s0 + ln:
                    return si * (ln + 2 * HAL) + HAL + (t0 - s0)

        Win = self.W["w_in"][l]
        for tt_ in range(T // 512):
            hk = self.make_h(tt_)
            hr = [self.hT[:, kc, :] for kc in range(16)]
            for ch in range(4):
                hold = {}

                def cons_g(m, pt, pk, hold=hold):
                    t, tk = self.tmp()
                    self.act(t[:], pt[:], AF.Sigmoid, reads=[pk], writes=[tk])
                    hold["g"] = (t, tk)

                self.proj_fm(Win, CCG + ch * 128, 128, hr, hk, cons_g)

                def cons_a(m, pt, pk, ch=ch, tt_=tt_, hold=hold):
                    t, tk = hold["g"]
                    for (s0, ln) in segs:
                        a = max(s0, tt_ * 512)
                        b = min(s0 + ln, (tt_ + 1) * 512)
                        if a >= b:
                            continue
                        p0 = upos(a)
                        self.tt(u[:, ch, p0:p0 + (b - a)], pt[:, a - tt_ * 512:b - tt_ * 512],
                                t[:, a - tt_ * 512:b - tt_ * 512], ALU.mult, [pk, tk], ["u"])

                self.proj_fm(Win, CCA + ch * 128, 128, hr, hk, cons_a)
        if mode == "S":
            cand, off = self.carve(off, 4 * 4 * 16, F32)
            cand = cand.rearrange("p (c r w) -> p c r w", c=4, r=4)
            for side, nm, selb, lo, pos in ((0, "uR", 0, 1, 0), (1, "uL", 4, 0, HAL + T)):
                src = self.xout_v(nm, 8192).rearrange("r (c p w) -> c p r w", c=4, p=128)
                for ch in range(4):
                    self.DMA(cand[:, ch, :, :], src[ch], reads=["xout"], writes=["cand"])
                for ch in range(4):
                    dst = u[:, ch, pos:pos + HAL]
                    for r in range(4):
                        sc_ = prm[:, PRM["sel"] + selb + r:PRM["sel"] + selb + r + 1]
                        if r == 0:
                            self.ts(dst, cand[:, ch, r, lo:lo + HAL], sc_, None, ALU.mult, reads=["cand", "prm", "u"],
                                    writes=["u"])
                        else:
                            self.stt(dst, cand[:, ch, r, lo:lo + HAL], sc_, dst, ALU.mult, ALU.add,
                                     reads=["cand", "prm", "u"], writes=["u"])
        for ch in range(4):
            wb = PRM["cfw"] + (l * 4 + ch) * 31
            bb = PRM["cfb"] + l * 4 + ch
            for (s0, ln) in segs:
                p0 = upos(s0) - HAL
                o = acc[:, ch, s0:s0 + ln]
                self.ts(o, u[:, ch, p0:p0 + ln], prm[:, wb:wb + 1], prm[:, bb:bb + 1], ALU.mult, ALU.add,
                        reads=["u", "prm"], writes=[("acc", ch)])
                for k in range(1, 31):
                    self.stt(o, u[:, ch, p0 + k:p0 + k + ln], prm[:, wb + k:wb + k + 1], o, ALU.mult, ALU.add,
                             reads=["u", "prm", ("acc", ch)], writes=[("acc", ch)])
        for tt_ in range(T // 512):
            sl = slice(tt_ * 512, (tt_ + 1) * 512)
            pm, pmk = self.ps()
            self.PE([(pm[:], self.onesF[:], acc[:, ch, sl], ch == 0, ch == 3) for ch in range(4)],
                    [("acc", ch) for ch in range(4)] + ["onesF"], [pmk])
            mean, mk = self.tmp()
            self.ts(mean[:], pm[:], 1.0 / 512, None, ALU.mult, reads=[pmk], writes=[mk])
            for ch in range(4):
                self.tt(acc[:, ch, sl], acc[:, ch, sl], mean[:], ALU.subtract, [("acc", ch), mk], [("acc", ch)])
            pv, pvk = self.ps()
            for ch in range(4):
                t, tk = self.tmp()
                self.act(t[:], acc[:, ch, sl], AF.Square, reads=[("acc", ch)], writes=[tk])
                self.PE([(pv[:], self.onesF[:], t[:], ch == 0, ch == 3)], [tk, "onesF"] + ([pvk] if ch else []), [pvk])
            rs, rk = self.tmp()
            self.act(rs[:], pv[:], AF.Ln, bias=self.epsT[:, 0:1], scale=1.0 / 512, reads=[pvk, "epsT"], writes=[rk])
            self.act(rs[:], rs[:], AF.Exp, scale=-0.5, reads=[rk], writes=[rk])
            for ch in range(4):
                self.tt(acc[:, ch, sl], acc[:, ch, sl], rs[:], ALU.mult, [("acc", ch), rk], [("acc", ch)])
                gcol = PRM["cflg"] + l * 4 + ch
                bcol = PRM["cflb"] + l * 4 + ch
                self.act(yb[:, ch, sl], acc[:, ch, sl], AF.Silu, bias=prm[:, bcol:bcol + 1],
                         scale=prm[:, gcol:gcol + 1], reads=[("acc", ch), "prm"], writes=["mo"])
        self.fw.barrier()
        self.branch_out(l, 1, yb, 4, self.W["w_conf_out"][l], T, first=True, h2off=off)

    def mixer_C(self, l, mode, T):
        prm = self.prm
        segs = self.segs(mode)
        TP = T + 2 * len(segs)
        off = 0
        w_, off = self.carve(off, 4 * TP)
        w_ = w_.rearrange("p (c t) -> p c t", c=4)
        sbb, off = self.carve(off, 4 * T)
        sbb = sbb.rearrange("p (c t) -> p c t", c=4)
        yc, off = self.carve(off, 4 * T)
        yc = yc.rearrange("p (c t) -> p c t", c=4)
        acc, off = self.carve(off, T, F32)
        self.memset(w_, 0.0, ["u"])

        def upos(t0):
            for si, (s0, ln) in enumerate(segs):
                if s0 <= t0 < s0 + ln:
                    return si * (ln + 2) + 1 + (t0 - s0)

        Win = self.W["w_in"][l]
        for tt_ in range(T // 512):
            sl = slice(tt_ * 512, (tt_ + 1) * 512)
            hk = self.make_h(tt_)
            hr = [self.hT[:, kc, :] for kc in range(16)]
            for ch in range(4):
                self.proj_fm(Win, CSB + ch * 128, 128, hr, hk,
                             lambda m, pt, pk, ch=ch, sl=sl: self.cp(sbb[:, ch, sl], pt[:], [pk], ["sbb"]))
                hold = {}

                def cons_c(m, pt, pk, hold=hold):
                    t, tk = self.tmp()
                    self.cp(t[:], pt[:], [pk], [tk])
                    hold["c"] = (t, tk)

                self.proj_fm(Win, CSC + ch * 128, 128, hr, hk, cons_c)

                def cons_h(m, pt, pk, ch=ch, tt_=tt_, hold=hold):
                    t, tk = hold["c"]
                    for (s0, ln) in segs:
                        a = max(s0, tt_ * 512)
                        b = min(s0 + ln, (tt_ + 1) * 512)
                        if a >= b:
                            continue
                        p0 = upos(a)
                        self.tt(w_[:, ch, p0:p0 + (b - a)], pt[:, a - tt_ * 512:b - tt_ * 512],
                                t[:, a - tt_ * 512:b - tt_ * 512], ALU.mult, [pk, tk], ["u"])

                self.proj_fm(Win, CSH + ch * 128, 128, hr, hk, cons_h)
        if mode == "S":
            cand, off = self.carve(off, 16, F32)
            cand = cand.rearrange("p (c r w) -> p c r w", c=4, r=4)
            for side, nm, selb, pos in ((0, "wR", 0, 0), (1, "wL", 4, 1 + T)):
                src = self.xout_v(nm, 512).rearrange("r (c p w) -> c p r w", c=4, p=128)
                for ch in range(4):
                    self.DMA(cand[:, ch, :, :], src[ch], reads=["xout"], writes=["cand"], slow=True)
                for ch in range(4):
                    dst = w_[:, ch, pos:pos + 1]
                    for r in range(4):
                        sc_ = prm[:, PRM["sel"] + selb + r:PRM["sel"] + selb + r + 1]
                        if r == 0:
                            self.ts(dst, cand[:, ch, r, :], sc_, None, ALU.mult, reads=["cand", "prm", "u"], writes=["u"])
                        else:
                            self.stt(dst, cand[:, ch, r, :], sc_, dst, ALU.mult, ALU.add, reads=["cand", "prm", "u"],
                                     writes=["u"])
        for ch in range(4):
            wb = PRM["scw"] + (l * 4 + ch) * 3
            for (s0, ln) in segs:
                p0 = upos(s0) - 1
                o = acc[:, s0:s0 + ln]
                self.ts(o, w_[:, ch, p0:p0 + ln], prm[:, wb:wb + 1], None, ALU.mult, reads=["u", "prm"], writes=["acc"])
                for k in (1, 2):
                    self.stt(o, w_[:, ch, p0 + k:p0 + k + ln], prm[:, wb + k:wb + k + 1], o, ALU.mult, ALU.add,
                             reads=["u", "prm", "acc"], writes=["acc"])
            self.tt(yc[:, ch, 0:T], acc[:, 0:T], sbb[:, ch, 0:T], ALU.mult, ["acc", "sbb"], ["mo"])
        self.fw.barrier()
        self.branch_out(l, 2, yc, 4, self.W["w_sc_out"][l], T, first=False, h2off=off)

    def mixer_A(self, l, mode, T):
        prm = self.prm
        segs = self.segs(mode)
        NB = T // 128
        off = 0
        qT, off = self.carve(off, 8 * T)
        qT = qT.rearrange("p (h t) -> p h t", h=8)
        kT, off = self.carve(off, 2 * T)
        kT = kT.rearrange("p (h t) -> p h t", h=2)
        vtm, off = self.carve(off, NB * 256)
        vtm = vtm.rearrange("p (b c) -> p b c", c=256)
        base = off
        qkg, off = self.carve(off, 256, F32)
        sq, off = self.carve(off, 256, F32)
        qn, off = self.carve(off, 256, F32)
        qnb, off = self.carve(off, 256)
        st2, off = self.carve(off, 8, F32)
        if mode == "S":
            ropeT, off = self.carve(off, 8 * 128, F32)
            self.ropeT = ropeT.rearrange("p (b c f) -> p b c f", b=8, c=2)
            self.DMA(self.ropeT, self.rope_in, writes=["ropeT"])
            rt, off = self.carve(off, 512, F32)
        self.DMA(qkg, self.qkg_in[l], writes=["qkg"])
        Win = self.W["w_in"][l]
        for tt_ in range(T // 512):
            hk = self.make_h(tt_)
            for wb in range(6):
                w, wk = self.wload(Win, 0, 16, wb * 256, 256)
                for tb in range(4):
                    blk = tt_ * 4 + tb
                    tsl = slice(blk * 128, (blk + 1) * 128)
                    pt, pk = self.ps()
                    self.PE([(pt[:, 0:256], self.hT[:, kc, tb * 128:(tb + 1) * 128], w[:, kc, 0:256], kc == 0, kc == 15)
                             for kc in range(16)], [wk] + hk, [pk])
                    if wb == 5:
                        self.cp(vtm[:, blk, :], pt[:, 0:256], [pk], [("v", blk)])
                        if mode == "P":
                            self.cp(qn, pt[:, 0:256], [pk], ["qn"])
                            seq, r0 = blk // 2, (blk % 2) * 128
                            self.DMA(self.nv_out[seq, l, r0:r0 + 128, :], qn, reads=["qn"], writes=["nvo"], lane="out")
                        continue
                    self.qk_norm(pt, pk, sq, st2, qn, qkg[:, 0:128] if wb < 4 else qkg[:, 128:256])
                    if mode == "S":
                        self.rope(qn, blk, rt)
                    if wb == 4 and mode == "P":
                        seq, r0 = blk // 2, (blk % 2) * 128
                        self.DMA(self.nk_out[seq, l, r0:r0 + 128, :], qn, reads=["qn"], writes=["nko"], lane="out")
                    self.cp(qnb, qn, ["qn"], ["qnb"])
                    for hd in range(2):
                        self.TR(self.psT[:, hd * 128:(hd + 1) * 128], qnb[:, hd * 128:(hd + 1) * 128], self.identB[:],
                                ["qnb", "identB"], ["psT"])
                    if wb < 4:
                        dst = qT[:, wb * 2:wb * 2 + 2, tsl]
                        key = ("qT", wb // 2, blk)
                    else:
                        dst = kT[:, :, tsl]
                        key = ("kT", blk)
                    self.cp(dst, self.psT[:, 0:256].rearrange("p (h t) -> p h t", h=2), ["psT"], [key])
        self.fw.barrier()
        off = base
        pT = []
        for i in range(2):
            a, off = self.carve(off, 512)
            pT.append(a)
        sinkx, off = self.carve(off, 1024, F32)
        den, off = self.carve(off, 512, F32)
        s8, off = self.carve(off, 8, F32)
        sx = prm[:, PRM["sink"] + l * 8:PRM["sink"] + l * 8 + 8]
        self.act(s8, sx, AF.Exp, reads=["prm"], writes=["s8"])
        for h in range(8):
            self.cp(sinkx[:, h * 128:(h + 1) * 128], s8[:, h:h + 1].to_broadcast([128, 128]), ["s8"], ["sinkx"])
        if mode == "S":
            ctmp, off = self.carve(off, 4 * 256)
            ctmp = ctmp.rearrange("p (r c) -> p r c", r=4)
            ctxK, off = self.carve(off, 2 * 512)
            ctxK = ctxK.rearrange("p (h s) -> p h s", h=2)
            ctxV, off = self.carve(off, 4 * 256)
            ctxV = ctxV.rearrange("p (b c) -> p b c", b=4)
            candK, off = self.carve(off, 4 * 2 * 128)
            candK = candK.rearrange("p (r h s) -> p r h s", r=4, h=2)
            candV, off = self.carve(off, 4 * 256)
            candV = candV.rearrange("p (r c) -> p r c", r=4)
            biasB, off = self.carve(off, 10 * 128)
            biasB = biasB.rearrange("p (i t) -> p i t", i=10)
            self.DMAc(biasB, self.abias_in, writes=["biasB"])
            self.DMAc(ctxV, self.cv_in[l].rearrange("(b p) c -> p b c", p=128), writes=["ctx"])
            self.DMAc(ctmp, self.ck_in[l].rearrange("(b p) c -> p b c", p=128), writes=["ctmp"])
            for b in range(4):
                for h in range(2):
                    self.TR(self.psT[:, 0:128], ctmp[:, b, h * 128:(h + 1) * 128], self.identB[:], ["ctmp", "identB"], ["psT"])
                    self.cp(ctxK[:, h, b * 128:(b + 1) * 128], self.psT[:, 0:128], ["psT"], ["ctx"])

            def load_cands(nk_, nv_):
                self.DMAc(candV, self.xout_v(nv_, 32768).rearrange("r (t c) -> t r c", c=256), reads=["xout"],
                          writes=["cand"])
                self.DMAc(ctmp, self.xout_v(nk_, 32768).rearrange("r (t c) -> t r c", c=256), reads=["xout"],
                          writes=["ctmp"])
                for r in range(4):
                    for h in range(2):
                        self.TR(self.psT[:, 0:128], ctmp[:, r, h * 128:(h + 1) * 128], self.identB[:], ["ctmp", "identB"],
                                ["psT"])
                        self.cp(candK[:, r, h, :], self.psT[:, 0:128], ["psT"], ["cand"])
        pi = 0
        for (s0, ln) in segs:
            b0, nb = s0 // 128, ln // 128
            for qb in range(b0, b0 + nb):
                qsl = slice(qb * 128, (qb + 1) * 128)
                if mode == "S" and qb == 0:
                    load_cands("kl", "vl")
                if mode == "S" and qb == NB - 1:
                    load_cands("kf", "vf")
                for kvh in range(2):
                    hs = slice(kvh * 128, (kvh + 1) * 128)

                    def loc(kb, bias):
                        return (kT[:, kvh, kb * 128:(kb + 1) * 128], vtm[:, kb, hs], [("kT", kb), ("v", kb)], bias)

                    if mode == "P":
                        kbs = [loc(kb, None) for kb in range(b0, b0 + nb)]
                    else:
                        kbs = []
                        if qb == 0:
                            kbs += [(candK[:, r, kvh, :], candV[:, r, hs], ["cand"], 2 + r) for r in range(4)]
                        else:
                            kbs.append(loc(qb - 1, 0))
                        kbs.append(loc(qb, None))
                        if qb == NB - 1:
                            kbs += [(candK[:, r, kvh, :], candV[:, r, hs], ["cand"], 6 + r) for r in range(4)]
                        else:
                            kbs.append(loc(qb + 1, 1))
                        kbs += [(ctxK[:, kvh, cb * 128:(cb + 1) * 128], ctxV[:, cb, hs], ["ctx"], None) for cb in range(4)]
                    qrhs = qT[:, kvh * 4:(kvh + 1) * 4, qsl]
                    ppv, ppvk = self.psb[0], ("ps", 0)
                    pden, pdenk = self.psb[1], ("ps", 1)
                    nkb = len(kbs)
                    for i, (kap, vap, keys, bias) in enumerate(kbs):
                        psc, psck = self.psb[2 + pi % 4], ("ps", 2 + pi % 4)
                        mms = [(psc[:], kap, qrhs, True, bias is None)]
                        rk = list(keys) + [("qT", kvh, qb)]
                        if bias is not None:
                            mms += [(psc[:, g * 128:(g + 1) * 128], self.identB[:], biasB[:, bias, :], False, g == 3)
                                    for g in range(4)]
                            rk += ["identB", "biasB"]
                        self.PE(mms, rk, [psck])
                        p = pT[pi % 2]
                        pkk = ("pT", pi % 2)
                        pi += 1
                        self.act(p, psc[:], AF.Exp, scale=SCALE, reads=[psck], writes=[pkk])
                        self.PE([(ppv[:], vap, p, i == 0, i == nkb - 1)], list(keys) + [pkk] + ([ppvk] if i else []), [ppvk])
                        self.PE([(pden[:], self.onesB[:], p, i == 0, i == nkb - 1)], ["onesB", pkk] + ([pdenk] if i else []),
                                [pdenk])
                    self.tt(den, pden[:], sinkx[:, kvh * 512:(kvh + 1) * 512], ALU.add, [pdenk, "sinkx"], ["den"])
                    self.V(lambda e: e.reciprocal(out=den, in_=den), ["den"], ["den"])
                    self.tt(qrhs, ppv[:].rearrange("p (h t) -> p h t", h=4), den.rearrange("p (h t) -> p h t", h=4),
                            ALU.mult, [ppvk, "den"], [("qT", kvh, qb), "mo"])
        self.fw.barrier()
        self.branch_out(l, 0, qT, 8, self.W["w_attn_out"][l], T, first=False, h2off=8 * T)

    def mixer_D(self, l, mode, T, summary=False):
        prm = self.prm
        segs = self.segs(mode)
        NB = T // 128
        NCH = T // 32
        off = 0
        if not summary:
            od, off = self.carve(off, 4 * T)
            od = od.rearrange("p (h t) -> p h t", h=4)
        base = off
        Win = self.W["w_in"][l]
        for hh in range(4):
            off = base
            vt, off = self.carve(off, NB * 128)
            vt = vt.rearrange("p (b c) -> p b c", c=128)
            lf = []
            for d in range(2):
                a, off = self.carve(off, T, F32)
                lf.append(a)
            Bc, off = self.carve(off, T, F32)
            Ex, off = self.carve(off, T, F32)
            kk, off = self.carve(off, T)
            Kt, off = self.carve(off, T)
            Ktm, off = self.carve(off, 4 * 128)
            Ktm = Ktm.rearrange("p (c d) -> p c d", c=4)
            Sf32, off = self.carve(off, 128, F32)
            Sbf = []
            for i in range(2):
                a, off = self.carve(off, 128)
                Sbf.append(a)
            Ach, off = self.carve(off, NCH, F32)
            At, off = self.carve(off, 2, F32)
            if not summary:
                qf, off = self.carve(off, T)
                gT, off = self.carve(off, T)
                o32, off = self.carve(off, T, F32)
                Qi, off = self.carve(off, T)
                Qc, off = self.carve(off, T)
                Kc, off = self.carve(off, T)
                attS, off = self.carve(off, 128)
                Bmid, off = self.carve(off, NCH, F32)
                if mode == "S":
                    Sc, off = self.carve(off, 4 * 128, F32)
                    Sc = Sc.rearrange("p (r c) -> p r c", r=4)
                    Ac, off = self.carve(off, 4, F32)
                    coef, off = self.carve(off, 2, F32)
                    tS, off = self.carve(off, 128, F32)
            for tt_ in range(T // 512):
                sl = slice(tt_ * 512, (tt_ + 1) * 512)
                hk = self.make_h(tt_)
                hr = [self.hT[:, kc, :] for kc in range(16)]
                if not summary:
                    self.proj_fm(Win, CHQ + hh * 128, 128, hr, hk,
                                 lambda m, pt, pk, sl=sl: self.act(qf[:, sl], pt[:], AF.Silu, reads=[pk], writes=["qf0"]))
                    self.proj_fm(Win, CHG + hh * 128, 128, hr, hk,
                                 lambda m, pt, pk, sl=sl: self.act(gT[:, sl], pt[:], AF.Silu, reads=[pk], writes=["gT"]))
                for d, cbase in ((0, CHF), (1, CHB)):
                    def cons_f(m, pt, pk, d=d, sl=sl):
                        t, tk = self.tmp()
                        self.act(t[:], pt[:], AF.Sigmoid, reads=[pk], writes=[tk])
                        self.ts(t[:], t[:], self.oml[:, d, l, hh:hh + 1], self.lbv[:, d, l, hh:hh + 1], ALU.mult, ALU.add,
                                reads=[tk, "oml", "lbv"], writes=[tk])
                        self.act(lf[d][:, sl], t[:], AF.Ln, reads=[tk], writes=[("lf", d)])
                    self.proj_fm(Win, cbase + hh * 128, 128, hr, hk, cons_f)
                w, wk = self.wload(Win, 0, 16, CHI + hh * 128, 128)
                for tb in range(4):
                    blk = tt_ * 4 + tb
                    pt, pk = self.ps()
                    self.PE([(pt[:, 0:128], self.hT[:, kc, tb * 128:(tb + 1) * 128], w[:, kc, 0:128], kc == 0, kc == 15)
                             for kc in range(16)], [wk] + hk, [pk])
                    self.cp(vt[:, blk, :], pt[:, 0:128], [pk], ["vt"])
            if not summary:
                self.ts(qf[:, 0:T], qf[:, 0:T], SCALE, None, ALU.mult, reads=["qf0"], writes=["qf"])
            for d in range(2):
                self.act(Ex[:, 0:T], lf[d][:, 0:T], AF.Exp, reads=[("lf", d)], writes=["Ex"])
                self.ts(kk[:, 0:T], Ex[:, 0:T], -1.0, 1.0, ALU.mult, ALU.add, reads=["Ex"], writes=["kk"])
                B3 = Bc.rearrange("p (c j) -> p c j", j=32)
                E3 = Ex.rearrange("p (c j) -> p c j", j=32)
                self.V(lambda e, d=d: e.tensor_tensor_scan(out=Bc[:, 0:T], data0=self.smask[:, 0:T], data1=lf[d][:, 0:T],
                                                          initial=0.0, op0=ALU.mult, op1=ALU.add),
                       [("lf", d), "smask"], ["Bc"])
                self.cp(Ach[:, 0:NCH], B3[:, 0:NCH, 31], ["Bc"], ["Btot"])
                if d == 1:
                    self.tt(B3[:, 0:NCH, :], Ach[:, 0:NCH].unsqueeze(2).to_broadcast([128, NCH, 32]), B3[:, 0:NCH, :],
                            ALU.subtract, ["Bc", "Btot"], ["Bc"])
                    self.tt(Bc[:, 0:T], Bc[:, 0:T], lf[d][:, 0:T], ALU.add, ["Bc", ("lf", d)], ["Bc"])
                if not summary:
                    self.cp(Bmid[:, 0:NCH], B3[:, 0:NCH, 15], ["Bc"], ["Bmid"])
                    self.act(Ex[:, 0:T], Bc[:, 0:T], AF.Exp, reads=["Bc", "kk"], writes=["Ex"])
                    self.tt(Qi[:, 0:T], qf[:, 0:T], Ex[:, 0:T], ALU.mult, ["qf", "Ex"], ["Qi"])
                self.tt(E3[:, 0:NCH, :], Ach[:, 0:NCH].unsqueeze(2).to_broadcast([128, NCH, 32]), B3[:, 0:NCH, :],
                        ALU.subtract, ["Bc", "Btot", "Ex", "Qi", "kk"], ["Ex"])
                self.act(Ex[:, 0:T], Ex[:, 0:T], AF.Exp, reads=["Ex"], writes=["Ex"])
                self.tt(Kt[:, 0:T], kk[:, 0:T], Ex[:, 0:T], ALU.mult, ["kk", "Ex"], ["Kt"])
                if not summary:
                    self.tt(E3[:, 0:NCH, :], B3[:, 0:NCH, :], Bmid[:, 0:NCH].unsqueeze(2).to_broadcast([128, NCH, 32]),
                            ALU.subtract, ["Bc", "Bmid", "Ex", "Kt"], ["Ex"])
                    self.ts(Ex[:, 0:T], Ex[:, 0:T], -40.0, 40.0, ALU.max, ALU.min, reads=["Ex"], writes=["Ex"])
                    self.act(Bc[:, 0:T], Ex[:, 0:T], AF.Exp, reads=["Ex", "Bmid"], writes=["Bc"])
                    self.tt(Qc[:, 0:T], qf[:, 0:T], Bc[:, 0:T], ALU.mult, ["qf", "Bc"], ["Qc"])
                    self.act(Bc[:, 0:T], Ex[:, 0:T], AF.Exp, scale=-1.0, reads=["Ex", "Qc"], writes=["Bc"])
                    self.tt(Kc[:, 0:T], kk[:, 0:T], Bc[:, 0:T], ALU.mult, ["kk", "Bc"], ["Kc"])
                if summary:
                    self.V(lambda e: e.reduce_sum(out=At[:, 0:1], in_=Ach[:, 0:NCH], axis=AX.X), ["Btot"], ["At"])
                    self.act(At[:, 0:1], At[:, 0:1], AF.Exp, reads=["At"], writes=["At"])
                    dst = self.xin_v("Af" if d == 0 else "Ab", 512).rearrange("(h p o) -> h p o", h=4, o=1)[hh]
                    self.xput("A%d" % d, dst, At[:, 0:1], ["At"])
                self.act(Ach[:, 0:NCH], Ach[:, 0:NCH], AF.Exp, reads=["Btot", "Ex", "At"], writes=["Ach"])
                for si, (s0, ln) in enumerate(segs):
                    nblk = ln // 128
                    blks = list(range(s0 // 128, s0 // 128 + nblk))
                    if d == 1:
                        blks = blks[::-1]
                    if mode == "P" or summary:
                        self.memset(Sf32, 0.0, ["S"])
                    else:
                        s0in = (self.s0f_in if d == 0 else self.s0b_in)[l, hh]
                        self.DMA(Sf32, s0in, writes=["S"])
                        self.DMA(Sc, self.xout_v("Sf" if d == 0 else "Sb", 65536).rearrange(
                            "r (h p c) -> h p r c", h=4, p=128)[hh], reads=["xout"], writes=["Sc"])
                        self.DMA(Ac.rearrange("p (r o) -> p r o", o=1), self.xout_v("Af" if d == 0 else "Ab", 512).rearrange(
                            "r (h p o) -> h p r o", h=4, o=1)[hh], reads=["xout"], writes=["Ac"], slow=True)
                        order = [0, 1, 2, 3] if d == 0 else [3, 2, 1, 0]
                        sb_ = PRM["sel"] + (8 if d == 0 else 12)
                        for r in order:
                            m_ = prm[:, sb_ + r:sb_ + r + 1]
                            om_ = prm[:, sb_ + 16 + r:sb_ + 16 + r + 1]
                            self.ts(coef[:, 0:1], Ac[:, r:r + 1], m_, om_, ALU.mult, ALU.add, reads=["Ac", "prm"],
                                    writes=["coef"])
                            self.ts(tS, Sc[:, r, :], m_, None, ALU.mult, reads=["Sc", "prm"], writes=["tS"])
                            self.stt(Sf32, Sf32, coef[:, 0:1], tS, ALU.mult, ALU.add, reads=["S", "coef", "tS"], writes=["S"])
                    self.cp(Sbf[0], Sf32, ["S"], [("Sb", 0)])
                    sidx = 0
                    for blk in blks:
                        bsl = slice(blk * 128, (blk + 1) * 128)
                        if not summary:
                            pa, pak = self.ps()
                            self.PE([(pa[:, 0:128], Kc[:, bsl], Qc[:, bsl], True, True)], ["Kc", "Qc"], [pak])
                            self.tt(attS, pa[:, 0:128], self.hmask[:, d, :], ALU.mult, [pak, "hmask"], ["attS"])
                        self.TR(self.psT[:, 512:640], Kt[:, bsl], self.identB[:], ["Kt", "identB"], ["psT2"])
                        for c in range(4):
                            self.ts(Ktm[:, c, :], self.psT[:, 512:640], prm[:, PRM["cm"] + c:PRM["cm"] + c + 1], None,
                                    ALU.mult, reads=["psT2", "prm"], writes=[("Ktm", c)])
                        if not summary:
                            po, pok = self.ps()
                            self.PE([(po[:, 0:128], vt[:, blk, :], attS, True, False)], ["vt", "attS"], [pok])
                        corder = [0, 1, 2, 3] if d == 0 else [3, 2, 1, 0]
                        for ci, c in enumerate(corder):
                            gch = blk * 4 + c
                            csl = slice(gch * 32, gch * 32 + 32)
                            if not summary:
                                self.PE([(po[:, c * 32:(c + 1) * 32], Sbf[sidx % 2], Qi[:, csl], False, ci == 3)],
                                        [("Sb", sidx % 2), "Qi", pok], [pok])
                            pu, puk = self.ps()
                            self.PE([(pu[:, 0:128], Ktm[:, c, :], vt[:, blk, :], True, True)], [("Ktm", c), "vt"], [puk])
                            self.stt(Sf32, Sf32, Ach[:, gch:gch + 1], pu[:, 0:128], ALU.mult, ALU.add,
                                     reads=["S", "Ach", puk], writes=["S"])
                            sidx += 1
                            if not summary:
                                self.cp(Sbf[sidx % 2], Sf32, ["S"], [("Sb", sidx % 2)])
                        if not summary:
                            if d == 0:
                                self.cp(o32[:, bsl], po[:, 0:128], [pok], ["o32"])
                            else:
                                self.tt(o32[:, bsl], o32[:, bsl], po[:, 0:128], ALU.add, [pok, "o32"], ["o32"])
                    if mode == "P":
                        dst = (self.nsf_out if d == 0 else self.nsb_out)[si, l, hh]
                        self.DMA(dst, Sf32, reads=["S"], writes=["nso"], lane="out")
                    if summary:
                        dst = self.xin_v("Sf" if d == 0 else "Sb", 65536).rearrange("(h p c) -> h p c", h=4, p=128)[hh]
                        self.xput("S%d" % d, dst, Sf32, ["S"])
            if not summary:
                for tt_ in range(T // 512):
                    sl = slice(tt_ * 512, (tt_ + 1) * 512)
                    t, tk = self.tmp()
                    self.act(t[:], o32[:, sl], AF.Square, reads=["o32"], writes=[tk])
                    pm, pmk = self.ps()
                    self.PE([(pm[:], self.onesF[:], t[:], True, True)], [tk, "onesF"], [pmk])
                    r, rk = self.tmp()
                    self.act(r[:], pm[:], AF.Ln, bias=self.epsT[:, 0:1], scale=1.0 / 128, reads=[pmk, "epsT"], writes=[rk])
                    self.act(r[:], r[:], AF.Exp, scale=-0.5, reads=[rk], writes=[rk])
                    self.tt(r[:], r[:], o32[:, sl], ALU.mult, [rk, "o32"], [rk])
                    self.stt(od[:, hh, sl], r[:], prm[:, PRM["hgng"] + l:PRM["hgng"] + l + 1], gT[:, sl], ALU.mult, ALU.mult,
                             reads=[rk, "prm", "gT"], writes=["mo"])
            self.fw.barrier()
        if not summary:
            self.branch_out(l, 3, od, 4, self.W["w_hg_out"][l], T, first=False, h2off=4 * T)

    def wo_ffn(self, l, mode, T):
        kind = 0 if mode == "P" else 1
        NTT = T // 512
        g1 = self.modT[:, l, 32:48, kind]
        g2 = self.modT[:, l, 80:96, kind]
        for cb in range(8):
            w, wk = self.wload(self.W["w_o"][l], 0, 16, cb * 256, 256)
            for m in range(2):
                ch = cb * 2 + m
                for tt_ in range(NTT):
                    sl = slice(tt_ * 512, (tt_ + 1) * 512)
                    pt, pk = self.ps()
                    self.PE([(pt[:], w[:, kc, m * 128:(m + 1) * 128], self.mg[:, kc, sl], kc == 0, kc == 15)
                             for kc in range(16)], [wk] + [("mg", kc) for kc in range(16)], [pk])
                    self.stt(self.xT[:, ch, sl], pt[:], g1[:, ch:ch + 1], self.xT[:, ch, sl], ALU.mult, ALU.add,
                             reads=[pk, "modT", "x"], writes=["x"])
        self.fw.barrier()
        self.set_norm(l, 2, kind)
        self.norm_stats(T)
        ff, off = self.carve(0, 11 * T)
        ff = ff.rearrange("p (j t) -> p j t", j=11)
        tiles = self.h_tiles(T, off)
        for grp in range(4):
            j = 0
            while j < 11:
                nj = 2 if j + 1 < 11 else 1
                col = (grp * 11 + j) * 128
                w1, w1k = self.wload(self.W["ffn_w1"][l], 0, 16, col, 128 * nj)
                held = {}
                for m in range(nj):
                    for ti, (hb, hk) in enumerate(tiles):
                        pt, pk = self.ps()
                        self.PE([(pt[:], w1[:, kc, m * 128:(m + 1) * 128], hb[:, kc, :], kc == 0, kc == 15)
                                 for kc in range(16)], [w1k] + hk, [pk])
                        t, tk = self.tmp()
                        self.act(t[:], pt[:], AF.Silu, reads=[pk], writes=[tk])
                        held[(m, ti)] = (t, tk)
                w3, w3k = self.wload(self.W["ffn_w3"][l], 0, 16, col, 128 * nj)
                for m in range(nj):
                    for ti, (hb, hk) in enumerate(tiles):
                        pt, pk = self.ps()
                        self.PE([(pt[:], w3[:, kc, m * 128:(m + 1) * 128], hb[:, kc, :], kc == 0, kc == 15)
                                 for kc in range(16)], [w3k] + hk, [pk])
                        t, tk = held[(m, ti)]
                        self.tt(ff[:, j + m, ti * 512:(ti + 1) * 512], t[:], pt[:], ALU.mult, [tk, pk], [("ff", j + m)])
                j += nj
            for cb in range(8):
                w, wk = self.wload(self.W["ffn_w2"][l], grp * 11, 11, cb * 256, 256)
                for m in range(2):
                    ch = cb * 2 + m
                    for tt_ in range(NTT):
                        sl = slice(tt_ * 512, (tt_ + 1) * 512)
                        pt, pk = self.ps()
                        self.PE([(pt[:], w[:, jj, m * 128:(m + 1) * 128], ff[:, jj, sl], jj == 0, jj == 10)
                                 for jj in range(11)], [wk] + [("ff", jj) for jj in range(11)], [pk])
                        self.stt(self.xT[:, ch, sl], pt[:], g2[:, ch:ch + 1], self.xT[:, ch, sl], ALU.mult, ALU.add,
                                 reads=[pk, "modT", "x"], writes=["x"])


_CACHE = {}


def host_consts(core):
    j = core % 4
    ident = np.eye(128, dtype=np.float32)
    s_ = np.arange(128)[:, None]
    t_ = np.arange(128)[None, :]
    same = (s_ // 32) == (t_ // 32)
    hm = np.zeros((128, 2, 128), np.float32)
    hm[:, 0, :] = (same & (s_ <= t_)).astype(np.float32)
    hm[:, 1, :] = (same & (s_ >= t_)).astype(np.float32)
    sm = np.ones((128, 1024), np.float32)
    sm[:, ::32] = 0.0
    low = np.where(s_ >= t_, 0.0, NEG).astype(np.float32)
    up = np.where(s_ <= t_, 0.0, NEG).astype(np.float32)
    allneg = np.full((128, 128), NEG, np.float32)
    ab = np.zeros((128, 10, 128), np.float32)
    ab[:, 0, :] = low
    ab[:, 1, :] = up
    for r in range(4):
        ab[:, 2 + r, :] = low if r == j - 1 else allneg
        ab[:, 6 + r, :] = up if r == j + 1 else allneg
    tg = j * 1024 + np.arange(1024)
    row = (tg // 64).astype(np.float32)
    col = (tg % 64).astype(np.float32)
    inv = (np.float32(10000.0) ** (-np.arange(0, 64, 2, dtype=np.float32) / np.float32(64))).astype(np.float32)
    ang = np.stack([row[:, None] * inv, col[:, None] * inv], axis=1).astype(np.float32)
    cs = np.stack([np.cos(ang), np.sin(ang)], axis=1).reshape(1024, 2, 64)
    rope = np.ascontiguousarray(cs.reshape(8, 128, 2, 64).transpose(1, 0, 2, 3)).astype(np.float32)
    return ident, hm, sm, ab, rope


def make_in_maps(inp, pr):
    in_maps = []
    xp = inp["x_prompt"].astype(np.float32)
    xs = inp["x_sample"].astype(np.float32)
    for c in range(8):
        g, j = c // 4, c % 4
        ident, hm, sm, ab, rope = host_consts(c)
        m = {
            "xT_p": np.ascontiguousarray(xp[2 * c:2 * c + 2].reshape(512, D).T),
            "xT_s": np.ascontiguousarray(xs[g, j * 1024:(j + 1) * 1024].T),
            "ck": np.ascontiguousarray(inp["cache_k"][g].reshape(L, 512, 256)),
            "cv": np.ascontiguousarray(inp["cache_v"][g].reshape(L, 512, 256)),
            "s0f": np.ascontiguousarray(inp["state_hgrn_fwd"][g]),
            "s0b": np.ascontiguousarray(inp["state_hgrn_bwd"][g]),
            "cvec": np.ascontiguousarray(np.stack([fm(inp["c_ctx"], 16), fm(inp["c"][g], 16)], axis=-1)),
            "prm": pack_params(inp, c),
            "qkg": np.ascontiguousarray(np.broadcast_to(
                np.concatenate([inp["q_norm_g"], inp["k_norm_g"]], axis=1)[:, None, :], (L, 128, 256))).astype(np.float32),
            "ident": ident, "hmask": hm, "smask": sm, "abias": ab, "rope": rope,
        }
        for n in pr.W:
            m[n] = np.ascontiguousarray(inp[n][:pr.nl], dtype=np.float32)
        in_maps.append(m)
    return in_maps


def run_and_gather(nc, in_maps):
    res = run_bass_kernel_spmd(nc, in_maps, core_ids=list(range(8)))
    R = res.results
    yp = np.stack([R[c]["yT_p"].T.reshape(2, 256, D) for c in range(8)]).reshape(16, 256, D)
    ys = np.stack([np.concatenate([R[g * 4 + j]["yT_s"].T for j in range(4)], 0) for g in range(2)])
    nk = np.concatenate([R[c]["nk"] for c in range(8)], 0).reshape(16, L, 256, 2, 128)
    nv = np.concatenate([R[c]["nv"] for c in range(8)], 0).reshape(16, L, 256, 2, 128)
    nsf = np.concatenate([R[c]["nsf"] for c in range(8)], 0)
    nsb = np.concatenate([R[c]["nsb"] for c in range(8)], 0)
    return (yp.astype(np.float32), ys.astype(np.float32), nk.astype(np.float32), nv.astype(np.float32),
            nsf.astype(np.float32), nsb.astype(np.float32))


def kernel(**inp):
    inp = {k: np.asarray(v) for k, v in inp.items()}
    if "pr" not in _CACHE:
        pr = Prog(do_sample=True)
        _CACHE["pr"] = pr
        _CACHE["nc"] = pr.build()
    pr = _CACHE["pr"]
    return run_and_gather(_CACHE["nc"], make_in_maps(inp, pr))
```

```python
import numpy as np
from contextlib import ExitStack
import concourse.bass as bass
import concourse.mybir as mybir
from concourse.bass_utils import run_bass_kernel_spmd

F32 = mybir.dt.float32
BF16 = mybir.dt.bfloat16
AF = mybir.ActivationFunctionType
ALU = mybir.AluOpType
AX = mybir.AxisListType

D = 2048
KD = 16
L = 4
DFF = 5632
INC = 6656
EPS = 1e-6
SCALE = 128 ** -0.5
NEG = -30000.0
ARENA = 25600
CQ, CK, CV, CCA, CCG, CSB, CSC, CSH, CHQ, CHF, CHB, CHI, CHG = (
    0, 1024, 1280, 1536, 2048, 2560, 3072, 3584, 4096, 4608, 5120, 5632, 6144)


class Sem:
    def __init__(self, name):
        self.name = name
        self.h = None
        self.count = 0


class FW:
    ENGS = ("pe", "act", "dve", "pool", "sp")

    def __init__(self):
        self.sems = {}
        self.rec = {e: [] for e in self.ENGS}
        self.waited = {e: {} for e in self.ENGS}
        self.lastw = {}
        self.readers = {}
        for e in self.ENGS:
            self.sem(e)

    def sem(self, name):
        if name not in self.sems:
            self.sems[name] = Sem(name)
        return self.sems[name]

    def op(self, eng, fn, reads=(), writes=(), lane=None, linc=16):
        deps = {}

        def add(ev):
            if ev is not None and deps.get(ev[0], 0) < ev[1]:
                deps[ev[0]] = ev[1]

        for k in reads:
            add(self.lastw.get(k))
        for k in writes:
            add(self.lastw.get(k))
            for ev in self.readers.get(k, {}).items():
                add(ev)
        if eng == "pe":
            deps.pop("pe", None)
        waits = []
        wd = self.waited[eng]
        for s, v in deps.items():
            if wd.get(s, 0) < v:
                wd[s] = v
                waits.append((s, v))
        if lane is None:
            sm = self.sems[eng]
            sm.count += 1
            inc = 1
        else:
            sm = self.sem(lane)
            sm.count += linc
            inc = linc
        ev = (sm.name, sm.count)
        self.rec[eng].append((waits, fn, sm.name, inc))
        for k in writes:
            self.lastw[k] = ev
            self.readers[k] = {}
        for k in reads:
            r = self.readers.setdefault(k, {})
            if r.get(ev[0], 0) < ev[1]:
                r[ev[0]] = ev[1]
        return ev

    def barrier(self, engs=("pe", "act", "dve", "sp")):
        cur = [(s.name, s.count) for s in self.sems.values() if s.count > 0]
        for e in engs:
            waits = []
            wd = self.waited[e]
            for s, v in cur:
                if wd.get(s, 0) < v:
                    wd[s] = v
                    waits.append((s, v))
            if waits:
                self.rec[e].append((waits, None, None, 0))

    def emit(self, nc, stack):
        for s in self.sems.values():
            s.h = stack.enter_context(nc.semaphore(s.name))
        block = stack.enter_context(nc.Block())
        sems = self.sems

        def run(e, lst):
            for waits, fn, sname, inc in lst:
                for s, v in waits:
                    e.wait_ge(sems[s].h, v)
                if fn is not None:
                    fn(e).then_inc(sems[sname].h, inc)

        rec = self.rec

        @block.tensor
        def _(e):
            run(e, rec["pe"])

        @block.scalar
        def _(e):
            run(e, rec["act"])

        @block.vector
        def _(e):
            run(e, rec["dve"])

        @block.gpsimd
        def _(e):
            run(e, rec["pool"])

        @block.sync
        def _(e):
            run(e, rec["sp"])


PRM = {}
_off = 0
for _n, _w in (("n1g", L * 16), ("n2g", L * 16), ("adab", L * 96), ("bgate", L * 64), ("cfw", L * 4 * 31),
               ("cfb", L * 4), ("cflg", L * 4), ("cflb", L * 4), ("scw", L * 4 * 3), ("hglb", 2 * L * 4),
               ("hgng", L), ("sink", L * 8), ("sel", 32), ("cm", 4)):
    PRM[_n] = _off
    _off += _w
NP_ = _off

XR = {"A": 1280, "B": 1152}
XO = dict(kf=("A", 0), kl=("A", 32768), vf=("A", 65536), vl=("A", 98304), uL=("A", 131072), uR=("A", 139264),
          wL=("A", 147456), wR=("A", 147968), Sf=("B", 0), Sb=("B", 65536), Af=("B", 131072), Ab=("B", 131584))


def fm(v, nchunks):
    v = np.asarray(v, np.float32)
    lead = v.shape[:-1]
    v = v.reshape(lead + (nchunks, 128))
    return np.moveaxis(v, -1, 0)


def pack_params(inp, core):
    P = np.zeros((128, NP_), np.float32)

    def put(name, arr):
        a = np.ascontiguousarray(arr).reshape(128, -1)
        P[:, PRM[name]:PRM[name] + a.shape[1]] = a

    put("n1g", fm(inp["norm1_g"], 16))
    put("n2g", fm(inp["norm2_g"], 16))
    put("adab", fm(inp["ada_b"], 96))
    put("bgate", fm(inp["b_gate"], 64))
    put("cfw", np.transpose(fm(np.transpose(inp["conf_dw_w"], (0, 1, 2)), 4), (0, 1, 3, 2)))
    put("cfb", fm(inp["conf_dw_b"], 4))
    put("cflg", fm(inp["conf_ln_g"], 4))
    put("cflb", fm(inp["conf_ln_b"], 4))
    put("scw", np.transpose(fm(inp["sc_conv_w"], 4), (0, 1, 3, 2)))
    put("hglb", fm(inp["hg_lb"], 4))
    put("hgng", np.transpose(np.asarray(inp["hg_norm_g"], np.float32), (1, 0)))
    put("sink", np.broadcast_to(np.asarray(inp["attn_sink"], np.float32).reshape(1, L * 8), (128, L * 8)))
    j = core % 4
    sel = np.zeros(16, np.float32)
    for i in range(4):
        sel[i] = 1.0 if i == j - 1 else 0.0
        sel[4 + i] = 1.0 if i == j + 1 else 0.0
        sel[8 + i] = 1.0 if i < j else 0.0
        sel[12 + i] = 1.0 if i > j else 0.0
    sel = np.concatenate([sel, 1.0 - sel])
    put("sel", np.broadcast_to(sel.reshape(1, 32), (128, 32)))
    cm = np.zeros((128, 4), np.float32)
    for p in range(128):
        cm[p, p // 32] = 1.0
    put("cm", cm)
    return P


class Prog:
    def __init__(self, do_sample=True, nl=L):
        self.do_sample = do_sample
        self.nl = nl
        self.nc = bass.Bass("TRN2", target_bir_lowering=False)
        self.fw = FW()
        self.wi = 0
        self.psi = 0
        self.tmpi = 0
        self.xkeys = []

    def dram_in(self, name, shape):
        return self.nc.dram_tensor(name, list(shape), F32, kind="ExternalInput").ap()

    def dram_out(self, name, shape):
        return self.nc.dram_tensor(name, list(shape), F32, kind="ExternalOutput").ap()

    def V(self, fn, reads=(), writes=()):
        return self.fw.op("dve", fn, reads, writes)

    def A(self, fn, reads=(), writes=()):
        return self.fw.op("act", fn, reads, writes)

    def PE(self, mms, reads=(), writes=()):
        def fn(e, mms=mms):
            ins = None
            for (o, l, r, st, sp) in mms:
                ins = e.matmul(o, l, r, start=st, stop=sp)
            return ins
        return self.fw.op("pe", fn, reads, writes)

    def TR(self, out, in_, ident, reads=(), writes=()):
        return self.fw.op("pe", lambda e: e.transpose(out, in_, ident), reads, writes)

    def DMA(self, out, in_, reads=(), writes=(), lane="ld", eng="sp", slow=False):
        if slow:
            ev = self.fw.op(eng, lambda e: e.dma_start(out=out, in_=in_, allow_slow_non_contiguous=True), reads, writes,
                            lane=lane)
        else:
            ev = self.fw.op(eng, lambda e: e.dma_start(out=out, in_=in_), reads, writes, lane=lane)
        wd = self.fw.waited[eng]
        if wd.get(ev[0], 0) < ev[1]:
            wd[ev[0]] = ev[1]
            self.fw.rec[eng].append(([ev], None, None, 0))
        return ev

    def DMAc(self, out, in_, reads=(), writes=()):
        self.fw.barrier(("pool",))
        return self.DMA(out, in_, reads, writes, lane="pl", eng="pool")

    def act(self, out, in_, func, bias=None, scale=1.0, reads=(), writes=()):
        if bias is None:
            return self.A(lambda e: e.activation(out=out, in_=in_, func=func, scale=scale), reads, writes)
        return self.A(lambda e: e.activation(out=out, in_=in_, func=func, bias=bias, scale=scale), reads, writes)

    def tt(self, out, a, b, op, reads=(), writes=()):
        return self.V(lambda e: e.tensor_tensor(out=out, in0=a, in1=b, op=op), reads, writes)

    def ts(self, out, a, s1, s2, op0, op1=None, reads=(), writes=()):
        if op1 is None:
            return self.V(lambda e: e.tensor_scalar(out=out, in0=a, scalar1=s1, scalar2=None, op0=op0), reads, writes)
        return self.V(lambda e: e.tensor_scalar(out=out, in0=a, scalar1=s1, scalar2=s2, op0=op0, op1=op1), reads, writes)

    def stt(self, out, in0, scalar, in1, op0, op1, reads=(), writes=()):
        return self.V(lambda e: e.scalar_tensor_tensor(out=out, in0=in0, scalar=scalar, in1=in1, op0=op0, op1=op1),
                      reads, writes)

    def cp(self, out, in_, reads=(), writes=()):
        return self.V(lambda e: e.tensor_copy(out=out, in_=in_), reads, writes)

    def memset(self, ap, val, writes=()):
        return self.V(lambda e: e.memset(ap, val), (), writes)

    def ps(self):
        i = self.psi % 6
        self.psi += 1
        return self.psb[i], ("ps", i)

    def tmp(self):
        i = self.tmpi % 4
        self.tmpi += 1
        return self.tmps[i], ("tmp", i)

    def wload(self, W, k0, nk, c0, ncol):
        s = self.wi % 2
        self.wi += 1
        src = W[k0 * 128:(k0 + nk) * 128, c0:c0 + ncol].rearrange("(k p) n -> p k n", p=128)
        dst = self.wsl[s][:, 0:nk, 0:ncol]
        self.fw.op("pool", lambda e: e.dma_start(out=dst, in_=src), (), [("w", s)], lane="w%d" % s)
        return self.wsl[s], ("w", s)

    def xin_v(self, name, n):
        b, o = XO[name]
        flat = self.xin[b].rearrange("r c -> (r c)")
        return flat[o:o + n]

    def xout_v(self, name, n):
        b, o = XO[name]
        return self.xout[b].rearrange("(r q) c -> r (q c)", r=4)[:, o:o + n]

    def xput(self, name, dst, src, reads):
        key = ("xin", XO.get(name, XO.get({"A0": "Af", "A1": "Ab", "S0": "Sf", "S1": "Sb"}.get(name, name)))[0], name,
               len(self.xkeys))
        self.xkeys.append(key)
        self.DMA(dst, src, reads=reads, writes=[key], lane="xw")

    def build(self):
        nc = self.nc
        di = self.dram_in
        NL_ = self.nl
        self.xp_in = di("xT_p", (D, 512))
        self.xs_in = di("xT_s", (D, 1024))
        self.ck_in = di("ck", (L, 512, 256))
        self.cv_in = di("cv", (L, 512, 256))
        self.s0f_in = di("s0f", (L, 4, 128, 128))
        self.s0b_in = di("s0b", (L, 4, 128, 128))
        self.cvec_in = di("cvec", (128, 16, 2))
        self.prm_in = di("prm", (128, NP_))
        self.qkg_in = di("qkg", (L, 128, 256))
        self.ident_in = di("ident", (128, 128))
        self.hmask_in = di("hmask", (128, 2, 128))
        self.abias_in = di("abias", (128, 10, 128))
        self.rope_in = di("rope", (128, 8, 2, 64))
        self.smask_in = di("smask", (128, 1024))
        self.W = {}
        for n, shp in (("ada_w", (NL_, D, 6 * D)), ("w_in", (NL_, D, INC)), ("w_attn_out", (NL_, 1024, D)),
                       ("w_conf_out", (NL_, 512, D)), ("w_sc_out", (NL_, 512, D)), ("w_hg_out", (NL_, 512, D)),
                       ("w_gate", (NL_, D, 4 * D)), ("w_o", (NL_, D, D)), ("ffn_w1", (NL_, D, DFF)),
                       ("ffn_w3", (NL_, D, DFF)), ("ffn_w2", (NL_, DFF, D))):
            self.W[n] = di(n, shp)
        do = self.dram_out
        self.yp_out = do("yT_p", (D, 512))
        self.ys_out = do("yT_s", (D, 1024))
        self.nk_out = do("nk", (2, L, 256, 256))
        self.nv_out = do("nv", (2, L, 256, 256))
        self.nsf_out = do("nsf", (2, L, 4, 128, 128))
        self.nsb_out = do("nsb", (2, L, 4, 128, 128))
        self.xin = {b: nc.dram_tensor("xch_in" + b, [128, XR[b]], F32).ap() for b in XR}
        self.xout = {b: nc.dram_tensor("xch_out" + b, [512, XR[b]], F32).ap() for b in XR}

        with ExitStack() as st:
            sb = lambda n, s, d: st.enter_context(nc.sbuf_tensor(n, s, d))
            self.xT = sb("xT", [128, 16, 1024], F32)
            self.mg = sb("mg", [128, 16, 1024], BF16)
            self.hT = sb("hT", [128, 16, 512], BF16)
            self.wsl = [sb("wsl%d" % i, [128, 16, 256], BF16) for i in range(2)]
            self.rstd = sb("rstd", [128, 1024], F32)
            self.prm = sb("prm_s", [128, NP_], F32)
            self.modT = sb("modT", [128, L, 96, 2], F32)
            self.gsh = sb("gsh", [128, 2, 16], F32)
            self.identF = sb("identF", [128, 128], F32)
            self.identB = sb("identB", [128, 128], BF16)
            self.onesB = sb("onesB", [128, 128], BF16)
            self.onesF = sb("onesF", [128, 128], F32)
            self.epsT = sb("epsT", [128, 1], F32)
            self.lbv = sb("lbv", [128, 2, L, 4], F32)
            self.oml = sb("oml", [128, 2, L, 4], F32)
            self.csb = sb("csb", [128, 16, 2], BF16)
            self.tmps = [sb("tmp%d" % i, [128, 512], F32) for i in range(4)]
            self.hmask = sb("hmask_s", [128, 2, 128], F32)
            self.smask = sb("smask_s", [128, 1024], F32)
            self.arena = sb("arena", [128, ARENA], BF16)
            self.psb = [st.enter_context(nc.psum_tensor("ps%d" % i, [128, 512], F32)) for i in range(6)]
            self.psT = st.enter_context(nc.psum_tensor("psT", [128, 1024], BF16))
            self.psX = st.enter_context(nc.psum_tensor("psX", [128, 512], F32))
            self.program()
            self.fw.barrier(("sp",))
            self.fw.emit(nc, st)
        return nc

    def carve(self, off, n, dtype=BF16):
        if dtype == BF16:
            assert off + n <= ARENA, (off, n)
            return self.arena[:, off:off + n], off + n
        off += off % 2
        assert off + 2 * n <= ARENA, (off, n)
        return self.arena[:, off:off + 2 * n].bitcast(F32), off + 2 * n

    def program(self):
        fw = self.fw
        self.DMA(self.prm[:], self.prm_in, writes=["prm"])
        self.DMA(self.identF[:], self.ident_in, writes=["identF"])
        self.DMA(self.hmask[:], self.hmask_in, writes=["hmask"])
        self.DMA(self.smask[:], self.smask_in, writes=["smask"])
        self.cp(self.identB[:], self.identF[:], ["identF"], ["identB"])
        self.memset(self.onesB[:], 1.0, ["onesB"])
        self.memset(self.onesF[:], 1.0, ["onesF"])
        self.memset(self.epsT[:], EPS, ["epsT"])
        self.prologue()
        import os
        if "P" in os.environ.get("KPASS", "PS"):
            for l in range(self.nl):
                self.layer_pass(l, "P")
            self.DMA(self.yp_out.rearrange("(k p) t -> p k t", p=128), self.xT[:, :, 0:512], reads=["x"], writes=["yp"],
                     lane="out")
            fw.barrier()
        if self.do_sample:
            for l in range(self.nl):
                self.layer_pass(l, "S")
            self.DMA(self.ys_out.rearrange("(k p) t -> p k t", p=128), self.xT[:, :, 0:1024], reads=["x"],
                     writes=["ys"], lane="out")

    def prologue(self):
        prm = self.prm
        t, tk = self.tmp()
        cview = t[:, 0:32].rearrange("p (k t) -> p k t", t=2)
        self.DMA(cview, self.cvec_in, writes=[tk])
        self.act(self.csb[:], cview, AF.Silu, reads=[tk], writes=["csb"])
        lbr = prm[:, PRM["hglb"]:PRM["hglb"] + 32].rearrange("p (d l h) -> p d l h", d=2, l=L)
        e_, ek = self.tmp()
        ev = e_[:, 0:32].rearrange("p (d l h) -> p d l h", d=2, l=L)
        self.act(ev, lbr, AF.Exp, reads=["prm"], writes=[ek])
        s_, sk = self.tmp()
        sv = s_[:, 0:8].rearrange("p (d h) -> p d h", d=2)
        self.tt(sv, ev[:, :, 0, :], ev[:, :, 1, :], ALU.add, [ek], [sk])
        self.tt(sv, sv, ev[:, :, 2, :], ALU.add, [ek, sk], [sk])
        self.tt(sv, sv, ev[:, :, 3, :], ALU.add, [ek, sk], [sk])
        self.V(lambda e: e.reciprocal(out=sv, in_=sv), [sk], [sk])
        for l in range(L):
            self.tt(ev[:, :, l, :], ev[:, :, l, :], sv, ALU.mult, [ek, sk], [ek])
        self.memset(self.lbv[:, :, 0, :], 0.0, ["lbv"])
        for l in range(1, L):
            self.tt(self.lbv[:, :, l, :], self.lbv[:, :, l - 1, :], ev[:, :, l, :], ALU.add, [ek, "lbv"], ["lbv"])
        self.ts(self.lbv[:], self.lbv[:], 0.0, None, ALU.max, reads=["lbv"], writes=["lbv"])
        self.ts(self.oml[:], self.lbv[:], -1.0, 1.0, ALU.mult, ALU.add, reads=["lbv"], writes=["oml"])
        for l in range(self.nl):
            for cb in range(48):
                w, wk = self.wload(self.W["ada_w"][l], 0, 16, cb * 256, 256)
                for m in range(2):
                    ch = cb * 2 + m
                    pt, pk = self.ps()
                    mms = [(pt[:, 0:2], w[:, kc, m * 128:(m + 1) * 128], self.csb[:, kc, :], kc == 0, kc == 15)
                           for kc in range(16)]
                    self.PE(mms, [wk, "csb"], [pk])
                    b = prm[:, PRM["adab"] + l * 96 + ch:PRM["adab"] + l * 96 + ch + 1]
                    self.ts(self.modT[:, l, ch, :], pt[:, 0:2], b, None, ALU.add, reads=[pk, "prm"], writes=["modT"])
        for part in (1, 4):
            v = self.modT[:, 0:self.nl, part * 16:(part + 1) * 16, :]
            self.ts(v, v, 1.0, None, ALU.add, reads=["modT"], writes=["modT"])

    def set_norm(self, l, which, kind):
        prm = self.prm
        nm = "n1g" if which == 1 else "n2g"
        g = prm[:, PRM[nm] + l * 16:PRM[nm] + l * 16 + 16]
        base = 0 if which == 1 else 48
        shift = self.modT[:, l, base:base + 16, kind]
        scale = self.modT[:, l, base + 16:base + 32, kind]
        self.tt(self.gsh[:, 0, :], g, scale, ALU.mult, ["prm", "modT"], ["gsh"])
        self.cp(self.gsh[:, 1, :], shift, ["modT"], ["gsh"])

    def tile_order(self, T):
        n = T // 512
        if n == 2 and getattr(self, "hcache", None) == 1:
            return [1, 0]
        return list(range(n))

    def norm_stats(self, T):
        self.hcache = None
        for tt_ in range(T // 512):
            sl = slice(tt_ * 512, (tt_ + 1) * 512)
            for kc in range(16):
                self.act(self.hT[:, kc, :], self.xT[:, kc, sl], AF.Square, reads=["x"], writes=[("h", kc)])
            mms = [(self.psX[:], self.onesB[:], self.hT[:, kc, :], kc == 0, kc == 15) for kc in range(16)]
            self.PE(mms, [("h", kc) for kc in range(16)] + ["onesB"], ["psX"])
            self.act(self.rstd[:, sl], self.psX[:], AF.Ln, bias=self.epsT[:, 0:1], scale=1.0 / D,
                     reads=["psX", "epsT"], writes=["rstd"])
            self.act(self.rstd[:, sl], self.rstd[:, sl], AF.Exp, scale=-0.5, reads=["rstd"], writes=["rstd"])

    def make_h(self, tt_, buf=None):
        sl = slice(tt_ * 512, (tt_ + 1) * 512)
        if buf is None:
            if getattr(self, "hcache", None) == tt_:
                return [("h", kc) for kc in range(16)]
            self.hcache = tt_
        hb, kp = (self.hT, "h") if buf is None else (buf, "h2")
        for kc in range(16):
            t, tk = self.tmp()
            self.tt(t[:], self.xT[:, kc, sl], self.rstd[:, sl], ALU.mult, ["x", "rstd"], [tk])
            self.act(hb[:, kc, :], t[:], AF.Identity, bias=self.gsh[:, 1, kc:kc + 1], scale=self.gsh[:, 0, kc:kc + 1],
                     reads=[tk, "gsh"], writes=[(kp, kc)])
        return [(kp, kc) for kc in range(16)]

    def h_tiles(self, T, h2off):
        if T == 512:
            return [(0, self.hT, self.make_h(0))]
        t0 = 1 if getattr(self, "hcache", None) == 1 else 0
        b, _ = self.carve(h2off, 16 * 512)
        b = b.rearrange("p (k t) -> p k t", k=16)
        return [(t0, self.hT, self.make_h(t0)), (1 - t0, b, self.make_h(1 - t0, b))]

    def proj_fm(self, W, c0, ncol, rhs_list, rkeys, consume):
        nk = len(rhs_list)
        nblk = (ncol + 255) // 256
        n = rhs_list[0].shape[1]
        for b in range(nblk):
            bc = min(256, ncol - b * 256)
            w, wk = self.wload(W, 0, nk, c0 + b * 256, bc)
            for m in range(bc // 128):
                pt, pk = self.ps()
                mms = [(pt[:, 0:n], w[:, k, m * 128:(m + 1) * 128], rhs_list[k], k == 0, k == nk - 1) for k in range(nk)]
                self.PE(mms, [wk] + list(rkeys), [pk])
                consume(b * 2 + m, pt, pk)

    def layer_pass(self, l, mode):
        import os
        fw = self.fw
        T = 512 if mode == "P" else 1024
        kind = 0 if mode == "P" else 1
        if l == 0:
            src = self.xp_in if mode == "P" else self.xs_in
            self.DMA(self.xT[:, :, 0:T], src.rearrange("(k p) t -> p k t", p=128), writes=["x"])
        self.set_norm(l, 1, kind)
        self.norm_stats(T)
        fw.barrier()
        mix = os.environ.get("KMIX", "BCADF")
        if mode == "S":
            self.pre_phase(l, T)
            fw.barrier()
        if "B" in mix:
            self.mixer_B(l, mode, T)
            fw.barrier()
        if "C" in mix:
            self.mixer_C(l, mode, T)
            fw.barrier()
        if "A" in mix:
            self.mixer_A(l, mode, T)
            fw.barrier()
        if "D" in mix:
            self.mixer_D(l, mode, T)
            fw.barrier()
        if "F" in mix:
            self.wo_ffn(l, mode, T)
            fw.barrier()

    def pre_phase(self, l, T):
        self.xkeys = []
        Win = self.W["w_in"][l]
        off = 0
        ropeT, off = self.carve(off, 8 * 128, F32)
        self.ropeT = ropeT.rearrange("p (b c f) -> p b c f", b=8, c=2)
        self.DMA(self.ropeT, self.rope_in, writes=["ropeT"])
        qkg, off = self.carve(off, 256, F32)
        self.DMA(qkg, self.qkg_in[l], writes=["qkg"])
        sq, off = self.carve(off, 256, F32)
        qn, off = self.carve(off, 256, F32)
        st2, off = self.carve(off, 8, F32)
        rt, off = self.carve(off, 512, F32)
        u16, off = self.carve(off, 16, F32)
        for tt_, tb, nk_, nv_, nu, nw, c16, wcol in ((0, 0, "kf", "vf", "uL", "wL", slice(0, 16), 0),
                                                     (1, 3, "kl", "vl", "uR", "wR", slice(496, 512), 15)):
            hk = self.make_h(tt_)
            blk = tt_ * 4 + tb
            w, wk = self.wload(Win, 0, 16, CK, 256)
            pt, pk = self.ps()
            self.PE([(pt[:, 0:256], self.hT[:, kc, tb * 128:(tb + 1) * 128], w[:, kc, 0:256], kc == 0, kc == 15)
                     for kc in range(16)], [wk] + hk, [pk])
            self.qk_norm(pt, pk, sq, st2, qn, qkg[:, 128:256])
            self.rope(qn, blk, rt)
            self.xput(nk_, self.xin_v(nk_, 32768).rearrange("(t c) -> t c", c=256), qn, ["qn"])
            w, wk = self.wload(Win, 0, 16, CV, 256)
            pt, pk = self.ps()
            self.PE([(pt[:, 0:256], self.hT[:, kc, tb * 128:(tb + 1) * 128], w[:, kc, 0:256], kc == 0, kc == 15)
                     for kc in range(16)], [wk] + hk, [pk])
            self.cp(qn, pt[:, 0:256], [pk], ["qn"])
            self.xput(nv_, self.xin_v(nv_, 32768).rearrange("(t c) -> t c", c=256), qn, ["qn"])
            hr = [self.hT[:, kc, c16] for kc in range(16)]
            for ch in range(4):
                hold = {}

                def cg(m, pt, pk, hold=hold):
                    t, tk = self.tmp()
                    self.act(t[:, 0:16], pt[:, 0:16], AF.Sigmoid, reads=[pk], writes=[tk])
                    hold["g"] = (t, tk)

                self.proj_fm(Win, CCG + ch * 128, 128, hr, hk, cg)

                def ca(m, pt, pk, hold=hold, ch=ch, nu=nu):
                    t, tk = hold["g"]
                    self.tt(u16, pt[:, 0:16], t[:, 0:16], ALU.mult, [pk, tk], ["u16"])
                    dst = self.xin_v(nu, 8192).rearrange("(c p w) -> c p w", c=4, p=128)[ch]
                    self.xput(nu, dst, u16, ["u16"])

                self.proj_fm(Win, CCA + ch * 128, 128, hr, hk, ca)

                def cc(m, pt, pk, hold=hold):
                    t, tk = self.tmp()
                    self.cp(t[:, 0:16], pt[:, 0:16], [pk], [tk])
                    hold["c"] = (t, tk)

                self.proj_fm(Win, CSC + ch * 128, 128, hr, hk, cc)

                def chh(m, pt, pk, hold=hold, ch=ch, nw=nw, wcol=wcol):
                    t, tk = hold["c"]
                    self.tt(u16, pt[:, 0:16], t[:, 0:16], ALU.mult, [pk, tk], ["u16"])
                    dst = self.xin_v(nw, 512).rearrange("(c p o) -> c p o", c=4, o=1)[ch]
                    self.xput(nw, dst, u16[:, wcol:wcol + 1], ["u16"])

                self.proj_fm(Win, CSH + ch * 128, 128, hr, hk, chh)
        self.fw.barrier()
        self.mixer_D(l, "S", T, summary=True)
        self.fw.barrier()
        grp = [[0, 1, 2, 3], [4, 5, 6, 7]]
        ka = [k for k in self.xkeys if k[1] == "A"]
        kb = [k for k in self.xkeys if k[1] == "B"]
        self.fw.op("pool", lambda e: e.collective_compute("AllGather", ALU.bypass, replica_groups=grp,
                                                          ins=[self.xin["A"]], outs=[self.xout["A"]]),
                   reads=ka, writes=["xoutA"], lane="cc", linc=1)
        self.fw.op("pool", lambda e: e.collective_compute("AllGather", ALU.bypass, replica_groups=grp,
                                                          ins=[self.xin["B"]], outs=[self.xout["B"]]),
                   reads=kb + ["xoutA"], writes=["xout", "xoutA"], lane="cc", linc=1)

    def qk_norm(self, pt, pk, sq, st2, qn, gain):
        self.act(sq, pt[:, 0:256], AF.Square, reads=[pk], writes=["sq"])
        self.V(lambda e: e.reduce_sum(out=st2[:, 0:2], in_=sq.rearrange("p (h d) -> p h d", h=2), axis=AX.X),
               ["sq"], ["st2"])
        self.act(st2[:, 2:4], st2[:, 0:2], AF.Ln, bias=self.epsT[:, 0:1], scale=1.0 / 128,
                 reads=["st2", "epsT"], writes=["st2b"])
        self.act(st2[:, 4:6], st2[:, 2:4], AF.Exp, scale=-0.5, reads=["st2b"], writes=["st2c"])
        for hd in range(2):
            self.stt(qn[:, hd * 128:(hd + 1) * 128], pt[:, hd * 128:(hd + 1) * 128], st2[:, 4 + hd:5 + hd],
                     gain, ALU.mult, ALU.mult, reads=[pk, "st2c", "qkg"], writes=["qn"])

    def rope(self, qn, blk, rt):
        v = qn.rearrange("p (h a f r) -> p h a f r", h=2, a=2, f=2)
        a1 = v[:, :, :, 0, :]
        a2 = v[:, :, :, 1, :]
        cos = self.ropeT[:, blk, 0, :].rearrange("p (a r) -> p a r", a=2).unsqueeze(1).to_broadcast([128, 2, 2, 32])
        sin = self.ropeT[:, blk, 1, :].rearrange("p (a r) -> p a r", a=2).unsqueeze(1).to_broadcast([128, 2, 2, 32])
        t = [rt[:, i * 128:(i + 1) * 128].rearrange("p (h a r) -> p h a r", h=2, a=2) for i in range(4)]
        self.tt(t[0], a1, cos, ALU.mult, ["qn", "ropeT"], ["rt0"])
        self.tt(t[1], a2, sin, ALU.mult, ["qn", "ropeT"], ["rt1"])
        self.tt(t[2], a2, cos, ALU.mult, ["qn", "ropeT"], ["rt2"])
        self.tt(t[3], a1, sin, ALU.mult, ["qn", "ropeT"], ["rt3"])
        self.tt(a1, t[0], t[1], ALU.subtract, ["rt0", "rt1"], ["qn"])
        self.tt(a2, t[2], t[3], ALU.add, ["rt2", "rt3", "qn"], ["qn"])

    def branch_out(self, l, n, moT, nk, Wout, T, first, h2off):
        prm = self.prm
        tiles = self.h_tiles(T, h2off)
        for cb in range(8):
            wg, wgk = self.wload(self.W["w_gate"][l], 0, 16, n * D + cb * 256, 256)
            pgs = {}
            for m in range(2):
                for ti, (tix, hb, hk) in enumerate(tiles):
                    pg, pgk = self.ps() if len(tiles) == 1 else (self.psb[m * 2 + ti], ("ps", m * 2 + ti))
                    self.PE([(pg[:], wg[:, kc, m * 128:(m + 1) * 128], hb[:, kc, :], kc == 0, kc == 15)
                             for kc in range(16)], [wgk] + hk, [pgk])
                    pgs[(m, ti)] = (pg, pgk)
            wo, wok = self.wload(Wout, 0, nk, cb * 256, 256)
            for m in range(2):
                ch = cb * 2 + m
                for ti in range(len(tiles)):
                    sl = slice(tiles[ti][0] * 512, (tiles[ti][0] + 1) * 512)
                    pg, pgk = pgs[(m, ti)]
                    py, pyk = self.ps() if len(tiles) == 1 else (self.psb[4 + ti], ("ps", 4 + ti))
                    self.PE([(py[:], wo[:, k, m * 128:(m + 1) * 128], moT[:, k, sl], k == 0, k == nk - 1)
                             for k in range(nk)], [wok, "mo"], [pyk])
                    t, tk = self.tmp()
                    bcol = PRM["bgate"] + l * 64 + n * 16 + ch
                    self.act(t[:], pg[:], AF.Sigmoid, bias=prm[:, bcol:bcol + 1], reads=[pgk, "prm"], writes=[tk])
                    if first:
                        self.tt(self.mg[:, ch, sl], t[:], py[:], ALU.mult, [tk, pyk], [("mg", ch)])
                    else:
                        self.tt(t[:], t[:], py[:], ALU.mult, [tk, pyk], [tk])
                        self.tt(self.mg[:, ch, sl], self.mg[:, ch, sl], t[:], ALU.add, [tk, ("mg", ch)], [("mg", ch)])

    def segs(self, mode):
        return [(0, 256), (256, 256)] if mode == "P" else [(0, 1024)]

    def mixer_B(self, l, mode, T):
        prm = self.prm
        segs = self.segs(mode)
        HAL = 15
        TP = T + 2 * HAL * len(segs)
        off = 0
        u, off = self.carve(off, 4 * TP)
        u = u.rearrange("p (c t) -> p c t", c=4)
        acc, off = self.carve(off, 4 * T, F32)
        acc = acc.rearrange("p (c t) -> p c t", c=4)
        yb, off = self.carve(off, 4 * T)
        yb = yb.rearrange("p (c t) -> p c t", c=4)
        self.memset(u, 0.0, ["u"])

        def upos(t0):
            for si, (s0, ln) in enumerate(segs):
                if s0 <= t0 < s0 + ln:
                    return si * (ln + 2 * HAL) + HAL + (t0 - s0)

        Win = self.W["w_in"][l]
        if mode == "S":
            cand, off = self.carve(off, 4 * 4 * 16, F32)
        tiles = self.h_tiles(T, off)

        def pieces(tt_):
            for (s0, ln) in segs:
                a = max(s0, tt_ * 512)
                b = min(s0 + ln, (tt_ + 1) * 512)
                if a < b:
                    yield upos(a), a - tt_ * 512, b - tt_ * 512

        for cp_ in range(2):
            wg, wgk = self.wload(Win, 0, 16, CCG + cp_ * 256, 256)
            held = {}
            for m in range(2):
                for (tt_, hb, hk) in tiles:
                    pt, pk = self.ps()
                    self.PE([(pt[:], wg[:, kc, m * 128:(m + 1) * 128], hb[:, kc, :], kc == 0, kc == 15)
                             for kc in range(16)], [wgk] + hk, [pk])
                    t, tk = self.tmp()
                    self.act(t[:], pt[:], AF.Sigmoid, reads=[pk], writes=[tk])
                    held[(m, tt_)] = (t, tk)
            wa, wak = self.wload(Win, 0, 16, CCA + cp_ * 256, 256)
            for m in range(2):
                ch = cp_ * 2 + m
                for (tt_, hb, hk) in tiles:
                    pt, pk = self.ps()
                    self.PE([(pt[:], wa[:, kc, m * 128:(m + 1) * 128], hb[:, kc, :], kc == 0, kc == 15)
                             for kc in range(16)], [wak] + hk, [pk])
                    t, tk = held[(m, tt_)]
                    for p0, a, b in pieces(tt_):
                        self.tt(u[:, ch, p0:p0 + (b - a)], pt[:, a:b], t[:, a:b], ALU.mult, [pk, tk], ["u"])
        if mode == "S":
            cand = cand.rearrange("p (c r w) -> p c r w", c=4, r=4)
            for side, nm, selb, lo, pos in ((0, "uR", 0, 1, 0), (1, "uL", 4, 0, HAL + T)):
                src = self.xout_v(nm, 8192).rearrange("r (c p w) -> c p r w", c=4, p=128)
                for ch in range(4):
                    self.DMA(cand[:, ch, :, :], src[ch], reads=["xout"], writes=["cand"])
                for ch in range(4):
                    dst = u[:, ch, pos:pos + HAL]
                    for r in range(4):
                        sc_ = prm[:, PRM["sel"] + selb + r:PRM["sel"] + selb + r + 1]
                        if r == 0:
                            self.ts(dst, cand[:, ch, r, lo:lo + HAL], sc_, None, ALU.mult, reads=["cand", "prm", "u"],
                                    writes=["u"])
                        else:
                            self.stt(dst, cand[:, ch, r, lo:lo + HAL], sc_, dst, ALU.mult, ALU.add,
                                     reads=["cand", "prm", "u"], writes=["u"])
        for ch in range(4):
            wb = PRM["cfw"] + (l * 4 + ch) * 31
            bb = PRM["cfb"] + l * 4 + ch
            for (s0, ln) in segs:
                p0 = upos(s0) - HAL
                o = acc[:, ch, s0:s0 + ln]
                self.ts(o, u[:, ch, p0:p0 + ln], prm[:, wb:wb + 1], prm[:, bb:bb + 1], ALU.mult, ALU.add,
                        reads=["u", "prm"], writes=[("acc", ch)])
                for k in range(1, 31):
                    self.stt(o, u[:, ch, p0 + k:p0 + k + ln], prm[:, wb + k:wb + k + 1], o, ALU.mult, ALU.add,
                             reads=["u", "prm", ("acc", ch)], writes=[("acc", ch)])
        for tt_ in range(T // 512):
            sl = slice(tt_ * 512, (tt_ + 1) * 512)
            pm, pmk = self.ps()
            self.PE([(pm[:], self.onesF[:], acc[:, ch, sl], ch == 0, ch == 3) for ch in range(4)],
                    [("acc", ch) for ch in range(4)] + ["onesF"], [pmk])
            mean, mk = self.tmp()
            self.ts(mean[:], pm[:], 1.0 / 512, None, ALU.mult, reads=[pmk], writes=[mk])
            for ch in range(4):
                self.tt(acc[:, ch, sl], acc[:, ch, sl], mean[:], ALU.subtract, [("acc", ch), mk], [("acc", ch)])
            pv, pvk = self.ps()
            for ch in range(4):
                t, tk = self.tmp()
                self.act(t[:], acc[:, ch, sl], AF.Square, reads=[("acc", ch)], writes=[tk])
                self.PE([(pv[:], self.onesF[:], t[:], ch == 0, ch == 3)], [tk, "onesF"] + ([pvk] if ch else []), [pvk])
            rs, rk = self.tmp()
            self.act(rs[:], pv[:], AF.Ln, bias=self.epsT[:, 0:1], scale=1.0 / 512, reads=[pvk, "epsT"], writes=[rk])
            self.act(rs[:], rs[:], AF.Exp, scale=-0.5, reads=[rk], writes=[rk])
            for ch in range(4):
                self.tt(acc[:, ch, sl], acc[:, ch, sl], rs[:], ALU.mult, [("acc", ch), rk], [("acc", ch)])
                gcol = PRM["cflg"] + l * 4 + ch
                bcol = PRM["cflb"] + l * 4 + ch
                self.act(yb[:, ch, sl], acc[:, ch, sl], AF.Silu, bias=prm[:, bcol:bcol + 1],
                         scale=prm[:, gcol:gcol + 1], reads=[("acc", ch), "prm"], writes=["mo"])
        self.fw.barrier()
        self.branch_out(l, 1, yb, 4, self.W["w_conf_out"][l], T, first=True, h2off=off)

    def mixer_C(self, l, mode, T):
        prm = self.prm
        segs = self.segs(mode)
        TP = T + 2 * len(segs)
        off = 0
        w_, off = self.carve(off, 4 * TP)
        w_ = w_.rearrange("p (c t) -> p c t", c=4)
        sbb, off = self.carve(off, 4 * T)
        sbb = sbb.rearrange("p (c t) -> p c t", c=4)
        yc, off = self.carve(off, 4 * T)
        yc = yc.rearrange("p (c t) -> p c t", c=4)
        acc, off = self.carve(off, T, F32)
        self.memset(w_, 0.0, ["u"])

        def upos(t0):
            for si, (s0, ln) in enumerate(segs):
                if s0 <= t0 < s0 + ln:
                    return si * (ln + 2) + 1 + (t0 - s0)

        Win = self.W["w_in"][l]
        if mode == "S":
            cand, off = self.carve(off, 16, F32)
        tiles = self.h_tiles(T, off)

        def pieces(tt_):
            for (s0, ln) in segs:
                a = max(s0, tt_ * 512)
                b = min(s0 + ln, (tt_ + 1) * 512)
                if a < b:
                    yield upos(a), a - tt_ * 512, b - tt_ * 512

        for cp_ in range(2):
            held = {}
            for sec, cbase in (("b", CSB), ("c", CSC), ("h", CSH)):
                w, wk = self.wload(Win, 0, 16, cbase + cp_ * 256, 256)
                for m in range(2):
                    ch = cp_ * 2 + m
                    for (tt_, hb, hk) in tiles:
                        sl = slice(tt_ * 512, (tt_ + 1) * 512)
                        pt, pk = self.ps()
                        self.PE([(pt[:], w[:, kc, m * 128:(m + 1) * 128], hb[:, kc, :], kc == 0, kc == 15)
                                 for kc in range(16)], [wk] + hk, [pk])
                        if sec == "b":
                            self.cp(sbb[:, ch, sl], pt[:], [pk], ["sbb"])
                        elif sec == "c":
                            t, tk = self.tmp()
                            self.cp(t[:], pt[:], [pk], [tk])
                            held[(m, tt_)] = (t, tk)
                        else:
                            t, tk = held[(m, tt_)]
                            for p0, a, b in pieces(tt_):
                                self.tt(w_[:, ch, p0:p0 + (b - a)], pt[:, a:b], t[:, a:b], ALU.mult, [pk, tk], ["u"])
        if mode == "S":
            cand = cand.rearrange("p (c r w) -> p c r w", c=4, r=4)
            for side, nm, selb, pos in ((0, "wR", 0, 0), (1, "wL", 4, 1 + T)):
                src = self.xout_v(nm, 512).rearrange("r (c p w) -> c p r w", c=4, p=128)
                for ch in range(4):
                    self.DMA(cand[:, ch, :, :], src[ch], reads=["xout"], writes=["cand"], slow=True)
                for ch in range(4):
                    dst = w_[:, ch, pos:pos + 1]
                    for r in range(4):
                        sc_ = prm[:, PRM["sel"] + selb + r:PRM["sel"] + selb + r + 1]
                        if r == 0:
                            self.ts(dst, cand[:, ch, r, :], sc_, None, ALU.mult, reads=["cand", "prm", "u"], writes=["u"])
                        else:
                            self.stt(dst, cand[:, ch, r, :], sc_, dst, ALU.mult, ALU.add, reads=["cand", "prm", "u"],
                                     writes=["u"])
        for ch in range(4):
            wb = PRM["scw"] + (l * 4 + ch) * 3
            for (s0, ln) in segs:
                p0 = upos(s0) - 1
                o = acc[:, s0:s0 + ln]
                self.ts(o, w_[:, ch, p0:p0 + ln], prm[:, wb:wb + 1], None, ALU.mult, reads=["u", "prm"], writes=["acc"])
                for k in (1, 2):
                    self.stt(o, w_[:, ch, p0 + k:p0 + k + ln], prm[:, wb + k:wb + k + 1], o, ALU.mult, ALU.add,
                             reads=["u", "prm", "acc"], writes=["acc"])
            self.tt(yc[:, ch, 0:T], acc[:, 0:T], sbb[:, ch, 0:T], ALU.mult, ["acc", "sbb"], ["mo"])
        self.fw.barrier()
        self.branch_out(l, 2, yc, 4, self.W["w_sc_out"][l], T, first=False, h2off=off)

    def mixer_A(self, l, mode, T):
        prm = self.prm
        segs = self.segs(mode)
        NB = T // 128
        off = 0
        qT, off = self.carve(off, 8 * T)
        qT = qT.rearrange("p (h t) -> p h t", h=8)
        kT, off = self.carve(off, 2 * T)
        kT = kT.rearrange("p (h t) -> p h t", h=2)
        vtm, off = self.carve(off, NB * 256)
        vtm = vtm.rearrange("p (b c) -> p b c", c=256)
        base = off
        qkg, off = self.carve(off, 256, F32)
        sq, off = self.carve(off, 256, F32)
        qn, off = self.carve(off, 256, F32)
        qnb, off = self.carve(off, 256)
        st2, off = self.carve(off, 8, F32)
        if mode == "S":
            ropeT, off = self.carve(off, 8 * 128, F32)
            self.ropeT = ropeT.rearrange("p (b c f) -> p b c f", b=8, c=2)
            self.DMA(self.ropeT, self.rope_in, writes=["ropeT"])
            rt, off = self.carve(off, 512, F32)
        self.DMA(qkg, self.qkg_in[l], writes=["qkg"])
        Win = self.W["w_in"][l]
        tiles = self.h_tiles(T, off)
        for wb in range(6):
            w, wk = self.wload(Win, 0, 16, wb * 256, 256)
            for (tt_, hb, hk) in tiles:
                for tb in range(4):
                    blk = tt_ * 4 + tb
                    tsl = slice(blk * 128, (blk + 1) * 128)
                    pt, pk = self.ps()
                    self.PE([(pt[:, 0:256], hb[:, kc, tb * 128:(tb + 1) * 128], w[:, kc, 0:256], kc == 0, kc == 15)
                             for kc in range(16)], [wk] + hk, [pk])
                    if wb == 5:
                        self.cp(vtm[:, blk, :], pt[:, 0:256], [pk], [("v", blk)])
                        if mode == "P":
                            self.cp(qn, pt[:, 0:256], [pk], ["qn"])
                            seq, r0 = blk // 2, (blk % 2) * 128
                            self.DMA(self.nv_out[seq, l, r0:r0 + 128, :], qn, reads=["qn"], writes=["nvo"], lane="out")
                        continue
                    self.qk_norm(pt, pk, sq, st2, qn, qkg[:, 0:128] if wb < 4 else qkg[:, 128:256])
                    if mode == "S":
                        self.rope(qn, blk, rt)
                    if wb == 4 and mode == "P":
                        seq, r0 = blk // 2, (blk % 2) * 128
                        self.DMA(self.nk_out[seq, l, r0:r0 + 128, :], qn, reads=["qn"], writes=["nko"], lane="out")
                    self.cp(qnb, qn, ["qn"], ["qnb"])
                    for hd in range(2):
                        self.TR(self.psT[:, hd * 128:(hd + 1) * 128], qnb[:, hd * 128:(hd + 1) * 128], self.identB[:],
                                ["qnb", "identB"], ["psT"])
                    if wb < 4:
                        dst = qT[:, wb * 2:wb * 2 + 2, tsl]
                        key = ("qT", wb // 2, blk)
                    else:
                        dst = kT[:, :, tsl]
                        key = ("kT", blk)
                    self.cp(dst, self.psT[:, 0:256].rearrange("p (h t) -> p h t", h=2), ["psT"], [key])
        self.fw.barrier()
        off = base
        pT = []
        for i in range(2):
            a, off = self.carve(off, 512)
            pT.append(a)
        sinkx, off = self.carve(off, 1024, F32)
        den, off = self.carve(off, 512, F32)
        s8, off = self.carve(off, 8, F32)
        sx = prm[:, PRM["sink"] + l * 8:PRM["sink"] + l * 8 + 8]
        self.act(s8, sx, AF.Exp, reads=["prm"], writes=["s8"])
        for h in range(8):
            self.cp(sinkx[:, h * 128:(h + 1) * 128], s8[:, h:h + 1].to_broadcast([128, 128]), ["s8"], ["sinkx"])
        if mode == "S":
            ctmp, off = self.carve(off, 4 * 256)
            ctmp = ctmp.rearrange("p (r c) -> p r c", r=4)
            ctxK, off = self.carve(off, 2 * 512)
            ctxK = ctxK.rearrange("p (h s) -> p h s", h=2)
            ctxV, off = self.carve(off, 4 * 256)
            ctxV = ctxV.rearrange("p (b c) -> p b c", b=4)
            candK, off = self.carve(off, 4 * 2 * 128)
            candK = candK.rearrange("p (r h s) -> p r h s", r=4, h=2)
            candV, off = self.carve(off, 4 * 256)
            candV = candV.rearrange("p (r c) -> p r c", r=4)
            biasB, off = self.carve(off, 10 * 128)
            biasB = biasB.rearrange("p (i t) -> p i t", i=10)
            self.DMAc(biasB, self.abias_in, writes=["biasB"])
            self.DMAc(ctxV, self.cv_in[l].rearrange("(b p) c -> p b c", p=128), writes=["ctx"])
            self.DMAc(ctmp, self.ck_in[l].rearrange("(b p) c -> p b c", p=128), writes=["ctmp"])
            for b in range(4):
                for h in range(2):
                    self.TR(self.psT[:, 0:128], ctmp[:, b, h * 128:(h + 1) * 128], self.identB[:], ["ctmp", "identB"], ["psT"])
                    self.cp(ctxK[:, h, b * 128:(b + 1) * 128], self.psT[:, 0:128], ["psT"], ["ctx"])

            def load_cands(nk_, nv_):
                self.DMAc(candV, self.xout_v(nv_, 32768).rearrange("r (t c) -> t r c", c=256), reads=["xout"],
                          writes=["cand"])
                self.DMAc(ctmp, self.xout_v(nk_, 32768).rearrange("r (t c) -> t r c", c=256), reads=["xout"],
                          writes=["ctmp"])
                for r in range(4):
                    for h in range(2):
                        self.TR(self.psT[:, 0:128], ctmp[:, r, h * 128:(h + 1) * 128], self.identB[:], ["ctmp", "identB"],
                                ["psT"])
                        self.cp(candK[:, r, h, :], self.psT[:, 0:128], ["psT"], ["cand"])
        pi = 0
        for (s0, ln) in segs:
            b0, nb = s0 // 128, ln // 128
            for qb in range(b0, b0 + nb):
                qsl = slice(qb * 128, (qb + 1) * 128)
                if mode == "S" and qb == 0:
                    load_cands("kl", "vl")
                if mode == "S" and qb == NB - 1:
                    load_cands("kf", "vf")
                for kvh in range(2):
                    hs = slice(kvh * 128, (kvh + 1) * 128)

                    def loc(kb, bias):
                        return (kT[:, kvh, kb * 128:(kb + 1) * 128], vtm[:, kb, hs], [("kT", kb), ("v", kb)], bias)

                    if mode == "P":
                        kbs = [loc(kb, None) for kb in range(b0, b0 + nb)]
                    else:
                        kbs = []
                        if qb == 0:
                            kbs += [(candK[:, r, kvh, :], candV[:, r, hs], ["cand"], 2 + r) for r in range(4)]
                        else:
                            kbs.append(loc(qb - 1, 0))
                        kbs.append(loc(qb, None))
                        if qb == NB - 1:
                            kbs += [(candK[:, r, kvh, :], candV[:, r, hs], ["cand"], 6 + r) for r in range(4)]
                        else:
                            kbs.append(loc(qb + 1, 1))
                        kbs += [(ctxK[:, kvh, cb * 128:(cb + 1) * 128], ctxV[:, cb, hs], ["ctx"], None) for cb in range(4)]
                    qrhs = qT[:, kvh * 4:(kvh + 1) * 4, qsl]
                    ppv, ppvk = self.psb[0], ("ps", 0)
                    pden, pdenk = self.psb[1], ("ps", 1)
                    nkb = len(kbs)
                    for i, (kap, vap, keys, bias) in enumerate(kbs):
                        psc, psck = self.psb[2 + pi % 4], ("ps", 2 + pi % 4)
                        mms = [(psc[:], kap, qrhs, True, bias is None)]
                        rk = list(keys) + [("qT", kvh, qb)]
                        if bias is not None:
                            mms += [(psc[:, g * 128:(g + 1) * 128], self.identB[:], biasB[:, bias, :], False, g == 3)
                                    for g in range(4)]
                            rk += ["identB", "biasB"]
                        self.PE(mms, rk, [psck])
                        p = pT[pi % 2]
                        pkk = ("pT", pi % 2)
                        pi += 1
                        self.act(p, psc[:], AF.Exp, scale=SCALE, reads=[psck], writes=[pkk])
                        self.PE([(ppv[:], vap, p, i == 0, i == nkb - 1)], list(keys) + [pkk] + ([ppvk] if i else []), [ppvk])
                        self.PE([(pden[:], self.onesB[:], p, i == 0, i == nkb - 1)], ["onesB", pkk] + ([pdenk] if i else []),
                                [pdenk])
                    self.tt(den, pden[:], sinkx[:, kvh * 512:(kvh + 1) * 512], ALU.add, [pdenk, "sinkx"], ["den"])
                    self.V(lambda e: e.reciprocal(out=den, in_=den), ["den"], ["den"])
                    self.tt(qrhs, ppv[:].rearrange("p (h t) -> p h t", h=4), den.rearrange("p (h t) -> p h t", h=4),
                            ALU.mult, [ppvk, "den"], [("qT", kvh, qb), "mo"])
        self.fw.barrier()
        self.branch_out(l, 0, qT, 8, self.W["w_attn_out"][l], T, first=False, h2off=8 * T)

    def mixer_D(self, l, mode, T, summary=False):
        prm = self.prm
        segs = self.segs(mode)
        NB = T // 128
        NCH = T // 32
        off = 0
        if not summary:
            od, off = self.carve(off, 4 * T)
            od = od.rearrange("p (h t) -> p h t", h=4)
        base = off
        Win = self.W["w_in"][l]
        for hh in range(4):
            off = base
            vt, off = self.carve(off, NB * 128)
            vt = vt.rearrange("p (b c) -> p b c", c=128)
            lf = []
            for d in range(2):
                a, off = self.carve(off, T, F32)
                lf.append(a)
            Bc, off = self.carve(off, T, F32)
            Ex, off = self.carve(off, T, F32)
            kk, off = self.carve(off, T)
            Kt, off = self.carve(off, T)
            Ktm, off = self.carve(off, 4 * 128)
            Ktm = Ktm.rearrange("p (c d) -> p c d", c=4)
            Sf32, off = self.carve(off, 128, F32)
            Sbf = []
            for i in range(2):
                a, off = self.carve(off, 128)
                Sbf.append(a)
            Ach, off = self.carve(off, NCH, F32)
            At, off = self.carve(off, 2, F32)
            if not summary:
                qf, off = self.carve(off, T)
                gT, off = self.carve(off, T)
                o32, off = self.carve(off, T, F32)
                Qi, off = self.carve(off, T)
                Qc, off = self.carve(off, T)
                Kc, off = self.carve(off, T)
                attS, off = self.carve(off, 128)
                Bmid, off = self.carve(off, NCH, F32)
                if mode == "S":
                    Sc, off = self.carve(off, 4 * 128, F32)
                    Sc = Sc.rearrange("p (r c) -> p r c", r=4)
                    Ac, off = self.carve(off, 4, F32)
                    coef, off = self.carve(off, 2, F32)
                    tS, off = self.carve(off, 128, F32)
            for tt_ in self.tile_order(T):
                sl = slice(tt_ * 512, (tt_ + 1) * 512)
                hk = self.make_h(tt_)
                hr = [self.hT[:, kc, :] for kc in range(16)]
                if not summary:
                    self.proj_fm(Win, CHQ + hh * 128, 128, hr, hk,
                                 lambda m, pt, pk, sl=sl: self.act(qf[:, sl], pt[:], AF.Silu, reads=[pk], writes=["qf0"]))
                    self.proj_fm(Win, CHG + hh * 128, 128, hr, hk,
                                 lambda m, pt, pk, sl=sl: self.act(gT[:, sl], pt[:], AF.Silu, reads=[pk], writes=["gT"]))
                for d, cbase in ((0, CHF), (1, CHB)):
                    def cons_f(m, pt, pk, d=d, sl=sl):
                        t, tk = self.tmp()
                        self.act(t[:], pt[:], AF.Sigmoid, reads=[pk], writes=[tk])
                        self.ts(t[:], t[:], self.oml[:, d, l, hh:hh + 1], self.lbv[:, d, l, hh:hh + 1], ALU.mult, ALU.add,
                                reads=[tk, "oml", "lbv"], writes=[tk])
                        self.act(lf[d][:, sl], t[:], AF.Ln, reads=[tk], writes=[("lf", d)])
                    self.proj_fm(Win, cbase + hh * 128, 128, hr, hk, cons_f)
                w, wk = self.wload(Win, 0, 16, CHI + hh * 128, 128)
                for tb in range(4):
                    blk = tt_ * 4 + tb
                    pt, pk = self.ps()
                    self.PE([(pt[:, 0:128], self.hT[:, kc, tb * 128:(tb + 1) * 128], w[:, kc, 0:128], kc == 0, kc == 15)
                             for kc in range(16)], [wk] + hk, [pk])
                    self.cp(vt[:, blk, :], pt[:, 0:128], [pk], ["vt"])
            if not summary:
                self.ts(qf[:, 0:T], qf[:, 0:T], SCALE, None, ALU.mult, reads=["qf0"], writes=["qf"])
            for d in range(2):
                self.act(Ex[:, 0:T], lf[d][:, 0:T], AF.Exp, reads=[("lf", d)], writes=["Ex"])
                self.ts(kk[:, 0:T], Ex[:, 0:T], -1.0, 1.0, ALU.mult, ALU.add, reads=["Ex"], writes=["kk"])
                B3 = Bc.rearrange("p (c j) -> p c j", j=32)
                E3 = Ex.rearrange("p (c j) -> p c j", j=32)
                self.V(lambda e, d=d: e.tensor_tensor_scan(out=Bc[:, 0:T], data0=self.smask[:, 0:T], data1=lf[d][:, 0:T],
                                                          initial=0.0, op0=ALU.mult, op1=ALU.add),
                       [("lf", d), "smask"], ["Bc"])
                self.cp(Ach[:, 0:NCH], B3[:, 0:NCH, 31], ["Bc"], ["Btot"])
                if d == 1:
                    self.tt(B3[:, 0:NCH, :], Ach[:, 0:NCH].unsqueeze(2).to_broadcast([128, NCH, 32]), B3[:, 0:NCH, :],
                            ALU.subtract, ["Bc", "Btot"], ["Bc"])
                    self.tt(Bc[:, 0:T], Bc[:, 0:T], lf[d][:, 0:T], ALU.add, ["Bc", ("lf", d)], ["Bc"])
                if not summary:
                    self.cp(Bmid[:, 0:NCH], B3[:, 0:NCH, 15], ["Bc"], ["Bmid"])
                    self.act(Ex[:, 0:T], Bc[:, 0:T], AF.Exp, reads=["Bc", "kk"], writes=["Ex"])
                    self.tt(Qi[:, 0:T], qf[:, 0:T], Ex[:, 0:T], ALU.mult, ["qf", "Ex"], ["Qi"])
                self.tt(E3[:, 0:NCH, :], Ach[:, 0:NCH].unsqueeze(2).to_broadcast([128, NCH, 32]), B3[:, 0:NCH, :],
                        ALU.subtract, ["Bc", "Btot", "Ex", "Qi", "kk"], ["Ex"])
                self.act(Ex[:, 0:T], Ex[:, 0:T], AF.Exp, reads=["Ex"], writes=["Ex"])
                self.tt(Kt[:, 0:T], kk[:, 0:T], Ex[:, 0:T], ALU.mult, ["kk", "Ex"], ["Kt"])
                if not summary:
                    self.tt(E3[:, 0:NCH, :], B3[:, 0:NCH, :], Bmid[:, 0:NCH].unsqueeze(2).to_broadcast([128, NCH, 32]),
                            ALU.subtract, ["Bc", "Bmid", "Ex", "Kt"], ["Ex"])
                    self.ts(Ex[:, 0:T], Ex[:, 0:T], -40.0, 40.0, ALU.max, ALU.min, reads=["Ex"], writes=["Ex"])
                    self.act(Bc[:, 0:T], Ex[:, 0:T], AF.Exp, reads=["Ex", "Bmid"], writes=["Bc"])
                    self.tt(Qc[:, 0:T], qf[:, 0:T], Bc[:, 0:T], ALU.mult, ["qf", "Bc"], ["Qc"])
                    self.act(Bc[:, 0:T], Ex[:, 0:T], AF.Exp, scale=-1.0, reads=["Ex", "Qc"], writes=["Bc"])
                    self.tt(Kc[:, 0:T], kk[:, 0:T], Bc[:, 0:T], ALU.mult, ["kk", "Bc"], ["Kc"])
                if summary:
                    self.V(lambda e: e.reduce_sum(out=At[:, 0:1], in_=Ach[:, 0:NCH], axis=AX.X), ["Btot"], ["At"])
                    self.act(At[:, 0:1], At[:, 0:1], AF.Exp, reads=["At"], writes=["At"])
                    dst = self.xin_v("Af" if d == 0 else "Ab", 512).rearrange("(h p o) -> h p o", h=4, o=1)[hh]
                    self.xput("A%d" % d, dst, At[:, 0:1], ["At"])
                self.act(Ach[:, 0:NCH], Ach[:, 0:NCH], AF.Exp, reads=["Btot", "Ex", "At"], writes=["Ach"])
                for si, (s0, ln) in enumerate(segs):
                    nblk = ln // 128
                    blks = list(range(s0 // 128, s0 // 128 + nblk))
                    if d == 1:
                        blks = blks[::-1]
                    if mode == "P" or summary:
                        self.memset(Sf32, 0.0, ["S"])
                    else:
                        s0in = (self.s0f_in if d == 0 else self.s0b_in)[l, hh]
                        self.DMA(Sf32, s0in, writes=["S"])
                        self.DMA(Sc, self.xout_v("Sf" if d == 0 else "Sb", 65536).rearrange(
                            "r (h p c) -> h p r c", h=4, p=128)[hh], reads=["xout"], writes=["Sc"])
                        self.DMA(Ac.rearrange("p (r o) -> p r o", o=1), self.xout_v("Af" if d == 0 else "Ab", 512).rearrange(
                            "r (h p o) -> h p r o", h=4, o=1)[hh], reads=["xout"], writes=["Ac"], slow=True)
                        order = [0, 1, 2, 3] if d == 0 else [3, 2, 1, 0]
                        sb_ = PRM["sel"] + (8 if d == 0 else 12)
                        for r in order:
                            m_ = prm[:, sb_ + r:sb_ + r + 1]
                            om_ = prm[:, sb_ + 16 + r:sb_ + 16 + r + 1]
                            self.ts(coef[:, 0:1], Ac[:, r:r + 1], m_, om_, ALU.mult, ALU.add, reads=["Ac", "prm"],
                                    writes=["coef"])
                            self.ts(tS, Sc[:, r, :], m_, None, ALU.mult, reads=["Sc", "prm"], writes=["tS"])
                            self.stt(Sf32, Sf32, coef[:, 0:1], tS, ALU.mult, ALU.add, reads=["S", "coef", "tS"], writes=["S"])
                    self.cp(Sbf[0], Sf32, ["S"], [("Sb", 0)])
                    sidx = 0
                    for blk in blks:
                        bsl = slice(blk * 128, (blk + 1) * 128)
                        if not summary:
                            pa, pak = self.ps()
                            self.PE([(pa[:, 0:128], Kc[:, bsl], Qc[:, bsl], True, True)], ["Kc", "Qc"], [pak])
                            self.tt(attS, pa[:, 0:128], self.hmask[:, d, :], ALU.mult, [pak, "hmask"], ["attS"])
                        self.TR(self.psT[:, 512:640], Kt[:, bsl], self.identB[:], ["Kt", "identB"], ["psT2"])
                        for c in range(4):
                            self.ts(Ktm[:, c, :], self.psT[:, 512:640], prm[:, PRM["cm"] + c:PRM["cm"] + c + 1], None,
                                    ALU.mult, reads=["psT2", "prm"], writes=[("Ktm", c)])
                        if not summary:
                            po, pok = self.ps()
                            self.PE([(po[:, 0:128], vt[:, blk, :], attS, True, False)], ["vt", "attS"], [pok])
                        corder = [0, 1, 2, 3] if d == 0 else [3, 2, 1, 0]
                        for ci, c in enumerate(corder):
                            gch = blk * 4 + c
                            csl = slice(gch * 32, gch * 32 + 32)
                            if not summary:
                                self.PE([(po[:, c * 32:(c + 1) * 32], Sbf[sidx % 2], Qi[:, csl], False, ci == 3)],
                                        [("Sb", sidx % 2), "Qi", pok], [pok])
                            pu, puk = self.ps()
                            self.PE([(pu[:, 0:128], Ktm[:, c, :], vt[:, blk, :], True, True)], [("Ktm", c), "vt"], [puk])
                            self.stt(Sf32, Sf32, Ach[:, gch:gch + 1], pu[:, 0:128], ALU.mult, ALU.add,
                                     reads=["S", "Ach", puk], writes=["S"])
                            sidx += 1
                            if not summary:
                                self.cp(Sbf[sidx % 2], Sf32, ["S"], [("Sb", sidx % 2)])
                        if not summary:
                            if d == 0:
                                self.cp(o32[:, bsl], po[:, 0:128], [pok], ["o32"])
                            else:
                                self.tt(o32[:, bsl], o32[:, bsl], po[:, 0:128], ALU.add, [pok, "o32"], ["o32"])
                    if mode == "P":
                        dst = (self.nsf_out if d == 0 else self.nsb_out)[si, l, hh]
                        self.DMA(dst, Sf32, reads=["S"], writes=["nso"], lane="out")
                    if summary:
                        dst = self.xin_v("Sf" if d == 0 else "Sb", 65536).rearrange("(h p c) -> h p c", h=4, p=128)[hh]
                        self.xput("S%d" % d, dst, Sf32, ["S"])
            if not summary:
                for tt_ in range(T // 512):
                    sl = slice(tt_ * 512, (tt_ + 1) * 512)
                    t, tk = self.tmp()
                    self.act(t[:], o32[:, sl], AF.Square, reads=["o32"], writes=[tk])
                    pm, pmk = self.ps()
                    self.PE([(pm[:], self.onesF[:], t[:], True, True)], [tk, "onesF"], [pmk])
                    r, rk = self.tmp()
                    self.act(r[:], pm[:], AF.Ln, bias=self.epsT[:, 0:1], scale=1.0 / 128, reads=[pmk, "epsT"], writes=[rk])
                    self.act(r[:], r[:], AF.Exp, scale=-0.5, reads=[rk], writes=[rk])
                    self.tt(r[:], r[:], o32[:, sl], ALU.mult, [rk, "o32"], [rk])
                    self.stt(od[:, hh, sl], r[:], prm[:, PRM["hgng"] + l:PRM["hgng"] + l + 1], gT[:, sl], ALU.mult, ALU.mult,
                             reads=[rk, "prm", "gT"], writes=["mo"])
            self.fw.barrier()
        if not summary:
            self.branch_out(l, 3, od, 4, self.W["w_hg_out"][l], T, first=False, h2off=4 * T)

    def wo_ffn(self, l, mode, T):
        kind = 0 if mode == "P" else 1
        NTT = T // 512
        g1 = self.modT[:, l, 32:48, kind]
        g2 = self.modT[:, l, 80:96, kind]
        for cb in range(8):
            w, wk = self.wload(self.W["w_o"][l], 0, 16, cb * 256, 256)
            for m in range(2):
                ch = cb * 2 + m
                for tt_ in range(NTT):
                    sl = slice(tt_ * 512, (tt_ + 1) * 512)
                    pt, pk = self.ps()
                    self.PE([(pt[:], w[:, kc, m * 128:(m + 1) * 128], self.mg[:, kc, sl], kc == 0, kc == 15)
                             for kc in range(16)], [wk] + [("mg", kc) for kc in range(16)], [pk])
                    self.stt(self.xT[:, ch, sl], pt[:], g1[:, ch:ch + 1], self.xT[:, ch, sl], ALU.mult, ALU.add,
                             reads=[pk, "modT", "x"], writes=["x"])
        self.fw.barrier()
        self.set_norm(l, 2, kind)
        self.norm_stats(T)
        ff, off = self.carve(0, 11 * T)
        ff = ff.rearrange("p (j t) -> p j t", j=11)
        tiles = self.h_tiles(T, off)
        for grp in range(4):
            j = 0
            while j < 11:
                nj = 2 if j + 1 < 11 else 1
                col = (grp * 11 + j) * 128
                w1, w1k = self.wload(self.W["ffn_w1"][l], 0, 16, col, 128 * nj)
                held = {}
                for m in range(nj):
                    for ti, (tix, hb, hk) in enumerate(tiles):
                        pt, pk = self.ps()
                        self.PE([(pt[:], w1[:, kc, m * 128:(m + 1) * 128], hb[:, kc, :], kc == 0, kc == 15)
                                 for kc in range(16)], [w1k] + hk, [pk])
                        t, tk = self.tmp()
                        self.act(t[:], pt[:], AF.Silu, reads=[pk], writes=[tk])
                        held[(m, ti)] = (t, tk)
                w3, w3k = self.wload(self.W["ffn_w3"][l], 0, 16, col, 128 * nj)
                for m in range(nj):
                    for ti, (tix, hb, hk) in enumerate(tiles):
                        pt, pk = self.ps()
                        self.PE([(pt[:], w3[:, kc, m * 128:(m + 1) * 128], hb[:, kc, :], kc == 0, kc == 15)
                                 for kc in range(16)], [w3k] + hk, [pk])
                        t, tk = held[(m, ti)]
                        self.tt(ff[:, j + m, tix * 512:(tix + 1) * 512], t[:], pt[:], ALU.mult, [tk, pk], [("ff", j + m)])
                j += nj
            for cb in range(8):
                w, wk = self.wload(self.W["ffn_w2"][l], grp * 11, 11, cb * 256, 256)
                for m in range(2):
                    ch = cb * 2 + m
                    for tt_ in range(NTT):
                        sl = slice(tt_ * 512, (tt_ + 1) * 512)
                        pt, pk = self.ps()
                        self.PE([(pt[:], w[:, jj, m * 128:(m + 1) * 128], ff[:, jj, sl], jj == 0, jj == 10)
                                 for jj in range(11)], [wk] + [("ff", jj) for jj in range(11)], [pk])
                        self.stt(self.xT[:, ch, sl], pt[:], g2[:, ch:ch + 1], self.xT[:, ch, sl], ALU.mult, ALU.add,
                                 reads=[pk, "modT", "x"], writes=["x"])


_CACHE = {}


def host_consts(core):
    j = core % 4
    ident = np.eye(128, dtype=np.float32)
    s_ = np.arange(128)[:, None]
    t_ = np.arange(128)[None, :]
    same = (s_ // 32) == (t_ // 32)
    hm = np.zeros((128, 2, 128), np.float32)
    hm[:, 0, :] = (same & (s_ <= t_)).astype(np.float32)
    hm[:, 1, :] = (same & (s_ >= t_)).astype(np.float32)
    sm = np.ones((128, 1024), np.float32)
    sm[:, ::32] = 0.0
    low = np.where(s_ >= t_, 0.0, NEG).astype(np.float32)
    up = np.where(s_ <= t_, 0.0, NEG).astype(np.float32)
    allneg = np.full((128, 128), NEG, np.float32)
    ab = np.zeros((128, 10, 128), np.float32)
    ab[:, 0, :] = low
    ab[:, 1, :] = up
    for r in range(4):
        ab[:, 2 + r, :] = low if r == j - 1 else allneg
        ab[:, 6 + r, :] = up if r == j + 1 else allneg
    tg = j * 1024 + np.arange(1024)
    row = (tg // 64).astype(np.float32)
    col = (tg % 64).astype(np.float32)
    inv = (np.float32(10000.0) ** (-np.arange(0, 64, 2, dtype=np.float32) / np.float32(64))).astype(np.float32)
    ang = np.stack([row[:, None] * inv, col[:, None] * inv], axis=1).astype(np.float32)
    cs = np.stack([np.cos(ang), np.sin(ang)], axis=1).reshape(1024, 2, 64)
    rope = np.ascontiguousarray(cs.reshape(8, 128, 2, 64).transpose(1, 0, 2, 3)).astype(np.float32)
    return ident, hm, sm, ab, rope


def make_in_maps(inp, pr):
    in_maps = []
    xp = inp["x_prompt"].astype(np.float32)
    xs = inp["x_sample"].astype(np.float32)
    for c in range(8):
        g, j = c // 4, c % 4
        ident, hm, sm, ab, rope = host_consts(c)
        m = {
            "xT_p": np.ascontiguousarray(xp[2 * c:2 * c + 2].reshape(512, D).T),
            "xT_s": np.ascontiguousarray(xs[g, j * 1024:(j + 1) * 1024].T),
            "ck": np.ascontiguousarray(inp["cache_k"][g].reshape(L, 512, 256)),
            "cv": np.ascontiguousarray(inp["cache_v"][g].reshape(L, 512, 256)),
            "s0f": np.ascontiguousarray(inp["state_hgrn_fwd"][g]),
            "s0b": np.ascontiguousarray(inp["state_hgrn_bwd"][g]),
            "cvec": np.ascontiguousarray(np.stack([fm(inp["c_ctx"], 16), fm(inp["c"][g], 16)], axis=-1)),
            "prm": pack_params(inp, c),
            "qkg": np.ascontiguousarray(np.broadcast_to(
                np.concatenate([inp["q_norm_g"], inp["k_norm_g"]], axis=1)[:, None, :], (L, 128, 256))).astype(np.float32),
            "ident": ident, "hmask": hm, "smask": sm, "abias": ab, "rope": rope,
        }
        for n in pr.W:
            m[n] = np.ascontiguousarray(inp[n][:pr.nl], dtype=np.float32)
        in_maps.append(m)
    return in_maps


def run_and_gather(nc, in_maps):
    res = run_bass_kernel_spmd(nc, in_maps, core_ids=list(range(8)))
    R = res.results
    yp = np.stack([R[c]["yT_p"].T.reshape(2, 256, D) for c in range(8)]).reshape(16, 256, D)
    ys = np.stack([np.concatenate([R[g * 4 + j]["yT_s"].T for j in range(4)], 0) for g in range(2)])
    nk = np.concatenate([R[c]["nk"] for c in range(8)], 0).reshape(16, L, 256, 2, 128)
    nv = np.concatenate([R[c]["nv"] for c in range(8)], 0).reshape(16, L, 256, 2, 128)
    nsf = np.concatenate([R[c]["nsf"] for c in range(8)], 0)
    nsb = np.concatenate([R[c]["nsb"] for c in range(8)], 0)
    return (yp.astype(np.float32), ys.astype(np.float32), nk.astype(np.float32), nv.astype(np.float32),
            nsf.astype(np.float32), nsb.astype(np.float32))


def kernel(**inp):
    inp = {k: np.asarray(v) for k, v in inp.items()}
    if "pr" not in _CACHE:
        pr = Prog(do_sample=True)
        _CACHE["pr"] = pr
        _CACHE["nc"] = pr.build()
    pr = _CACHE["pr"]
    return run_and_gather(_CACHE["nc"], make_in_maps(inp, pr))
```

```python
import numpy as np
from contextlib import ExitStack
import concourse.bass as bass
import concourse.mybir as mybir
from concourse.bass_utils import run_bass_kernel_spmd

F32 = mybir.dt.float32
BF16 = mybir.dt.bfloat16
AF = mybir.ActivationFunctionType
ALU = mybir.AluOpType
AX = mybir.AxisListType

D = 2048
KD = 16
L = 4
DFF = 5632
INC = 6656
EPS = 1e-6
SCALE = 128 ** -0.5
NEG = -30000.0
ARENA = 25600
CQ, CK, CV, CCA, CCG, CSB, CSC, CSH, CHQ, CHF, CHB, CHI, CHG = (
    0, 1024, 1280, 1536, 2048, 2560, 3072, 3584, 4096, 4608, 5120, 5632, 6144)


class Sem:
    def __init__(self, name):
        self.name = name
        self.h = None
        self.count = 0


class FW:
    ENGS = ("pe", "act", "dve", "pool", "sp")

    def __init__(self):
        self.sems = {}
        self.rec = {e: [] for e in self.ENGS}
        self.waited = {e: {} for e in self.ENGS}
        self.lastw = {}
        self.readers = {}
        for e in self.ENGS:
            self.sem(e)

    def sem(self, name):
        if name not in self.sems:
            self.sems[name] = Sem(name)
        return self.sems[name]

    def op(self, eng, fn, reads=(), writes=(), lane=None, linc=16):
        deps = {}

        def add(ev):
            if ev is not None and deps.get(ev[0], 0) < ev[1]:
                deps[ev[0]] = ev[1]

        for k in reads:
            add(self.lastw.get(k))
        for k in writes:
            add(self.lastw.get(k))
            for ev in self.readers.get(k, {}).items():
                add(ev)
        if eng == "pe":
            deps.pop("pe", None)
        waits = []
        wd = self.waited[eng]
        for s, v in deps.items():
            if wd.get(s, 0) < v:
                wd[s] = v
                waits.append((s, v))
        if lane is None:
            sm = self.sems[eng]
            sm.count += 1
            inc = 1
        else:
            sm = self.sem(lane)
            sm.count += linc
            inc = linc
        ev = (sm.name, sm.count)
        self.rec[eng].append((waits, fn, sm.name, inc))
        for k in writes:
            self.lastw[k] = ev
            self.readers[k] = {}
        for k in reads:
            r = self.readers.setdefault(k, {})
            if r.get(ev[0], 0) < ev[1]:
                r[ev[0]] = ev[1]
        return ev

    def barrier(self, engs=("pe", "act", "dve", "sp")):
        cur = [(s.name, s.count) for s in self.sems.values() if s.count > 0]
        for e in engs:
            waits = []
            wd = self.waited[e]
            for s, v in cur:
                if wd.get(s, 0) < v:
                    wd[s] = v
                    waits.append((s, v))
            if waits:
                self.rec[e].append((waits, None, None, 0))

    def emit(self, nc, stack):
        for s in self.sems.values():
            s.h = stack.enter_context(nc.semaphore(s.name))
        block = stack.enter_context(nc.Block())
        sems = self.sems

        def run(e, lst):
            for waits, fn, sname, inc in lst:
                for s, v in waits:
                    e.wait_ge(sems[s].h, v)
                if fn is not None:
                    fn(e).then_inc(sems[sname].h, inc)

        rec = self.rec

        @block.tensor
        def _(e):
            run(e, rec["pe"])

        @block.scalar
        def _(e):
            run(e, rec["act"])

        @block.vector
        def _(e):
            run(e, rec["dve"])

        @block.gpsimd
        def _(e):
            run(e, rec["pool"])

        @block.sync
        def _(e):
            run(e, rec["sp"])


PRM = {}
_off = 0
for _n, _w in (("n1g", L * 16), ("n2g", L * 16), ("adab", L * 96), ("bgate", L * 64), ("cfw", L * 4 * 31),
               ("cfb", L * 4), ("cflg", L * 4), ("cflb", L * 4), ("scw", L * 4 * 3), ("hglb", 2 * L * 4),
               ("hgng", L), ("sink", L * 8), ("sel", 32), ("cm", 4)):
    PRM[_n] = _off
    _off += _w
NP_ = _off

XR = {"A": 1280, "B": 1152}
XO = dict(kf=("A", 0), kl=("A", 32768), vf=("A", 65536), vl=("A", 98304), uL=("A", 131072), uR=("A", 139264),
          wL=("A", 147456), wR=("A", 147968), Sf=("B", 0), Sb=("B", 65536), Af=("B", 131072), Ab=("B", 131584))


def fm(v, nchunks):
    v = np.asarray(v, np.float32)
    lead = v.shape[:-1]
    v = v.reshape(lead + (nchunks, 128))
    return np.moveaxis(v, -1, 0)


def pack_params(inp, core):
    P = np.zeros((128, NP_), np.float32)

    def put(name, arr):
        a = np.ascontiguousarray(arr).reshape(128, -1)
        P[:, PRM[name]:PRM[name] + a.shape[1]] = a

    put("n1g", fm(inp["norm1_g"], 16))
    put("n2g", fm(inp["norm2_g"], 16))
    put("adab", fm(inp["ada_b"], 96))
    put("bgate", fm(inp["b_gate"], 64))
    put("cfw", np.transpose(fm(np.transpose(inp["conf_dw_w"], (0, 1, 2)), 4), (0, 1, 3, 2)))
    put("cfb", fm(inp["conf_dw_b"], 4))
    put("cflg", fm(inp["conf_ln_g"], 4))
    put("cflb", fm(inp["conf_ln_b"], 4))
    put("scw", np.transpose(fm(inp["sc_conv_w"], 4), (0, 1, 3, 2)))
    put("hglb", fm(inp["hg_lb"], 4))
    put("hgng", np.transpose(np.asarray(inp["hg_norm_g"], np.float32), (1, 0)))
    put("sink", np.broadcast_to(np.asarray(inp["attn_sink"], np.float32).reshape(1, L * 8), (128, L * 8)))
    j = core % 4
    sel = np.zeros(16, np.float32)
    for i in range(4):
        sel[i] = 1.0 if i == j - 1 else 0.0
        sel[4 + i] = 1.0 if i == j + 1 else 0.0
        sel[8 + i] = 1.0 if i < j else 0.0
        sel[12 + i] = 1.0 if i > j else 0.0
    sel = np.concatenate([sel, 1.0 - sel])
    put("sel", np.broadcast_to(sel.reshape(1, 32), (128, 32)))
    cm = np.zeros((128, 4), np.float32)
    for p in range(128):
        cm[p, p // 32] = 1.0
    put("cm", cm)
    return P


class Prog:
    def __init__(self, do_sample=True, nl=L):
        self.do_sample = do_sample
        self.nl = nl
        self.nc = bass.Bass("TRN2", target_bir_lowering=False)
        self.fw = FW()
        self.wi = 0
        self.psi = 0
        self.tmpi = 0
        self.xkeys = []

    def dram_in(self, name, shape):
        return self.nc.dram_tensor(name, list(shape), F32, kind="ExternalInput").ap()

    def dram_out(self, name, shape):
        return self.nc.dram_tensor(name, list(shape), F32, kind="ExternalOutput").ap()

    def V(self, fn, reads=(), writes=()):
        return self.fw.op("dve", fn, reads, writes)

    def A(self, fn, reads=(), writes=()):
        return self.fw.op("act", fn, reads, writes)

    def PE(self, mms, reads=(), writes=()):
        def fn(e, mms=mms):
            ins = None
            for (o, l, r, st, sp) in mms:
                ins = e.matmul(o, l, r, start=st, stop=sp)
            return ins
        return self.fw.op("pe", fn, reads, writes)

    def TR(self, out, in_, ident, reads=(), writes=()):
        return self.fw.op("pe", lambda e: e.transpose(out, in_, ident), reads, writes)

    def DMA(self, out, in_, reads=(), writes=(), lane="ld", eng="sp", slow=False):
        if slow:
            ev = self.fw.op(eng, lambda e: e.dma_start(out=out, in_=in_, allow_slow_non_contiguous=True), reads, writes,
                            lane=lane)
        else:
            ev = self.fw.op(eng, lambda e: e.dma_start(out=out, in_=in_), reads, writes, lane=lane)
        wd = self.fw.waited[eng]
        if wd.get(ev[0], 0) < ev[1]:
            wd[ev[0]] = ev[1]
            self.fw.rec[eng].append(([ev], None, None, 0))
        return ev

    def DMAc(self, out, in_, reads=(), writes=()):
        self.fw.barrier(("pool",))
        return self.DMA(out, in_, reads, writes, lane="pl", eng="pool")

    def act(self, out, in_, func, bias=None, scale=1.0, reads=(), writes=()):
        if bias is None:
            return self.A(lambda e: e.activation(out=out, in_=in_, func=func, scale=scale), reads, writes)
        return self.A(lambda e: e.activation(out=out, in_=in_, func=func, bias=bias, scale=scale), reads, writes)

    def tt(self, out, a, b, op, reads=(), writes=()):
        return self.V(lambda e: e.tensor_tensor(out=out, in0=a, in1=b, op=op), reads, writes)

    def ts(self, out, a, s1, s2, op0, op1=None, reads=(), writes=()):
        if op1 is None:
            return self.V(lambda e: e.tensor_scalar(out=out, in0=a, scalar1=s1, scalar2=None, op0=op0), reads, writes)
        return self.V(lambda e: e.tensor_scalar(out=out, in0=a, scalar1=s1, scalar2=s2, op0=op0, op1=op1), reads, writes)

    def stt(self, out, in0, scalar, in1, op0, op1, reads=(), writes=()):
        return self.V(lambda e: e.scalar_tensor_tensor(out=out, in0=in0, scalar=scalar, in1=in1, op0=op0, op1=op1),
                      reads, writes)

    def cp(self, out, in_, reads=(), writes=()):
        return self.V(lambda e: e.tensor_copy(out=out, in_=in_), reads, writes)

    def memset(self, ap, val, writes=()):
        return self.V(lambda e: e.memset(ap, val), (), writes)

    def ps(self):
        i = self.psi % 6
        self.psi += 1
        return self.psb[i], ("ps", i)

    def tmp(self):
        i = self.tmpi % 4
        self.tmpi += 1
        return self.tmps[i], ("tmp", i)

    def wload(self, W, k0, nk, c0, ncol):
        s = self.wi % 2
        self.wi += 1
        src = W[k0 * 128:(k0 + nk) * 128, c0:c0 + ncol].rearrange("(k p) n -> p k n", p=128)
        dst = self.wsl[s][:, 0:nk, 0:ncol]
        self.fw.op("pool", lambda e: e.dma_start(out=dst, in_=src), (), [("w", s)], lane="w%d" % s)
        return self.wsl[s], ("w", s)

    def xin_v(self, name, n):
        b, o = XO[name]
        flat = self.xin[b].rearrange("r c -> (r c)")
        return flat[o:o + n]

    def xout_v(self, name, n):
        b, o = XO[name]
        return self.xout[b].rearrange("(r q) c -> r (q c)", r=4)[:, o:o + n]

    def xput(self, name, dst, src, reads):
        key = ("xin", XO.get(name, XO.get({"A0": "Af", "A1": "Ab", "S0": "Sf", "S1": "Sb"}.get(name, name)))[0], name,
               len(self.xkeys))
        self.xkeys.append(key)
        self.DMA(dst, src, reads=reads, writes=[key], lane="xw")

    def build(self):
        nc = self.nc
        di = self.dram_in
        NL_ = self.nl
        self.xp_in = di("xT_p", (D, 512))
        self.xs_in = di("xT_s", (D, 1024))
        self.ck_in = di("ck", (L, 512, 256))
        self.cv_in = di("cv", (L, 512, 256))
        self.s0f_in = di("s0f", (L, 4, 128, 128))
        self.s0b_in = di("s0b", (L, 4, 128, 128))
        self.cvec_in = di("cvec", (128, 16, 2))
        self.prm_in = di("prm", (128, NP_))
        self.qkg_in = di("qkg", (L, 128, 256))
        self.ident_in = di("ident", (128, 128))
        self.hmask_in = di("hmask", (128, 2, 128))
        self.abias_in = di("abias", (128, 10, 128))
        self.rope_in = di("rope", (128, 8, 2, 64))
        self.smask_in = di("smask", (128, 1024))
        self.W = {}
        for n, shp in (("ada_w", (NL_, D, 6 * D)), ("w_in", (NL_, D, INC)), ("w_attn_out", (NL_, 1024, D)),
                       ("w_conf_out", (NL_, 512, D)), ("w_sc_out", (NL_, 512, D)), ("w_hg_out", (NL_, 512, D)),
                       ("w_gate", (NL_, D, 4 * D)), ("w_o", (NL_, D, D)), ("ffn_w1", (NL_, D, DFF)),
                       ("ffn_w3", (NL_, D, DFF)), ("ffn_w2", (NL_, DFF, D))):
            self.W[n] = di(n, shp)
        do = self.dram_out
        self.yp_out = do("yT_p", (D, 512))
        self.ys_out = do("yT_s", (D, 1024))
        self.nk_out = do("nk", (2, L, 256, 256))
        self.nv_out = do("nv", (2, L, 256, 256))
        self.nsf_out = do("nsf", (2, L, 4, 128, 128))
        self.nsb_out = do("nsb", (2, L, 4, 128, 128))
        self.xin = {b: nc.dram_tensor("xch_in" + b, [128, XR[b]], F32).ap() for b in XR}
        self.xout = {b: nc.dram_tensor("xch_out" + b, [512, XR[b]], F32).ap() for b in XR}

        with ExitStack() as st:
            sb = lambda n, s, d: st.enter_context(nc.sbuf_tensor(n, s, d))
            self.xT = sb("xT", [128, 16, 1024], F32)
            self.mg = sb("mg", [128, 16, 1024], BF16)
            self.hT = sb("hT", [128, 16, 512], BF16)
            self.wsl = [sb("wsl%d" % i, [128, 16, 256], BF16) for i in range(2)]
            self.rstd = sb("rstd", [128, 1024], F32)
            self.prm = sb("prm_s", [128, NP_], F32)
            self.modT = sb("modT", [128, L, 96, 2], F32)
            self.gsh = sb("gsh", [128, 2, 16], F32)
            self.identF = sb("identF", [128, 128], F32)
            self.identB = sb("identB", [128, 128], BF16)
            self.onesB = sb("onesB", [128, 128], BF16)
            self.onesF = sb("onesF", [128, 128], F32)
            self.epsT = sb("epsT", [128, 1], F32)
            self.lbv = sb("lbv", [128, 2, L, 4], F32)
            self.oml = sb("oml", [128, 2, L, 4], F32)
            self.csb = sb("csb", [128, 16, 2], BF16)
            self.tmps = [sb("tmp%d" % i, [128, 512], F32) for i in range(4)]
            self.hmask = sb("hmask_s", [128, 2, 128], F32)
            self.smask = sb("smask_s", [128, 1024], F32)
            self.arena = sb("arena", [128, ARENA], BF16)
            self.psb = [st.enter_context(nc.psum_tensor("ps%d" % i, [128, 512], F32)) for i in range(6)]
            self.psT = st.enter_context(nc.psum_tensor("psT", [128, 1024], BF16))
            self.psX = st.enter_context(nc.psum_tensor("psX", [128, 512], F32))
            self.program()
            self.fw.barrier(("sp",))
            self.fw.emit(nc, st)
        return nc

    def carve(self, off, n, dtype=BF16):
        if dtype == BF16:
            assert off + n <= ARENA, (off, n)
            return self.arena[:, off:off + n], off + n
        off += off % 2
        assert off + 2 * n <= ARENA, (off, n)
        return self.arena[:, off:off + 2 * n].bitcast(F32), off + 2 * n

    def program(self):
        fw = self.fw
        self.DMA(self.prm[:], self.prm_in, writes=["prm"])
        self.DMA(self.identF[:], self.ident_in, writes=["identF"])
        self.DMA(self.hmask[:], self.hmask_in, writes=["hmask"])
        self.DMA(self.smask[:], self.smask_in, writes=["smask"])
        self.cp(self.identB[:], self.identF[:], ["identF"], ["identB"])
        self.memset(self.onesB[:], 1.0, ["onesB"])
        self.memset(self.onesF[:], 1.0, ["onesF"])
        self.memset(self.epsT[:], EPS, ["epsT"])
        self.prologue()
        import os
        if "P" in os.environ.get("KPASS", "PS"):
            for l in range(self.nl):
                self.layer_pass(l, "P")
            self.DMA(self.yp_out.rearrange("(k p) t -> p k t", p=128), self.xT[:, :, 0:512], reads=["x"], writes=["yp"],
                     lane="out")
            fw.barrier()
        if self.do_sample:
            for l in range(self.nl):
                self.layer_pass(l, "S")
            self.DMA(self.ys_out.rearrange("(k p) t -> p k t", p=128), self.xT[:, :, 0:1024], reads=["x"],
                     writes=["ys"], lane="out")

    def prologue(self):
        prm = self.prm
        t, tk = self.tmp()
        cview = t[:, 0:32].rearrange("p (k t) -> p k t", t=2)
        self.DMA(cview, self.cvec_in, writes=[tk])
        self.act(self.csb[:], cview, AF.Silu, reads=[tk], writes=["csb"])
        lbr = prm[:, PRM["hglb"]:PRM["hglb"] + 32].rearrange("p (d l h) -> p d l h", d=2, l=L)
        e_, ek = self.tmp()
        ev = e_[:, 0:32].rearrange("p (d l h) -> p d l h", d=2, l=L)
        self.act(ev, lbr, AF.Exp, reads=["prm"], writes=[ek])
        s_, sk = self.tmp()
        sv = s_[:, 0:8].rearrange("p (d h) -> p d h", d=2)
        self.tt(sv, ev[:, :, 0, :], ev[:, :, 1, :], ALU.add, [ek], [sk])
        self.tt(sv, sv, ev[:, :, 2, :], ALU.add, [ek, sk], [sk])
        self.tt(sv, sv, ev[:, :, 3, :], ALU.add, [ek, sk], [sk])
        self.V(lambda e: e.reciprocal(out=sv, in_=sv), [sk], [sk])
        for l in range(L):
            self.tt(ev[:, :, l, :], ev[:, :, l, :], sv, ALU.mult, [ek, sk], [ek])
        self.memset(self.lbv[:, :, 0, :], 0.0, ["lbv"])
        for l in range(1, L):
            self.tt(self.lbv[:, :, l, :], self.lbv[:, :, l - 1, :], ev[:, :, l, :], ALU.add, [ek, "lbv"], ["lbv"])
        self.ts(self.lbv[:], self.lbv[:], 0.0, None, ALU.max, reads=["lbv"], writes=["lbv"])
        self.ts(self.oml[:], self.lbv[:], -1.0, 1.0, ALU.mult, ALU.add, reads=["lbv"], writes=["oml"])
        for l in range(self.nl):
            for cb in range(48):
                w, wk = self.wload(self.W["ada_w"][l], 0, 16, cb * 256, 256)
                for m in range(2):
                    ch = cb * 2 + m
                    pt, pk = self.ps()
                    mms = [(pt[:, 0:2], w[:, kc, m * 128:(m + 1) * 128], self.csb[:, kc, :], kc == 0, kc == 15)
                           for kc in range(16)]
                    self.PE(mms, [wk, "csb"], [pk])
                    b = prm[:, PRM["adab"] + l * 96 + ch:PRM["adab"] + l * 96 + ch + 1]
                    self.ts(self.modT[:, l, ch, :], pt[:, 0:2], b, None, ALU.add, reads=[pk, "prm"], writes=["modT"])
        for part in (1, 4):
            v = self.modT[:, 0:self.nl, part * 16:(part + 1) * 16, :]
            self.ts(v, v, 1.0, None, ALU.add, reads=["modT"], writes=["modT"])

    def set_norm(self, l, which, kind):
        prm = self.prm
        nm = "n1g" if which == 1 else "n2g"
        g = prm[:, PRM[nm] + l * 16:PRM[nm] + l * 16 + 16]
        base = 0 if which == 1 else 48
        shift = self.modT[:, l, base:base + 16, kind]
        scale = self.modT[:, l, base + 16:base + 32, kind]
        self.tt(self.gsh[:, 0, :], g, scale, ALU.mult, ["prm", "modT"], ["gsh"])
        self.cp(self.gsh[:, 1, :], shift, ["modT"], ["gsh"])

    def tile_order(self, T):
        n = T // 512
        if n == 2 and getattr(self, "hcache", None) == 1:
            return [1, 0]
        return list(range(n))

    def norm_stats(self, T):
        self.hcache = None
        for tt_ in range(T // 512):
            sl = slice(tt_ * 512, (tt_ + 1) * 512)
            for kc in range(16):
                self.act(self.hT[:, kc, :], self.xT[:, kc, sl], AF.Square, reads=["x"], writes=[("h", kc)])
            mms = [(self.psX[:], self.onesB[:], self.hT[:, kc, :], kc == 0, kc == 15) for kc in range(16)]
            self.PE(mms, [("h", kc) for kc in range(16)] + ["onesB"], ["psX"])
            self.act(self.rstd[:, sl], self.psX[:], AF.Ln, bias=self.epsT[:, 0:1], scale=1.0 / D,
                     reads=["psX", "epsT"], writes=["rstd"])
            self.act(self.rstd[:, sl], self.rstd[:, sl], AF.Exp, scale=-0.5, reads=["rstd"], writes=["rstd"])

    def make_h(self, tt_, buf=None):
        sl = slice(tt_ * 512, (tt_ + 1) * 512)
        if buf is None:
            if getattr(self, "hcache", None) == tt_:
                return [("h", kc) for kc in range(16)]
            self.hcache = tt_
        hb, kp = (self.hT, "h") if buf is None else (buf, "h2")
        for kc in range(16):
            t, tk = self.tmp()
            self.tt(t[:], self.xT[:, kc, sl], self.rstd[:, sl], ALU.mult, ["x", "rstd"], [tk])
            self.act(hb[:, kc, :], t[:], AF.Identity, bias=self.gsh[:, 1, kc:kc + 1], scale=self.gsh[:, 0, kc:kc + 1],
                     reads=[tk, "gsh"], writes=[(kp, kc)])
        return [(kp, kc) for kc in range(16)]

    def h_tiles(self, T, h2off):
        if T == 512:
            return [(0, self.hT, self.make_h(0))]
        t0 = 1 if getattr(self, "hcache", None) == 1 else 0
        b, _ = self.carve(h2off, 16 * 512)
        b = b.rearrange("p (k t) -> p k t", k=16)
        return [(t0, self.hT, self.make_h(t0)), (1 - t0, b, self.make_h(1 - t0, b))]

    def proj_fm(self, W, c0, ncol, rhs_list, rkeys, consume):
        nk = len(rhs_list)
        nblk = (ncol + 255) // 256
        n = rhs_list[0].shape[1]
        for b in range(nblk):
            bc = min(256, ncol - b * 256)
            w, wk = self.wload(W, 0, nk, c0 + b * 256, bc)
            for m in range(bc // 128):
                pt, pk = self.ps()
                mms = [(pt[:, 0:n], w[:, k, m * 128:(m + 1) * 128], rhs_list[k], k == 0, k == nk - 1) for k in range(nk)]
                self.PE(mms, [wk] + list(rkeys), [pk])
                consume(b * 2 + m, pt, pk)

    def layer_pass(self, l, mode):
        import os
        fw = self.fw
        T = 512 if mode == "P" else 1024
        kind = 0 if mode == "P" else 1
        if l == 0:
            src = self.xp_in if mode == "P" else self.xs_in
            self.DMA(self.xT[:, :, 0:T], src.rearrange("(k p) t -> p k t", p=128), writes=["x"])
        self.set_norm(l, 1, kind)
        self.norm_stats(T)
        fw.barrier()
        mix = os.environ.get("KMIX", "BCADF")
        if mode == "S":
            self.pre_phase(l, T)
            fw.barrier()
        if "B" in mix:
            self.mixer_B(l, mode, T)
            fw.barrier()
        if "C" in mix:
            self.mixer_C(l, mode, T)
            fw.barrier()
        if "A" in mix:
            self.mixer_A(l, mode, T)
            fw.barrier()
        if "D" in mix:
            self.mixer_D(l, mode, T)
            fw.barrier()
        if "F" in mix:
            self.wo_ffn(l, mode, T)
            fw.barrier()

    def pre_phase(self, l, T):
        self.xkeys = []
        Win = self.W["w_in"][l]
        off = 0
        ropeT, off = self.carve(off, 8 * 128, F32)
        self.ropeT = ropeT.rearrange("p (b c f) -> p b c f", b=8, c=2)
        self.DMA(self.ropeT, self.rope_in, writes=["ropeT"])
        qkg, off = self.carve(off, 256, F32)
        self.DMA(qkg, self.qkg_in[l], writes=["qkg"])
        sq, off = self.carve(off, 256, F32)
        qn, off = self.carve(off, 256, F32)
        st2, off = self.carve(off, 8, F32)
        rt, off = self.carve(off, 512, F32)
        u16, off = self.carve(off, 16, F32)
        for tt_, tb, nk_, nv_, nu, nw, c16, wcol in ((0, 0, "kf", "vf", "uL", "wL", slice(0, 16), 0),
                                                     (1, 3, "kl", "vl", "uR", "wR", slice(496, 512), 15)):
            hk = self.make_h(tt_)
            blk = tt_ * 4 + tb
            w, wk = self.wload(Win, 0, 16, CK, 256)
            pt, pk = self.ps()
            self.PE([(pt[:, 0:256], self.hT[:, kc, tb * 128:(tb + 1) * 128], w[:, kc, 0:256], kc == 0, kc == 15)
                     for kc in range(16)], [wk] + hk, [pk])
            self.qk_norm(pt, pk, sq, st2, qn, qkg[:, 128:256])
            self.rope(qn, blk, rt)
            self.xput(nk_, self.xin_v(nk_, 32768).rearrange("(t c) -> t c", c=256), qn, ["qn"])
            w, wk = self.wload(Win, 0, 16, CV, 256)
            pt, pk = self.ps()
            self.PE([(pt[:, 0:256], self.hT[:, kc, tb * 128:(tb + 1) * 128], w[:, kc, 0:256], kc == 0, kc == 15)
                     for kc in range(16)], [wk] + hk, [pk])
            self.cp(qn, pt[:, 0:256], [pk], ["qn"])
            self.xput(nv_, self.xin_v(nv_, 32768).rearrange("(t c) -> t c", c=256), qn, ["qn"])
            hr = [self.hT[:, kc, c16] for kc in range(16)]
            for ch in range(4):
                hold = {}

                def cg(m, pt, pk, hold=hold):
                    t, tk = self.tmp()
                    self.act(t[:, 0:16], pt[:, 0:16], AF.Sigmoid, reads=[pk], writes=[tk])
                    hold["g"] = (t, tk)

                self.proj_fm(Win, CCG + ch * 128, 128, hr, hk, cg)

                def ca(m, pt, pk, hold=hold, ch=ch, nu=nu):
                    t, tk = hold["g"]
                    self.tt(u16, pt[:, 0:16], t[:, 0:16], ALU.mult, [pk, tk], ["u16"])
                    dst = self.xin_v(nu, 8192).rearrange("(c p w) -> c p w", c=4, p=128)[ch]
                    self.xput(nu, dst, u16, ["u16"])

                self.proj_fm(Win, CCA + ch * 128, 128, hr, hk, ca)

                def cc(m, pt, pk, hold=hold):
                    t, tk = self.tmp()
                    self.cp(t[:, 0:16], pt[:, 0:16], [pk], [tk])
                    hold["c"] = (t, tk)

                self.proj_fm(Win, CSC + ch * 128, 128, hr, hk, cc)

                def chh(m, pt, pk, hold=hold, ch=ch, nw=nw, wcol=wcol):
                    t, tk = hold["c"]
                    self.tt(u16, pt[:, 0:16], t[:, 0:16], ALU.mult, [pk, tk], ["u16"])
                    dst = self.xin_v(nw, 512).rearrange("(c p o) -> c p o", c=4, o=1)[ch]
                    self.xput(nw, dst, u16[:, wcol:wcol + 1], ["u16"])

                self.proj_fm(Win, CSH + ch * 128, 128, hr, hk, chh)
        self.fw.barrier()
        self.mixer_D(l, "S", T, summary=True)
        self.fw.barrier()
        grp = [[0, 1, 2, 3], [4, 5, 6, 7]]
        ka = [k for k in self.xkeys if k[1] == "A"]
        kb = [k for k in self.xkeys if k[1] == "B"]
        self.fw.op("pool", lambda e: e.collective_compute("AllGather", ALU.bypass, replica_groups=grp,
                                                          ins=[self.xin["A"]], outs=[self.xout["A"]]),
                   reads=ka, writes=["xoutA"], lane="cc", linc=1)
        self.fw.op("pool", lambda e: e.collective_compute("AllGather", ALU.bypass, replica_groups=grp,
                                                          ins=[self.xin["B"]], outs=[self.xout["B"]]),
                   reads=kb + ["xoutA"], writes=["xout", "xoutA"], lane="cc", linc=1)

    def qk_norm(self, pt, pk, sq, st2, qn, gain):
        self.act(sq, pt[:, 0:256], AF.Square, reads=[pk], writes=["sq"])
        self.V(lambda e: e.reduce_sum(out=st2[:, 0:2], in_=sq.rearrange("p (h d) -> p h d", h=2), axis=AX.X),
               ["sq"], ["st2"])
        self.act(st2[:, 2:4], st2[:, 0:2], AF.Ln, bias=self.epsT[:, 0:1], scale=1.0 / 128,
                 reads=["st2", "epsT"], writes=["st2b"])
        self.act(st2[:, 4:6], st2[:, 2:4], AF.Exp, scale=-0.5, reads=["st2b"], writes=["st2c"])
        for hd in range(2):
            self.stt(qn[:, hd * 128:(hd + 1) * 128], pt[:, hd * 128:(hd + 1) * 128], st2[:, 4 + hd:5 + hd],
                     gain, ALU.mult, ALU.mult, reads=[pk, "st2c", "qkg"], writes=["qn"])

    def rope(self, qn, blk, rt):
        v = qn.rearrange("p (h a f r) -> p h a f r", h=2, a=2, f=2)
        a1 = v[:, :, :, 0, :]
        a2 = v[:, :, :, 1, :]
        cos = self.ropeT[:, blk, 0, :].rearrange("p (a r) -> p a r", a=2).unsqueeze(1).to_broadcast([128, 2, 2, 32])
        sin = self.ropeT[:, blk, 1, :].rearrange("p (a r) -> p a r", a=2).unsqueeze(1).to_broadcast([128, 2, 2, 32])
        t = [rt[:, i * 128:(i + 1) * 128].rearrange("p (h a r) -> p h a r", h=2, a=2) for i in range(4)]
        self.tt(t[0], a1, cos, ALU.mult, ["qn", "ropeT"], ["rt0"])
        self.tt(t[1], a2, sin, ALU.mult, ["qn", "ropeT"], ["rt1"])
        self.tt(t[2], a2, cos, ALU.mult, ["qn", "ropeT"], ["rt2"])
        self.tt(t[3], a1, sin, ALU.mult, ["qn", "ropeT"], ["rt3"])
        self.tt(a1, t[0], t[1], ALU.subtract, ["rt0", "rt1"], ["qn"])
        self.tt(a2, t[2], t[3], ALU.add, ["rt2", "rt3", "qn"], ["qn"])

    def branch_out(self, l, n, moT, nk, Wout, T, first, h2off):
        prm = self.prm
        tiles = self.h_tiles(T, h2off)
        for cb in range(8):
            wg, wgk = self.wload(self.W["w_gate"][l], 0, 16, n * D + cb * 256, 256)
            pgs = {}
            for m in range(2):
                for ti, (tix, hb, hk) in enumerate(tiles):
                    pg, pgk = self.ps() if len(tiles) == 1 else (self.psb[m * 2 + ti], ("ps", m * 2 + ti))
                    self.PE([(pg[:], wg[:, kc, m * 128:(m + 1) * 128], hb[:, kc, :], kc == 0, kc == 15)
                             for kc in range(16)], [wgk] + hk, [pgk])
                    pgs[(m, ti)] = (pg, pgk)
            wo, wok = self.wload(Wout, 0, nk, cb * 256, 256)
            for m in range(2):
                ch = cb * 2 + m
                for ti in range(len(tiles)):
                    sl = slice(tiles[ti][0] * 512, (tiles[ti][0] + 1) * 512)
                    pg, pgk = pgs[(m, ti)]
                    py, pyk = self.ps() if len(tiles) == 1 else (self.psb[4 + ti], ("ps", 4 + ti))
                    self.PE([(py[:], wo[:, k, m * 128:(m + 1) * 128], moT[:, k, sl], k == 0, k == nk - 1)
                             for k in range(nk)], [wok, "mo"], [pyk])
                    t, tk = self.tmp()
                    bcol = PRM["bgate"] + l * 64 + n * 16 + ch
                    self.act(t[:], pg[:], AF.Sigmoid, bias=prm[:, bcol:bcol + 1], reads=[pgk, "prm"], writes=[tk])
                    if first:
                        self.tt(self.mg[:, ch, sl], t[:], py[:], ALU.mult, [tk, pyk], [("mg", ch)])
                    else:
                        self.tt(t[:], t[:], py[:], ALU.mult, [tk, pyk], [tk])
                        self.tt(self.mg[:, ch, sl], self.mg[:, ch, sl], t[:], ALU.add, [tk, ("mg", ch)], [("mg", ch)])

    def segs(self, mode):
        return [(0, 256), (256, 256)] if mode == "P" else [(0, 1024)]

    def mixer_B(self, l, mode, T):
        prm = self.prm
        segs = self.segs(mode)
        HAL = 15
        TP = T + 2 * HAL * len(segs)
        off = 0
        u, off = self.carve(off, 4 * TP)
        u = u.rearrange("p (c t) -> p c t", c=4)
        acc, off = self.carve(off, 4 * T, F32)
        acc = acc.rearrange("p (c t) -> p c t", c=4)
        yb, off = self.carve(off, 4 * T)
        yb = yb.rearrange("p (c t) -> p c t", c=4)
        self.memset(u, 0.0, ["u"])

        def upos(t0):
            for si, (s0, ln) in enumerate(segs):
                if s0 <= t0 < s0 + ln:
                    return si * (ln + 2 * HAL) + HAL + (t0 - s0)

        Win = self.W["w_in"][l]
        if mode == "S":
            cand, off = self.carve(off, 4 * 4 * 16, F32)
        tiles = self.h_tiles(T, off)

        def pieces(tt_):
            for (s0, ln) in segs:
                a = max(s0, tt_ * 512)
                b = min(s0 + ln, (tt_ + 1) * 512)
                if a < b:
                    yield upos(a), a - tt_ * 512, b - tt_ * 512

        for cp_ in range(2):
            wg, wgk = self.wload(Win, 0, 16, CCG + cp_ * 256, 256)
            held = {}
            for m in range(2):
                for (tt_, hb, hk) in tiles:
                    pt, pk = self.ps()
                    self.PE([(pt[:], wg[:, kc, m * 128:(m + 1) * 128], hb[:, kc, :], kc == 0, kc == 15)
                             for kc in range(16)], [wgk] + hk, [pk])
                    t, tk = self.tmp()
                    self.act(t[:], pt[:], AF.Sigmoid, reads=[pk], writes=[tk])
                    held[(m, tt_)] = (t, tk)
            wa, wak = self.wload(Win, 0, 16, CCA + cp_ * 256, 256)
            for m in range(2):
                ch = cp_ * 2 + m
                for (tt_, hb, hk) in tiles:
                    pt, pk = self.ps()
                    self.PE([(pt[:], wa[:, kc, m * 128:(m + 1) * 128], hb[:, kc, :], kc == 0, kc == 15)
                             for kc in range(16)], [wak] + hk, [pk])
                    t, tk = held[(m, tt_)]
                    for p0, a, b in pieces(tt_):
                        self.tt(u[:, ch, p0:p0 + (b - a)], pt[:, a:b], t[:, a:b], ALU.mult, [pk, tk], ["u"])
        if mode == "S":
            cand = cand.rearrange("p (c r w) -> p c r w", c=4, r=4)
            for side, nm, selb, lo, pos in ((0, "uR", 0, 1, 0), (1, "uL", 4, 0, HAL + T)):
                src = self.xout_v(nm, 8192).rearrange("r (c p w) -> c p r w", c=4, p=128)
                for ch in range(4):
                    self.DMA(cand[:, ch, :, :], src[ch], reads=["xout"], writes=["cand"])
                for ch in range(4):
                    dst = u[:, ch, pos:pos + HAL]
                    for r in range(4):
                        sc_ = prm[:, PRM["sel"] + selb + r:PRM["sel"] + selb + r + 1]
                        if r == 0:
                            self.ts(dst, cand[:, ch, r, lo:lo + HAL], sc_, None, ALU.mult, reads=["cand", "prm", "u"],
                                    writes=["u"])
                        else:
                            self.stt(dst, cand[:, ch, r, lo:lo + HAL], sc_, dst, ALU.mult, ALU.add,
                                     reads=["cand", "prm", "u"], writes=["u"])
        for ch in range(4):
            wb = PRM["cfw"] + (l * 4 + ch) * 31
            bb = PRM["cfb"] + l * 4 + ch
            for (s0, ln) in segs:
                p0 = upos(s0) - HAL
                o = acc[:, ch, s0:s0 + ln]
                self.ts(o, u[:, ch, p0:p0 + ln], prm[:, wb:wb + 1], prm[:, bb:bb + 1], ALU.mult, ALU.add,
                        reads=["u", "prm"], writes=[("acc", ch)])
                for k in range(1, 31):
                    self.stt(o, u[:, ch, p0 + k:p0 + k + ln], prm[:, wb + k:wb + k + 1], o, ALU.mult, ALU.add,
                             reads=["u", "prm", ("acc", ch)], writes=[("acc", ch)])
        for tt_ in range(T // 512):
            sl = slice(tt_ * 512, (tt_ + 1) * 512)
            pm, pmk = self.ps()
            self.PE([(pm[:], self.onesF[:], acc[:, ch, sl], ch == 0, ch == 3) for ch in range(4)],
                    [("acc", ch) for ch in range(4)] + ["onesF"], [pmk])
            mean, mk = self.tmp()
            self.ts(mean[:], pm[:], 1.0 / 512, None, ALU.mult, reads=[pmk], writes=[mk])
            for ch in range(4):
                self.tt(acc[:, ch, sl], acc[:, ch, sl], mean[:], ALU.subtract, [("acc", ch), mk], [("acc", ch)])
            pv, pvk = self.ps()
            for ch in range(4):
                t, tk = self.tmp()
                self.act(t[:], acc[:, ch, sl], AF.Square, reads=[("acc", ch)], writes=[tk])
                self.PE([(pv[:], self.onesF[:], t[:], ch == 0, ch == 3)], [tk, "onesF"] + ([pvk] if ch else []), [pvk])
            rs, rk = self.tmp()
            self.act(rs[:], pv[:], AF.Ln, bias=self.epsT[:, 0:1], scale=1.0 / 512, reads=[pvk, "epsT"], writes=[rk])
            self.act(rs[:], rs[:], AF.Exp, scale=-0.5, reads=[rk], writes=[rk])
            for ch in range(4):
                self.tt(acc[:, ch, sl], acc[:, ch, sl], rs[:], ALU.mult, [("acc", ch), rk], [("acc", ch)])
                gcol = PRM["cflg"] + l * 4 + ch
                bcol = PRM["cflb"] + l * 4 + ch
                self.act(yb[:, ch, sl], acc[:, ch, sl], AF.Silu, bias=prm[:, bcol:bcol + 1],
                         scale=prm[:, gcol:gcol + 1], reads=[("acc", ch), "prm"], writes=["mo"])
        self.fw.barrier()
        self.branch_out(l, 1, yb, 4, self.W["w_conf_out"][l], T, first=True, h2off=off)

    def mixer_C(self, l, mode, T):
        prm = self.prm
        segs = self.segs(mode)
        TP = T + 2 * len(segs)
        off = 0
        w_, off = self.carve(off, 4 * TP)
        w_ = w_.rearrange("p (c t) -> p c t", c=4)
        sbb, off = self.carve(off, 4 * T)
        sbb = sbb.rearrange("p (c t) -> p c t", c=4)
        yc, off = self.carve(off, 4 * T)
        yc = yc.rearrange("p (c t) -> p c t", c=4)
        acc, off = self.carve(off, T, F32)
        self.memset(w_, 0.0, ["u"])

        def upos(t0):
            for si, (s0, ln) in enumerate(segs):
                if s0 <= t0 < s0 + ln:
                    return si * (ln + 2) + 1 + (t0 - s0)

        Win = self.W["w_in"][l]
        if mode == "S":
            cand, off = self.carve(off, 16, F32)
        tiles = self.h_tiles(T, off)

        def pieces(tt_):
            for (s0, ln) in segs:
                a = max(s0, tt_ * 512)
                b = min(s0 + ln, (tt_ + 1) * 512)
                if a < b:
                    yield upos(a), a - tt_ * 512, b - tt_ * 512

        for cp_ in range(2):
            held = {}
            for sec, cbase in (("b", CSB), ("c", CSC), ("h", CSH)):
                w, wk = self.wload(Win, 0, 16, cbase + cp_ * 256, 256)
                for m in range(2):
                    ch = cp_ * 2 + m
                    for (tt_, hb, hk) in tiles:
                        sl = slice(tt_ * 512, (tt_ + 1) * 512)
                        pt, pk = self.ps()
                        self.PE([(pt[:], w[:, kc, m * 128:(m + 1) * 128], hb[:, kc, :], kc == 0, kc == 15)
                                 for kc in range(16)], [wk] + hk, [pk])
                        if sec == "b":
                            self.cp(sbb[:, ch, sl], pt[:], [pk], ["sbb"])
                        elif sec == "c":
                            t, tk = self.tmp()
                            self.cp(t[:], pt[:], [pk], [tk])
                            held[(m, tt_)] = (t, tk)
                        else:
                            t, tk = held[(m, tt_)]
                            for p0, a, b in pieces(tt_):
                                self.tt(w_[:, ch, p0:p0 + (b - a)], pt[:, a:b], t[:, a:b], ALU.mult, [pk, tk], ["u"])
        if mode == "S":
            cand = cand.rearrange("p (c r w) -> p c r w", c=4, r=4)
            for side, nm, selb, pos in ((0, "wR", 0, 0), (1, "wL", 4, 1 + T)):
                src = self.xout_v(nm, 512).rearrange("r (c p w) -> c p r w", c=4, p=128)
                for ch in range(4):
                    self.DMA(cand[:, ch, :, :], src[ch], reads=["xout"], writes=["cand"], slow=True)
                for ch in range(4):
                    dst = w_[:, ch, pos:pos + 1]
                    for r in range(4):
                        sc_ = prm[:, PRM["sel"] + selb + r:PRM["sel"] + selb + r + 1]
                        if r == 0:
                            self.ts(dst, cand[:, ch, r, :], sc_, None, ALU.mult, reads=["cand", "prm", "u"], writes=["u"])
                        else:
                            self.stt(dst, cand[:, ch, r, :], sc_, dst, ALU.mult, ALU.add, reads=["cand", "prm", "u"],
                                     writes=["u"])
        for ch in range(4):
            wb = PRM["scw"] + (l * 4 + ch) * 3
            for (s0, ln) in segs:
                p0 = upos(s0) - 1
                o = acc[:, s0:s0 + ln]
                self.ts(o, w_[:, ch, p0:p0 + ln], prm[:, wb:wb + 1], None, ALU.mult, reads=["u", "prm"], writes=["acc"])
                for k in (1, 2):
                    self.stt(o, w_[:, ch, p0 + k:p0 + k + ln], prm[:, wb + k:wb + k + 1], o, ALU.mult, ALU.add,
                             reads=["u", "prm", "acc"], writes=["acc"])
            self.tt(yc[:, ch, 0:T], acc[:, 0:T], sbb[:, ch, 0:T], ALU.mult, ["acc", "sbb"], ["mo"])
        self.fw.barrier()
        self.branch_out(l, 2, yc, 4, self.W["w_sc_out"][l], T, first=False, h2off=off)

    def mixer_A(self, l, mode, T):
        prm = self.prm
        segs = self.segs(mode)
        NB = T // 128
        off = 0
        qT, off = self.carve(off, 8 * T)
        qT = qT.rearrange("p (h t) -> p h t", h=8)
        kT, off = self.carve(off, 2 * T)
        kT = kT.rearrange("p (h t) -> p h t", h=2)
        vtm, off = self.carve(off, NB * 256)
        vtm = vtm.rearrange("p (b c) -> p b c", c=256)
        base = off
        qkg, off = self.carve(off, 256, F32)
        sq, off = self.carve(off, 256, F32)
        qn, off = self.carve(off, 256, F32)
        qnb, off = self.carve(off, 256)
        st2, off = self.carve(off, 8, F32)
        if mode == "S":
            ropeT, off = self.carve(off, 8 * 128, F32)
            self.ropeT = ropeT.rearrange("p (b c f) -> p b c f", b=8, c=2)
            self.DMA(self.ropeT, self.rope_in, writes=["ropeT"])
            rt, off = self.carve(off, 512, F32)
        self.DMA(qkg, self.qkg_in[l], writes=["qkg"])
        Win = self.W["w_in"][l]
        tiles = self.h_tiles(T, off)
        for wb in range(6):
            w, wk = self.wload(Win, 0, 16, wb * 256, 256)
            for (tt_, hb, hk) in tiles:
                for tb in range(4):
                    blk = tt_ * 4 + tb
                    tsl = slice(blk * 128, (blk + 1) * 128)
                    pt, pk = self.ps()
                    self.PE([(pt[:, 0:256], hb[:, kc, tb * 128:(tb + 1) * 128], w[:, kc, 0:256], kc == 0, kc == 15)
                             for kc in range(16)], [wk] + hk, [pk])
                    if wb == 5:
                        self.cp(vtm[:, blk, :], pt[:, 0:256], [pk], [("v", blk)])
                        if mode == "P":
                            self.cp(qn, pt[:, 0:256], [pk], ["qn"])
                            seq, r0 = blk // 2, (blk % 2) * 128
                            self.DMA(self.nv_out[seq, l, r0:r0 + 128, :], qn, reads=["qn"], writes=["nvo"], lane="out")
                        continue
                    self.qk_norm(pt, pk, sq, st2, qn, qkg[:, 0:128] if wb < 4 else qkg[:, 128:256])
                    if mode == "S":
                        self.rope(qn, blk, rt)
                    if wb == 4 and mode == "P":
                        seq, r0 = blk // 2, (blk % 2) * 128
                        self.DMA(self.nk_out[seq, l, r0:r0 + 128, :], qn, reads=["qn"], writes=["nko"], lane="out")
                    self.cp(qnb, qn, ["qn"], ["qnb"])
                    for hd in range(2):
                        self.TR(self.psT[:, hd * 128:(hd + 1) * 128], qnb[:, hd * 128:(hd + 1) * 128], self.identB[:],
                                ["qnb", "identB"], ["psT"])
                    if wb < 4:
                        dst = qT[:, wb * 2:wb * 2 + 2, tsl]
                        key = ("qT", wb // 2, blk)
                    else:
                        dst = kT[:, :, tsl]
                        key = ("kT", blk)
                    self.cp(dst, self.psT[:, 0:256].rearrange("p (h t) -> p h t", h=2), ["psT"], [key])
        self.fw.barrier()
        off = base
        pT = []
        for i in range(2):
            a, off = self.carve(off, 512)
            pT.append(a)
        sinkx, off = self.carve(off, 1024, F32)
        den, off = self.carve(off, 512, F32)
        s8, off = self.carve(off, 8, F32)
        sx = prm[:, PRM["sink"] + l * 8:PRM["sink"] + l * 8 + 8]
        self.act(s8, sx, AF.Exp, reads=["prm"], writes=["s8"])
        for h in range(8):
            self.cp(sinkx[:, h * 128:(h + 1) * 128], s8[:, h:h + 1].to_broadcast([128, 128]), ["s8"], ["sinkx"])
        if mode == "S":
            ctmp, off = self.carve(off, 4 * 256)
            ctmp = ctmp.rearrange("p (r c) -> p r c", r=4)
            ctxK, off = self.carve(off, 2 * 512)
            ctxK = ctxK.rearrange("p (h s) -> p h s", h=2)
            ctxV, off = self.carve(off, 4 * 256)
            ctxV = ctxV.rearrange("p (b c) -> p b c", b=4)
            candK, off = self.carve(off, 4 * 2 * 128)
            candK = candK.rearrange("p (r h s) -> p r h s", r=4, h=2)
            candV, off = self.carve(off, 4 * 256)
            candV = candV.rearrange("p (r c) -> p r c", r=4)
            biasB, off = self.carve(off, 10 * 128)
            biasB = biasB.rearrange("p (i t) -> p i t", i=10)
            self.DMAc(biasB, self.abias_in, writes=["biasB"])
            self.DMAc(ctxV, self.cv_in[l].rearrange("(b p) c -> p b c", p=128), writes=["ctx"])
            self.DMAc(ctmp, self.ck_in[l].rearrange("(b p) c -> p b c", p=128), writes=["ctmp"])
            for b in range(4):
                for h in range(2):
                    self.TR(self.psT[:, 0:128], ctmp[:, b, h * 128:(h + 1) * 128], self.identB[:], ["ctmp", "identB"], ["psT"])
                    self.cp(ctxK[:, h, b * 128:(b + 1) * 128], self.psT[:, 0:128], ["psT"], ["ctx"])

            def load_cands(nk_, nv_):
                self.DMAc(candV, self.xout_v(nv_, 32768).rearrange("r (t c) -> t r c", c=256), reads=["xout"],
                          writes=["cand"])
                self.DMAc(ctmp, self.xout_v(nk_, 32768).rearrange("r (t c) -> t r c", c=256), reads=["xout"],
                          writes=["ctmp"])
                for r in range(4):
                    for h in range(2):
                        self.TR(self.psT[:, 0:128], ctmp[:, r, h * 128:(h + 1) * 128], self.identB[:], ["ctmp", "identB"],
                                ["psT"])
                        self.cp(candK[:, r, h, :], self.psT[:, 0:128], ["psT"], ["cand"])
        pi = 0
        for (s0, ln) in segs:
            b0, nb = s0 // 128, ln // 128
            for qb in range(b0, b0 + nb):
                qsl = slice(qb * 128, (qb + 1) * 128)
                if mode == "S" and qb == 0:
                    load_cands("kl", "vl")
                if mode == "S" and qb == NB - 1:
                    load_cands("kf", "vf")
                for kvh in range(2):
                    hs = slice(kvh * 128, (kvh + 1) * 128)

                    def loc(kb, bias):
                        return (kT[:, kvh, kb * 128:(kb + 1) * 128], vtm[:, kb, hs], [("kT", kb), ("v", kb)], bias)

                    if mode == "P":
                        kbs = [loc(kb, None) for kb in range(b0, b0 + nb)]
                    else:
                        kbs = []
                        if qb == 0:
                            kbs += [(candK[:, r, kvh, :], candV[:, r, hs], ["cand"], 2 + r) for r in range(4)]
                        else:
                            kbs.append(loc(qb - 1, 0))
                        kbs.append(loc(qb, None))
                        if qb == NB - 1:
                            kbs += [(candK[:, r, kvh, :], candV[:, r, hs], ["cand"], 6 + r) for r in range(4)]
                        else:
                            kbs.append(loc(qb + 1, 1))
                        kbs += [(ctxK[:, kvh, cb * 128:(cb + 1) * 128], ctxV[:, cb, hs], ["ctx"], None) for cb in range(4)]
                    qrhs = qT[:, kvh * 4:(kvh + 1) * 4, qsl]
                    ppv, ppvk = self.psb[0], ("ps", 0)
                    pden, pdenk = self.psb[1], ("ps", 1)
                    nkb = len(kbs)
                    for i, (kap, vap, keys, bias) in enumerate(kbs):
                        psc, psck = self.psb[2 + pi % 4], ("ps", 2 + pi % 4)
                        mms = [(psc[:], kap, qrhs, True, bias is None)]
                        rk = list(keys) + [("qT", kvh, qb)]
                        if bias is not None:
                            mms += [(psc[:, g * 128:(g + 1) * 128], self.identB[:], biasB[:, bias, :], False, g == 3)
                                    for g in range(4)]
                            rk += ["identB", "biasB"]
                        self.PE(mms, rk, [psck])
                        p = pT[pi % 2]
                        pkk = ("pT", pi % 2)
                        pi += 1
                        self.act(p, psc[:], AF.Exp, scale=SCALE, reads=[psck], writes=[pkk])
                        self.PE([(ppv[:], vap, p, i == 0, i == nkb - 1)], list(keys) + [pkk] + ([ppvk] if i else []), [ppvk])
                        self.PE([(pden[:], self.onesB[:], p, i == 0, i == nkb - 1)], ["onesB", pkk] + ([pdenk] if i else []),
                                [pdenk])
                    self.tt(den, pden[:], sinkx[:, kvh * 512:(kvh + 1) * 512], ALU.add, [pdenk, "sinkx"], ["den"])
                    self.V(lambda e: e.reciprocal(out=den, in_=den), ["den"], ["den"])
                    self.tt(qrhs, ppv[:].rearrange("p (h t) -> p h t", h=4), den.rearrange("p (h t) -> p h t", h=4),
                            ALU.mult, [ppvk, "den"], [("qT", kvh, qb), "mo"])
        self.fw.barrier()
        self.branch_out(l, 0, qT, 8, self.W["w_attn_out"][l], T, first=False, h2off=8 * T)

    def mixer_D(self, l, mode, T, summary=False):
        prm = self.prm
        segs = self.segs(mode)
        NB = T // 128
        NCH = T // 32
        off = 0
        if not summary:
            od, off = self.carve(off, 4 * T)
            od = od.rearrange("p (h t) -> p h t", h=4)
        base = off
        Win = self.W["w_in"][l]
        for hh in range(4):
            off = base
            vt, off = self.carve(off, NB * 128)
            vt = vt.rearrange("p (b c) -> p b c", c=128)
            lf = []
            for d in range(2):
                a, off = self.carve(off, T, F32)
                lf.append(a)
            Bc, off = self.carve(off, T, F32)
            Ex, off = self.carve(off, T, F32)
            kk, off = self.carve(off, T)
            Kt, off = self.carve(off, T)
            Ktm, off = self.carve(off, 4 * 128)
            Ktm = Ktm.rearrange("p (c d) -> p c d", c=4)
            Sf32, off = self.carve(off, 128, F32)
            Sbf = []
            for i in range(2):
                a, off = self.carve(off, 128)
                Sbf.append(a)
            Ach, off = self.carve(off, NCH, F32)
            At, off = self.carve(off, 2, F32)
            if not summary:
                qf, off = self.carve(off, T)
                gT, off = self.carve(off, T)
                o32, off = self.carve(off, T, F32)
                Qi, off = self.carve(off, T)
                Qc, off = self.carve(off, T)
                Kc, off = self.carve(off, T)
                attS, off = self.carve(off, 128)
                Bmid, off = self.carve(off, NCH, F32)
                if mode == "S":
                    Sc, off = self.carve(off, 4 * 128, F32)
                    Sc = Sc.rearrange("p (r c) -> p r c", r=4)
                    Ac, off = self.carve(off, 4, F32)
                    coef, off = self.carve(off, 2, F32)
                    tS, off = self.carve(off, 128, F32)
            for tt_ in self.tile_order(T):
                sl = slice(tt_ * 512, (tt_ + 1) * 512)
                hk = self.make_h(tt_)
                hr = [self.hT[:, kc, :] for kc in range(16)]
                if not summary:
                    self.proj_fm(Win, CHQ + hh * 128, 128, hr, hk,
                                 lambda m, pt, pk, sl=sl: self.act(qf[:, sl], pt[:], AF.Silu, reads=[pk], writes=["qf0", "qf"]))
                    self.proj_fm(Win, CHG + hh * 128, 128, hr, hk,
                                 lambda m, pt, pk, sl=sl: self.act(gT[:, sl], pt[:], AF.Silu, reads=[pk], writes=["gT"]))
                for d, cbase in ((0, CHF), (1, CHB)):
                    def cons_f(m, pt, pk, d=d, sl=sl):
                        t, tk = self.tmp()
                        self.act(t[:], pt[:], AF.Sigmoid, reads=[pk], writes=[tk])
                        self.ts(t[:], t[:], self.oml[:, d, l, hh:hh + 1], self.lbv[:, d, l, hh:hh + 1], ALU.mult, ALU.add,
                                reads=[tk, "oml", "lbv"], writes=[tk])
                        self.act(lf[d][:, sl], t[:], AF.Ln, reads=[tk], writes=[("lf", d)])
                    self.proj_fm(Win, cbase + hh * 128, 128, hr, hk, cons_f)
                w, wk = self.wload(Win, 0, 16, CHI + hh * 128, 128)
                for tb in range(4):
                    blk = tt_ * 4 + tb
                    pt, pk = self.ps()
                    self.PE([(pt[:, 0:128], self.hT[:, kc, tb * 128:(tb + 1) * 128], w[:, kc, 0:128], kc == 0, kc == 15)
                             for kc in range(16)], [wk] + hk, [pk])
                    self.cp(vt[:, blk, :], pt[:, 0:128], [pk], ["vt"])
            if not summary:
                self.ts(qf[:, 0:T], qf[:, 0:T], SCALE, None, ALU.mult, reads=["qf0"], writes=["qf"])
            for d in range(2):
                self.act(Ex[:, 0:T], lf[d][:, 0:T], AF.Exp, reads=[("lf", d)], writes=["Ex"])
                self.ts(kk[:, 0:T], Ex[:, 0:T], -1.0, 1.0, ALU.mult, ALU.add, reads=["Ex"], writes=["kk"])
                B3 = Bc.rearrange("p (c j) -> p c j", j=32)
                E3 = Ex.rearrange("p (c j) -> p c j", j=32)
                self.V(lambda e, d=d: e.tensor_tensor_scan(out=Bc[:, 0:T], data0=self.smask[:, 0:T], data1=lf[d][:, 0:T],
                                                          initial=0.0, op0=ALU.mult, op1=ALU.add),
                       [("lf", d), "smask"], ["Bc"])
                self.cp(Ach[:, 0:NCH], B3[:, 0:NCH, 31], ["Bc"], ["Btot", "Ach"])
                if d == 1:
                    self.tt(B3[:, 0:NCH, :], Ach[:, 0:NCH].unsqueeze(2).to_broadcast([128, NCH, 32]), B3[:, 0:NCH, :],
                            ALU.subtract, ["Bc", "Btot"], ["Bc"])
                    self.tt(Bc[:, 0:T], Bc[:, 0:T], lf[d][:, 0:T], ALU.add, ["Bc", ("lf", d)], ["Bc"])
                if not summary:
                    self.cp(Bmid[:, 0:NCH], B3[:, 0:NCH, 15], ["Bc"], ["Bmid"])
                    self.act(Ex[:, 0:T], Bc[:, 0:T], AF.Exp, reads=["Bc", "kk"], writes=["Ex"])
                    self.tt(Qi[:, 0:T], qf[:, 0:T], Ex[:, 0:T], ALU.mult, ["qf", "Ex"], ["Qi"])
                self.tt(E3[:, 0:NCH, :], Ach[:, 0:NCH].unsqueeze(2).to_broadcast([128, NCH, 32]), B3[:, 0:NCH, :],
                        ALU.subtract, ["Bc", "Btot", "Ex", "Qi", "kk"], ["Ex"])
                self.act(Ex[:, 0:T], Ex[:, 0:T], AF.Exp, reads=["Ex"], writes=["Ex"])
                self.tt(Kt[:, 0:T], kk[:, 0:T], Ex[:, 0:T], ALU.mult, ["kk", "Ex"], ["Kt"])
                if not summary:
                    self.tt(E3[:, 0:NCH, :], B3[:, 0:NCH, :], Bmid[:, 0:NCH].unsqueeze(2).to_broadcast([128, NCH, 32]),
                            ALU.subtract, ["Bc", "Bmid", "Ex", "Kt"], ["Ex"])
                    self.ts(Ex[:, 0:T], Ex[:, 0:T], -40.0, 40.0, ALU.max, ALU.min, reads=["Ex"], writes=["Ex"])
                    self.act(Bc[:, 0:T], Ex[:, 0:T], AF.Exp, reads=["Ex", "Bmid"], writes=["Bc"])
                    self.tt(Qc[:, 0:T], qf[:, 0:T], Bc[:, 0:T], ALU.mult, ["qf", "Bc"], ["Qc"])
                    self.act(Bc[:, 0:T], Ex[:, 0:T], AF.Exp, scale=-1.0, reads=["Ex", "Qc"], writes=["Bc"])
                    self.tt(Kc[:, 0:T], kk[:, 0:T], Bc[:, 0:T], ALU.mult, ["kk", "Bc"], ["Kc"])
                if summary:
                    self.V(lambda e: e.reduce_sum(out=At[:, 0:1], in_=Ach[:, 0:NCH], axis=AX.X), ["Btot"], ["At"])
                    self.act(At[:, 0:1], At[:, 0:1], AF.Exp, reads=["At"], writes=["At"])
                    dst = self.xin_v("Af" if d == 0 else "Ab", 512).rearrange("(h p o) -> h p o", h=4, o=1)[hh]
                    self.xput("A%d" % d, dst, At[:, 0:1], ["At"])
                self.act(Ach[:, 0:NCH], Ach[:, 0:NCH], AF.Exp, reads=["Btot", "Ex", "At"], writes=["Ach"])
                for si, (s0, ln) in enumerate(segs):
                    nblk = ln // 128
                    blks = list(range(s0 // 128, s0 // 128 + nblk))
                    if d == 1:
                        blks = blks[::-1]
                    if mode == "P" or summary:
                        self.memset(Sf32, 0.0, ["S"])
                    else:
                        s0in = (self.s0f_in if d == 0 else self.s0b_in)[l, hh]
                        self.DMA(Sf32, s0in, writes=["S"])
                        self.DMA(Sc, self.xout_v("Sf" if d == 0 else "Sb", 65536).rearrange(
                            "r (h p c) -> h p r c", h=4, p=128)[hh], reads=["xout"], writes=["Sc"])
                        self.DMA(Ac.rearrange("p (r o) -> p r o", o=1), self.xout_v("Af" if d == 0 else "Ab", 512).rearrange(
                            "r (h p o) -> h p r o", h=4, o=1)[hh], reads=["xout"], writes=["Ac"], slow=True)
                        order = [0, 1, 2, 3] if d == 0 else [3, 2, 1, 0]
                        sb_ = PRM["sel"] + (8 if d == 0 else 12)
                        for r in order:
                            m_ = prm[:, sb_ + r:sb_ + r + 1]
                            om_ = prm[:, sb_ + 16 + r:sb_ + 16 + r + 1]
                            self.ts(coef[:, 0:1], Ac[:, r:r + 1], m_, om_, ALU.mult, ALU.add, reads=["Ac", "prm"],
                                    writes=["coef"])
                            self.ts(tS, Sc[:, r, :], m_, None, ALU.mult, reads=["Sc", "prm"], writes=["tS"])
                            self.stt(Sf32, Sf32, coef[:, 0:1], tS, ALU.mult, ALU.add, reads=["S", "coef", "tS"], writes=["S"])
                    self.cp(Sbf[0], Sf32, ["S"], [("Sb", 0)])
                    sidx = 0
                    for blk in blks:
                        bsl = slice(blk * 128, (blk + 1) * 128)
                        if not summary:
                            pa, pak = self.ps()
                            self.PE([(pa[:, 0:128], Kc[:, bsl], Qc[:, bsl], True, True)], ["Kc", "Qc"], [pak])
                            self.tt(attS, pa[:, 0:128], self.hmask[:, d, :], ALU.mult, [pak, "hmask"], ["attS"])
                        self.TR(self.psT[:, 512:640], Kt[:, bsl], self.identB[:], ["Kt", "identB"], ["psT2"])
                        for c in range(4):
                            self.ts(Ktm[:, c, :], self.psT[:, 512:640], prm[:, PRM["cm"] + c:PRM["cm"] + c + 1], None,
                                    ALU.mult, reads=["psT2", "prm"], writes=[("Ktm", c)])
                        if not summary:
                            po, pok = self.ps()
                            self.PE([(po[:, 0:128], vt[:, blk, :], attS, True, False)], ["vt", "attS"], [pok])
                        corder = [0, 1, 2, 3] if d == 0 else [3, 2, 1, 0]
                        for ci, c in enumerate(corder):
                            gch = blk * 4 + c
                            csl = slice(gch * 32, gch * 32 + 32)
                            if not summary:
                                self.PE([(po[:, c * 32:(c + 1) * 32], Sbf[sidx % 2], Qi[:, csl], False, ci == 3)],
                                        [("Sb", sidx % 2), "Qi", pok], [pok])
                            pu, puk = self.ps()
                            self.PE([(pu[:, 0:128], Ktm[:, c, :], vt[:, blk, :], True, True)], [("Ktm", c), "vt"], [puk])
                            self.stt(Sf32, Sf32, Ach[:, gch:gch + 1], pu[:, 0:128], ALU.mult, ALU.add,
                                     reads=["S", "Ach", puk], writes=["S"])
                            sidx += 1
                            if not summary:
                                self.cp(Sbf[sidx % 2], Sf32, ["S"], [("Sb", sidx % 2)])
                        if not summary:
                            if d == 0:
                                self.cp(o32[:, bsl], po[:, 0:128], [pok], ["o32"])
                            else:
                                self.tt(o32[:, bsl], o32[:, bsl], po[:, 0:128], ALU.add, [pok, "o32"], ["o32"])
                    if mode == "P":
                        dst = (self.nsf_out if d == 0 else self.nsb_out)[si, l, hh]
                        self.DMA(dst, Sf32, reads=["S"], writes=["nso"], lane="out")
                    if summary:
                        dst = self.xin_v("Sf" if d == 0 else "Sb", 65536).rearrange("(h p c) -> h p c", h=4, p=128)[hh]
                        self.xput("S%d" % d, dst, Sf32, ["S"])
            if not summary:
                for tt_ in range(T // 512):
                    sl = slice(tt_ * 512, (tt_ + 1) * 512)
                    t, tk = self.tmp()
                    self.act(t[:], o32[:, sl], AF.Square, reads=["o32"], writes=[tk])
                    pm, pmk = self.ps()
                    self.PE([(pm[:], self.onesF[:], t[:], True, True)], [tk, "onesF"], [pmk])
                    r, rk = self.tmp()
                    self.act(r[:], pm[:], AF.Ln, bias=self.epsT[:, 0:1], scale=1.0 / 128, reads=[pmk, "epsT"], writes=[rk])
                    self.act(r[:], r[:], AF.Exp, scale=-0.5, reads=[rk], writes=[rk])
                    self.tt(r[:], r[:], o32[:, sl], ALU.mult, [rk, "o32"], [rk])
                    self.stt(od[:, hh, sl], r[:], prm[:, PRM["hgng"] + l:PRM["hgng"] + l + 1], gT[:, sl], ALU.mult, ALU.mult,
                             reads=[rk, "prm", "gT"], writes=["mo"])
        if not summary:
            self.fw.barrier()
            self.branch_out(l, 3, od, 4, self.W["w_hg_out"][l], T, first=False, h2off=4 * T)

    def wo_ffn(self, l, mode, T):
        kind = 0 if mode == "P" else 1
        NTT = T // 512
        g1 = self.modT[:, l, 32:48, kind]
        g2 = self.modT[:, l, 80:96, kind]
        for cb in range(8):
            w, wk = self.wload(self.W["w_o"][l], 0, 16, cb * 256, 256)
            for m in range(2):
                ch = cb * 2 + m
                for tt_ in range(NTT):
                    sl = slice(tt_ * 512, (tt_ + 1) * 512)
                    pt, pk = self.ps()
                    self.PE([(pt[:], w[:, kc, m * 128:(m + 1) * 128], self.mg[:, kc, sl], kc == 0, kc == 15)
                             for kc in range(16)], [wk] + [("mg", kc) for kc in range(16)], [pk])
                    self.stt(self.xT[:, ch, sl], pt[:], g1[:, ch:ch + 1], self.xT[:, ch, sl], ALU.mult, ALU.add,
                             reads=[pk, "modT", "x"], writes=["x"])
        self.fw.barrier()
        self.set_norm(l, 2, kind)
        self.norm_stats(T)
        ff, off = self.carve(0, 11 * T)
        ff = ff.rearrange("p (j t) -> p j t", j=11)
        tiles = self.h_tiles(T, off)
        for grp in range(4):
            j = 0
            while j < 11:
                nj = 2 if j + 1 < 11 else 1
                col = (grp * 11 + j) * 128
                w1, w1k = self.wload(self.W["ffn_w1"][l], 0, 16, col, 128 * nj)
                held = {}
                for m in range(nj):
                    for ti, (tix, hb, hk) in enumerate(tiles):
                        pt, pk = self.ps()
                        self.PE([(pt[:], w1[:, kc, m * 128:(m + 1) * 128], hb[:, kc, :], kc == 0, kc == 15)
                                 for kc in range(16)], [w1k] + hk, [pk])
                        t, tk = self.tmp()
                        self.act(t[:], pt[:], AF.Silu, reads=[pk], writes=[tk])
                        held[(m, ti)] = (t, tk)
                w3, w3k = self.wload(self.W["ffn_w3"][l], 0, 16, col, 128 * nj)
                for m in range(nj):
                    for ti, (tix, hb, hk) in enumerate(tiles):
                        pt, pk = self.ps()
                        self.PE([(pt[:], w3[:, kc, m * 128:(m + 1) * 128], hb[:, kc, :], kc == 0, kc == 15)
                                 for kc in range(16)], [w3k] + hk, [pk])
                        t, tk = held[(m, ti)]
                        self.tt(ff[:, j + m, tix * 512:(tix + 1) * 512], t[:], pt[:], ALU.mult, [tk, pk], [("ff", j + m)])
                j += nj
            for cb in range(8):
                w, wk = self.wload(self.W["ffn_w2"][l], grp * 11, 11, cb * 256, 256)
                for m in range(2):
                    ch = cb * 2 + m
                    for tt_ in range(NTT):
                        sl = slice(tt_ * 512, (tt_ + 1) * 512)
                        pt, pk = self.ps()
                        self.PE([(pt[:], w[:, jj, m * 128:(m + 1) * 128], ff[:, jj, sl], jj == 0, jj == 10)
                                 for jj in range(11)], [wk] + [("ff", jj) for jj in range(11)], [pk])
                        self.stt(self.xT[:, ch, sl], pt[:], g2[:, ch:ch + 1], self.xT[:, ch, sl], ALU.mult, ALU.add,
                                 reads=[pk, "modT", "x"], writes=["x"])


_CACHE = {}


def host_consts(core):
    j = core % 4
    ident = np.eye(128, dtype=np.float32)
    s_ = np.arange(128)[:, None]
    t_ = np.arange(128)[None, :]
    same = (s_ // 32) == (t_ // 32)
    hm = np.zeros((128, 2, 128), np.float32)
    hm[:, 0, :] = (same & (s_ <= t_)).astype(np.float32)
    hm[:, 1, :] = (same & (s_ >= t_)).astype(np.float32)
    sm = np.ones((128, 1024), np.float32)
    sm[:, ::32] = 0.0
    low = np.where(s_ >= t_, 0.0, NEG).astype(np.float32)
    up = np.where(s_ <= t_, 0.0, NEG).astype(np.float32)
    allneg = np.full((128, 128), NEG, np.float32)
    ab = np.zeros((128, 10, 128), np.float32)
    ab[:, 0, :] = low
    ab[:, 1, :] = up
    for r in range(4):
        ab[:, 2 + r, :] = low if r == j - 1 else allneg
        ab[:, 6 + r, :] = up if r == j + 1 else allneg
    tg = j * 1024 + np.arange(1024)
    row = (tg // 64).astype(np.float32)
    col = (tg % 64).astype(np.float32)
    inv = (np.float32(10000.0) ** (-np.arange(0, 64, 2, dtype=np.float32) / np.float32(64))).astype(np.float32)
    ang = np.stack([row[:, None] * inv, col[:, None] * inv], axis=1).astype(np.float32)
    cs = np.stack([np.cos(ang), np.sin(ang)], axis=1).reshape(1024, 2, 64)
    rope = np.ascontiguousarray(cs.reshape(8, 128, 2, 64).transpose(1, 0, 2, 3)).astype(np.float32)
    return ident, hm, sm, ab, rope


def make_in_maps(inp, pr):
    in_maps = []
    xp = inp["x_prompt"].astype(np.float32)
    xs = inp["x_sample"].astype(np.float32)
    for c in range(8):
        g, j = c // 4, c % 4
        ident, hm, sm, ab, rope = host_consts(c)
        m = {
            "xT_p": np.ascontiguousarray(xp[2 * c:2 * c + 2].reshape(512, D).T),
            "xT_s": np.ascontiguousarray(xs[g, j * 1024:(j + 1) * 1024].T),
            "ck": np.ascontiguousarray(inp["cache_k"][g].reshape(L, 512, 256)),
            "cv": np.ascontiguousarray(inp["cache_v"][g].reshape(L, 512, 256)),
            "s0f": np.ascontiguousarray(inp["state_hgrn_fwd"][g]),
            "s0b": np.ascontiguousarray(inp["state_hgrn_bwd"][g]),
            "cvec": np.ascontiguousarray(np.stack([fm(inp["c_ctx"], 16), fm(inp["c"][g], 16)], axis=-1)),
            "prm": pack_params(inp, c),
            "qkg": np.ascontiguousarray(np.broadcast_to(
                np.concatenate([inp["q_norm_g"], inp["k_norm_g"]], axis=1)[:, None, :], (L, 128, 256))).astype(np.float32),
            "ident": ident, "hmask": hm, "smask": sm, "abias": ab, "rope": rope,
        }
        for n in pr.W:
            m[n] = np.ascontiguousarray(inp[n][:pr.nl], dtype=np.float32)
        in_maps.append(m)
    return in_maps


def run_and_gather(nc, in_maps):
    res = run_bass_kernel_spmd(nc, in_maps, core_ids=list(range(8)))
    R = res.results
    yp = np.stack([R[c]["yT_p"].T.reshape(2, 256, D) for c in range(8)]).reshape(16, 256, D)
    ys = np.stack([np.concatenate([R[g * 4 + j]["yT_s"].T for j in range(4)], 0) for g in range(2)])
    nk = np.concatenate([R[c]["nk"] for c in range(8)], 0).reshape(16, L, 256, 2, 128)
    nv = np.concatenate([R[c]["nv"] for c in range(8)], 0).reshape(16, L, 256, 2, 128)
    nsf = np.concatenate([R[c]["nsf"] for c in range(8)], 0)
    nsb = np.concatenate([R[c]["nsb"] for c in range(8)], 0)
    return (yp.astype(np.float32), ys.astype(np.float32), nk.astype(np.float32), nv.astype(np.float32),
            nsf.astype(np.float32), nsb.astype(np.float32))


def kernel(**inp):
    inp = {k: np.asarray(v) for k, v in inp.items()}
    if "pr" not in _CACHE:
        pr = Prog(do_sample=True)
        _CACHE["pr"] = pr
        _CACHE["nc"] = pr.build()
    pr = _CACHE["pr"]
    return run_and_gather(_CACHE["nc"], make_in_maps(inp, pr))
```
